# Optimizing a Trainium2 kernel written in Bass

```python
import math
import jax
import jax.numpy as jnp
from jax import lax
import numpy as np


D_MODEL = 2048
BATCH = 2
SEQ = 16384
DEPTH = 2

GRID_W = 64
CTX_LEN = 256
D_MIX = D_MODEL
GROUP_W = D_MIX // 4
HEAD_DIM = 64
ROPE_BASE = 10000.0
Q_BLOCK = 128
N_MOD = 9
FFN_DIM = ((8 * D_MODEL // 3 + 255) // 256) * 256

DIFF_HEADS = GROUP_W // (2 * HEAD_DIM)
DIFF_VDIM = 2 * HEAD_DIM

RWKV_HEADS = GROUP_W // HEAD_DIM
RWKV_DECAY_RANK = 64
RWKV_A_RANK = 64
RWKV_GATE_RANK = 128
RWKV_SHIFT = 3
RWKV_DECAY_SCALE = 0.606531
RWKV_LN_EPS = 64e-5

MLA_HEADS = GROUP_W // HEAD_DIM
MLA_NOPE = 64
MLA_ROPE = 32
MLA_V = GROUP_W // MLA_HEADS
MLA_Q_RANK = 384
MLA_KV_RANK = 128

SSD_HEADS = GROUP_W // HEAD_DIM
SSD_GROUPS = 2
SSD_STATE = 128
SSD_CONV = 5
SSD_CHUNK = 128

RWKV_SIZES = (GROUP_W, GROUP_W, GROUP_W, RWKV_DECAY_RANK, RWKV_DECAY_RANK, RWKV_A_RANK, RWKV_A_RANK, RWKV_GATE_RANK)
RWKV_IN = sum(RWKV_SIZES)
SSD_XBC = GROUP_W + 2 * SSD_GROUPS * SSD_STATE
IN_SIZES = (DIFF_HEADS * 2 * HEAD_DIM, DIFF_HEADS * 2 * HEAD_DIM, DIFF_HEADS * DIFF_VDIM, RWKV_IN,
            MLA_Q_RANK, MLA_KV_RANK, MLA_ROPE, GROUP_W, SSD_XBC, 2 * SSD_HEADS)
IN_COLS = sum(IN_SIZES)

kernel_name = 'hybrid_parallel_group_diffusion_block'

F32 = jnp.float32


def split_sizes(t, sizes):
    return jnp.split(t, np.cumsum(sizes)[:-1].tolist(), axis=-1)


def rms_norm(t, g, eps=1e-6):
    tf = t.astype(F32)
    tf = tf * lax.rsqrt(jnp.mean(tf * tf, axis=-1, keepdims=True) + eps)
    return (tf * g.astype(F32)).astype(t.dtype)


def swiglu(h, w_gate, w_up, w_down):
    return (jax.nn.silu(h @ w_gate) * (h @ w_up)) @ w_down


def adaln(t, g, m_ctx, m_lat, j, with_ctx):
    h = rms_norm(t, g)
    shift_l, scale_l = m_lat[:, 3 * j, None], m_lat[:, 3 * j + 1, None]
    if not with_ctx:
        return h * (1 + scale_l) + shift_l
    h_ctx = h[:, :CTX_LEN] * (1 + m_ctx[3 * j + 1]) + m_ctx[3 * j]
    return jnp.concatenate([h_ctx, h[:, CTX_LEN:] * (1 + scale_l) + shift_l], axis=1)


def gated_add(t, y, m_ctx, m_lat, j, with_ctx):
    gate_l = m_lat[:, 3 * j + 2, None]
    if not with_ctx:
        return t + gate_l * y
    return t + jnp.concatenate([m_ctx[3 * j + 2] * y[:, :CTX_LEN], gate_l * y[:, CTX_LEN:]], axis=1)


def to_bwd(t):
    return jnp.concatenate([jnp.flip(t[:, :CTX_LEN], 1), jnp.flip(t[:, CTX_LEN:], 1)], axis=1)


def dwconv(t, w):
    k_w, ch = w.shape
    return lax.conv_general_dilated(t, w[:, None, :].astype(t.dtype), (1,), [(k_w // 2, k_w // 2)],
                                    dimension_numbers=('NWC', 'WIO', 'NWC'), feature_group_count=ch)


def prefix_dwconv(t, w):
    return jnp.concatenate([dwconv(t[:, :CTX_LEN], w), dwconv(t[:, CTX_LEN:], w)], axis=1)


def axial_rope_tables(rows, dim):
    quarter = dim // 4
    inv = ROPE_BASE ** (-jnp.arange(quarter, dtype=F32) / quarter)
    pos_r = jnp.repeat(jnp.arange(rows), GRID_W).astype(F32)
    pos_c = jnp.tile(jnp.arange(GRID_W), rows).astype(F32)
    ang = jnp.concatenate([pos_r[:, None] * inv, pos_c[:, None] * inv], axis=-1)
    return jnp.cos(ang), jnp.sin(ang)


def apply_rope(t, cos, sin):
    shape = (1, cos.shape[0]) + (1,) * (t.ndim - 3) + (cos.shape[1],)
    cos, sin = cos.reshape(shape), sin.reshape(shape)
    t1, t2 = jnp.split(t, 2, axis=-1)
    return jnp.concatenate([t1 * cos - t2 * sin, t2 * cos + t1 * sin], axis=-1).astype(t.dtype)


def rope_latent(t, cos, sin):
    return jnp.concatenate([t[:, :CTX_LEN], apply_rope(t[:, CTX_LEN:], cos, sin)], axis=1)


def block_attention(q, k, v, scale):
    bsz, n_q = q.shape[:2]
    qb = jnp.moveaxis(q.reshape((bsz, n_q // Q_BLOCK, Q_BLOCK) + q.shape[2:]), 1, 0)

    def one_block(q_blk):
        s = jnp.einsum('bqhmd,bkhmd->bhmqk', q_blk, k, preferred_element_type=F32) * scale
        p = jax.nn.softmax(s, axis=-1).astype(v.dtype)
        return jnp.einsum('bhmqk,bkhd->bqhmd', p, v)

    o = lax.map(one_block, qb)
    return jnp.moveaxis(o, 0, 1).reshape((bsz, n_q) + o.shape[3:])


def prefix_attention(q, k, v, scale, want_ctx):
    o_lat = block_attention(q[:, CTX_LEN:], k, v, scale)
    if not want_ctx:
        return o_lat
    o_ctx = block_attention(q[:, :CTX_LEN], k[:, :CTX_LEN], v[:, :CTX_LEN], scale)
    return jnp.concatenate([o_ctx, o_lat], axis=1)


def diff_attention(q, k, v, qk_g, lam, subln_g, lam_init, cos, sin, want_ctx):
    bsz, n, _ = q.shape
    q = rms_norm(q.reshape(bsz, n, DIFF_HEADS, 2, HEAD_DIM), qk_g[0])
    k = rms_norm(k.reshape(bsz, n, DIFF_HEADS, 2, HEAD_DIM), qk_g[1])
    q, k = rope_latent(q, cos, sin), rope_latent(k, cos, sin)
    v = v.reshape(bsz, n, DIFF_HEADS, DIFF_VDIM)
    lam_full = (jnp.exp(jnp.sum(lam[0] * lam[1])) - jnp.exp(jnp.sum(lam[2] * lam[3])) + lam_init).astype(v.dtype)
    o = prefix_attention(q, k, v, HEAD_DIM ** -0.5, want_ctx)
    o = o[..., 0, :] - lam_full * o[..., 1, :]
    o = rms_norm(o, subln_g) * (1.0 - lam_init)
    return o.reshape(bsz, -1, DIFF_HEADS * DIFF_VDIM)


def mla_mixer(c_q, c_kv, k_rope, q_norm_g, kv_norm_g, w_uq, w_ukv, nope_g, rope_g, cos, sin, want_ctx):
    bsz, n, _ = c_q.shape
    q = (rms_norm(c_q, q_norm_g) @ w_uq).reshape(bsz, n, MLA_HEADS, MLA_NOPE + MLA_ROPE)
    kv = (rms_norm(c_kv, kv_norm_g) @ w_ukv).reshape(bsz, n, MLA_HEADS, MLA_NOPE + MLA_V)
    q_nope = rms_norm(q[..., :MLA_NOPE], nope_g[0])
    k_nope = rms_norm(kv[..., :MLA_NOPE], nope_g[1])
    v = kv[..., MLA_NOPE:]
    q_rope = rope_latent(rms_norm(q[..., MLA_NOPE:], rope_g[0]), cos, sin)
    k_rope = rope_latent(rms_norm(k_rope[:, :, None, :], rope_g[1]), cos, sin)
    q = jnp.concatenate([q_nope, q_rope], axis=-1)[:, :, :, None, :]
    k = jnp.concatenate([k_nope, jnp.broadcast_to(k_rope, (bsz, n, MLA_HEADS, MLA_ROPE))], axis=-1)[:, :, :, None, :]
    o = prefix_attention(q, k, v, (MLA_NOPE + MLA_ROPE) ** -0.5, want_ctx)
    return o.reshape(bsz, -1, MLA_HEADS * MLA_V)


def wkv7_scan(r, w, k, v, kk, a):
    n_b, _, n_h, dh = r.shape

    def step(s, inp):
        r_t, w_t, k_t, v_t, kk_t, a_t = inp
        sa = jnp.einsum('nhvk,nhk->nhv', s, kk_t)
        s = s * w_t[:, :, None, :] - sa[..., None] * (kk_t * a_t)[:, :, None, :] + v_t[..., None] * k_t[:, :, None, :]
        return s, jnp.einsum('nhvk,nhk->nhv', s, r_t)

    s0 = jnp.zeros((n_b, n_h, dh, dh), F32)
    _, y = lax.scan(step, s0, tuple(jnp.moveaxis(t, 1, 0) for t in (r, w, k, v, kk, a)))
    return jnp.moveaxis(y, 0, 1)


def rwkv7_mixer(p, shift_w, w0, w_up, a0, a_up, g_up, k_k, k_a, r_k, ln_g, ln_b, want_ctx):
    dtype = p.dtype
    p = prefix_dwconv(p, shift_w)
    r, k, v, wd_f, wd_b, ad_f, ad_b, gd = split_sizes(p, RWKV_SIZES)
    bsz, n, ch = r.shape
    nh, dh = RWKV_HEADS, HEAD_DIM
    w_raw = w0[:, None, None] + jnp.einsum('dblr,drc->dblc', jnp.tanh(jnp.stack([wd_f, wd_b])), w_up)
    decay = jnp.exp(-RWKV_DECAY_SCALE * jax.nn.sigmoid(w_raw.astype(F32)))
    a = jax.nn.sigmoid(a0[:, None, None] + jnp.einsum('dblr,drc->dblc', jnp.stack([ad_f, ad_b]), a_up))
    kk = (k * k_k).astype(F32).reshape(bsz, n, nh, dh)
    kk = kk / jnp.maximum(jnp.sqrt(jnp.sum(kk * kk, axis=-1, keepdims=True)), 1e-12)
    k_dir = k[None] * (1 + (a - 1) * k_a)

    def orient(t):
        t = jnp.broadcast_to(t, (2, bsz, n, ch)).reshape(2, bsz, n, nh, dh)
        return jnp.concatenate([t[0], to_bwd(t[1])], axis=0).astype(F32)

    y = wkv7_scan(orient(r[None]), orient(decay), orient(k_dir), orient(v[None]),
                  orient(kk.reshape(1, bsz, n, ch)), orient(a)).reshape(2, bsz, n, nh, dh)
    y = y[0] + to_bwd(y[1])
    if not want_ctx:
        y, r, k, v, gd = (t[:, CTX_LEN:] for t in (y, r, k, v, gd))
    n_out = y.shape[1]
    mu = jnp.mean(y, axis=-1, keepdims=True)
    var = jnp.mean(jnp.square(y - mu), axis=-1, keepdims=True)
    yn = ((y - mu) * lax.rsqrt(var + RWKV_LN_EPS)).reshape(bsz, n_out, ch) * ln_g + ln_b
    rh, kh, vh = (t.reshape(bsz, n_out, nh, dh) for t in (r, k, v))
    bonus = (jnp.sum(rh * kh * r_k, axis=-1, keepdims=True) * vh).reshape(bsz, n_out, ch)
    g = jax.nn.sigmoid(gd) @ g_up
    return ((yn + bonus) * g).astype(dtype)


def ssd_scan(x, a, bm, cm):
    n_b, n, nh, hp = x.shape
    n_c = n // SSD_CHUNK
    causal = jnp.tril(jnp.ones((SSD_CHUNK, SSD_CHUNK), bool))[None, :, :, None]

    def chunks(t):
        return jnp.moveaxis(t.astype(F32).reshape((n_b, n_c, SSD_CHUNK) + t.shape[2:]), 1, 0)

    def step(h, inp):
        xq, aq, bq, cq = inp
        acum = jnp.cumsum(aq, axis=1)
        seg = acum[:, :, None, :] - acum[:, None, :, :]
        decay = jnp.exp(jnp.where(causal, seg, -jnp.inf))
        scores = jnp.einsum('nlhs,nmhs->nlmh', cq, bq) * decay
        y = jnp.einsum('nlmh,nmhp->nlhp', scores, xq)
        y = y + jnp.einsum('nlhs,nhps->nlhp', cq, h) * jnp.exp(acum)[..., None]
        to_end = jnp.exp(acum[:, -1:, :] - acum)
        h = h * jnp.exp(acum[:, -1, :])[:, :, None, None] + jnp.einsum('nmhs,nmh,nmhp->nhps', bq, to_end, xq)
        return h, y

    h0 = jnp.zeros((n_b, nh, hp, SSD_STATE), F32)
    _, y = lax.scan(step, h0, (chunks(x), chunks(a), chunks(bm), chunks(cm)))
    return jnp.moveaxis(y, 0, 1).reshape(n_b, n, nh, hp)


def ssd_mixer(zg, xbc, dt_raw, conv_w, conv_b, dt_bias, a_log, d_skip, norm_g, want_ctx):
    dtype = xbc.dtype
    xbc = jax.nn.silu(prefix_dwconv(xbc, conv_w) + conv_b)
    xs, bm, cm = split_sizes(xbc, (GROUP_W, SSD_GROUPS * SSD_STATE, SSD_GROUPS * SSD_STATE))
    bsz, n, _ = xs.shape
    nh, hp = SSD_HEADS, HEAD_DIM
    xh = xs.reshape(bsz, n, nh, hp)
    bm = jnp.repeat(bm.reshape(bsz, n, SSD_GROUPS, SSD_STATE), nh // SSD_GROUPS, axis=2)
    cm = jnp.repeat(cm.reshape(bsz, n, SSD_GROUPS, SSD_STATE), nh // SSD_GROUPS, axis=2)
    dt = jnp.moveaxis(jax.nn.softplus(dt_raw.reshape(bsz, n, 2, nh) + dt_bias), 2, 0)
    a_dt = dt * -jnp.exp(a_log)[:, None, None, :]
    x_dt = xh[None] * dt[..., None]

    def orient(t):
        return jnp.concatenate([t[0], to_bwd(t[1])], axis=0)

    y = ssd_scan(orient(x_dt), orient(a_dt), jnp.concatenate([bm, to_bwd(bm)], 0),
                 jnp.concatenate([cm, to_bwd(cm)], 0)).reshape(2, bsz, n, nh, hp)
    y = (y[0] + to_bwd(y[1])).astype(dtype) + d_skip[:, None] * xh
    y = y.reshape(bsz, n, nh * hp)
    if not want_ctx:
        y, zg = y[:, CTX_LEN:], zg[:, CTX_LEN:]
    return rms_norm(y * jax.nn.silu(zg), norm_g)


def setup_inputs(seed: int = 0) -> dict:
    key = jax.random.key(seed)
    keys = jax.random.split(key, 40)

    def nrm(i, shape, scale):
        return scale * jax.random.normal(keys[i], shape, F32)

    def gain(i, shape):
        return 1.0 + nrm(i, shape, 0.02)

    nl, d, gw = DEPTH, D_MODEL, GROUP_W
    dt0 = jnp.exp(jax.random.uniform(keys[36], (nl, 2, SSD_HEADS), F32, math.log(1e-3), math.log(1e-1)))
    return {
        'x': nrm(0, (BATCH, SEQ, d), 1.0),
        'c': nrm(1, (BATCH, d), 1.0),
        'ctx': nrm(2, (BATCH, CTX_LEN, d), 1.0),
        'c_ctx': nrm(3, (d,), 1.0),
        'mod_w': nrm(4, (nl, d, N_MOD * d), 0.5 * d ** -0.5),
        'mod_b': nrm(5, (nl, N_MOD, d), 0.02),
        'norm_g': gain(6, (nl, 3, d)),
        'ffn_w_gate': nrm(7, (nl, 2, d, FFN_DIM), d ** -0.5),
        'ffn_w_up': nrm(8, (nl, 2, d, FFN_DIM), d ** -0.5),
        'ffn_w_down': nrm(9, (nl, 2, FFN_DIM, d), FFN_DIM ** -0.5),
        'w_in': nrm(10, (nl, d, IN_COLS), d ** -0.5),
        'w_out': nrm(11, (nl, D_MIX, d), D_MIX ** -0.5),
        'diff_qk_g': gain(12, (nl, 2, HEAD_DIM)),
        'diff_lambda': nrm(13, (nl, 4, HEAD_DIM), 0.1),
        'diff_subln_g': gain(14, (nl, DIFF_VDIM)),
        'rwkv_shift_w': jnp.array([0.25, 0.5, 0.25], F32)[None, :, None] + nrm(15, (nl, RWKV_SHIFT, RWKV_IN), 0.05),
        'rwkv_w0': nrm(16, (nl, 2, gw), 0.5),
        'rwkv_w_up': nrm(17, (nl, 2, RWKV_DECAY_RANK, gw), 0.1 * RWKV_DECAY_RANK ** -0.5),
        'rwkv_a0': nrm(18, (nl, 2, gw), 0.5),
        'rwkv_a_up': nrm(19, (nl, 2, RWKV_A_RANK, gw), RWKV_A_RANK ** -0.5),
        'rwkv_g_up': nrm(20, (nl, RWKV_GATE_RANK, gw), RWKV_GATE_RANK ** -0.5),
        'rwkv_k_k': 0.85 + nrm(21, (nl, gw), 0.02),
        'rwkv_k_a': gain(22, (nl, gw)),
        'rwkv_r_k': nrm(23, (nl, RWKV_HEADS, HEAD_DIM), 0.1),
        'rwkv_ln_g': gain(24, (nl, gw)),
        'rwkv_ln_b': nrm(25, (nl, gw), 0.02),
        'mla_q_norm_g': gain(26, (nl, MLA_Q_RANK)),
        'mla_kv_norm_g': gain(27, (nl, MLA_KV_RANK)),
        'mla_w_uq': nrm(28, (nl, MLA_Q_RANK, MLA_HEADS * (MLA_NOPE + MLA_ROPE)), MLA_Q_RANK ** -0.5),
        'mla_w_ukv': nrm(29, (nl, MLA_KV_RANK, MLA_HEADS * (MLA_NOPE + MLA_V)), MLA_KV_RANK ** -0.5),
        'mla_nope_g': gain(30, (nl, 2, MLA_NOPE)),
        'mla_rope_g': gain(31, (nl, 2, MLA_ROPE)),
        'ssd_conv_w': nrm(32, (nl, SSD_CONV, SSD_XBC), SSD_CONV ** -0.5),
        'ssd_conv_b': nrm(33, (nl, SSD_XBC), 0.02),
        'ssd_dt_bias': dt0 + jnp.log(-jnp.expm1(-dt0)),
        'ssd_a_log': jnp.log(jax.random.uniform(keys[34], (nl, 2, SSD_HEADS), F32, 1.0, 16.0)),
        'ssd_d': gain(35, (nl, SSD_HEADS)),
        'ssd_norm_g': gain(37, (nl, GROUP_W)),
    }


def reference(x, c, ctx, c_ctx, mod_w, mod_b, norm_g, ffn_w_gate, ffn_w_up, ffn_w_down, w_in, w_out,
              diff_qk_g, diff_lambda, diff_subln_g,
              rwkv_shift_w, rwkv_w0, rwkv_w_up, rwkv_a0, rwkv_a_up, rwkv_g_up, rwkv_k_k, rwkv_k_a, rwkv_r_k,
              rwkv_ln_g, rwkv_ln_b,
              mla_q_norm_g, mla_kv_norm_g, mla_w_uq, mla_w_ukv, mla_nope_g, mla_rope_g,
              ssd_conv_w, ssd_conv_b, ssd_dt_bias, ssd_a_log, ssd_d, ssd_norm_g):
    bsz, n_lat, d = x.shape
    rows = n_lat // GRID_W
    cos_d, sin_d = axial_rope_tables(rows, HEAD_DIM)
    cos_m, sin_m = axial_rope_tables(rows, MLA_ROPE)
    z = jnp.concatenate([ctx, x], axis=1)
    for l in range(DEPTH):
        want_ctx = l < DEPTH - 1
        m_lat = (jax.nn.silu(c) @ mod_w[l]).reshape(bsz, N_MOD, d) + mod_b[l]
        m_ctx = (jax.nn.silu(c_ctx) @ mod_w[l]).reshape(N_MOD, d) + mod_b[l]
        h = adaln(z, norm_g[l, 0], m_ctx, m_lat, 0, True)
        z = gated_add(z, 0.5 * swiglu(h, ffn_w_gate[l, 0], ffn_w_up[l, 0], ffn_w_down[l, 0]), m_ctx, m_lat, 0, True)
        h = adaln(z, norm_g[l, 1], m_ctx, m_lat, 1, True)
        dq, dk, dv, rw, mcq, mckv, mkr, sz, sxbc, sdt = split_sizes(h @ w_in[l], IN_SIZES)
        lam_init = 0.8 - 0.6 * math.exp(-0.3 * l)
        o_diff = diff_attention(dq, dk, dv, diff_qk_g[l], diff_lambda[l], diff_subln_g[l], lam_init,
                                cos_d, sin_d, want_ctx)
        o_rwkv = rwkv7_mixer(rw, rwkv_shift_w[l], rwkv_w0[l], rwkv_w_up[l], rwkv_a0[l], rwkv_a_up[l], rwkv_g_up[l],
                             rwkv_k_k[l], rwkv_k_a[l], rwkv_r_k[l], rwkv_ln_g[l], rwkv_ln_b[l], want_ctx)
        o_mla = mla_mixer(mcq, mckv, mkr, mla_q_norm_g[l], mla_kv_norm_g[l], mla_w_uq[l], mla_w_ukv[l],
                          mla_nope_g[l], mla_rope_g[l], cos_m, sin_m, want_ctx)
        o_ssd = ssd_mixer(sz, sxbc, sdt, ssd_conv_w[l], ssd_conv_b[l], ssd_dt_bias[l], ssd_a_log[l], ssd_d[l],
                          ssd_norm_g[l], want_ctx)
        mix = jnp.concatenate([o_diff, o_rwkv, o_mla, o_ssd], axis=-1) @ w_out[l]
        if not want_ctx:
            z = z[:, CTX_LEN:]
        z = gated_add(z, mix, m_ctx, m_lat, 1, want_ctx)
        h = adaln(z, norm_g[l, 2], m_ctx, m_lat, 2, want_ctx)
        z = gated_add(z, 0.5 * swiglu(h, ffn_w_gate[l, 1], ffn_w_up[l, 1], ffn_w_down[l, 1]), m_ctx, m_lat, 2, want_ctx)
    return z
```

```python
import contextlib
import math
import numpy as np
import ml_dtypes
import concourse.bass as bass
import concourse.mybir as mybir
from concourse.bass_utils import run_bass_kernel_spmd

F32 = mybir.dt.float32
BF16 = mybir.dt.bfloat16
AF = mybir.ActivationFunctionType
ALU = mybir.AluOpType
NPBF = ml_dtypes.bfloat16

PE, ACT, DVE, POOL, SP = "tensor", "scalar", "vector", "gpsimd", "sync"
ENGS = [PE, ACT, DVE, POOL, SP]
EPOCH = 30000

D = 2048
KC = 16
FF = 5632
FC = 44
NCORE = 8
SEQ = 16384
CTX = 256
NTOK = SEQ + CTX
IN_COLS = 5552
INP = 5632
EPS = 1e-6


class Buf:
    __slots__ = ("name", "last_write", "readers", "dsem", "dcount")

    def __init__(self, name=""):
        self.name = name
        self.last_write = None
        self.readers = []
        self.dsem = None
        self.dcount = 0


class Prog:
    def __init__(self, nc, es):
        self.nc = nc
        self.es = es
        self.q = {e: [] for e in ENGS}
        self.cnt = {e: 0 for e in ENGS}
        self.epoch = {e: 0 for e in ENGS}
        self.waited = {e: {} for e in ENGS}
        self.esem = {e: self._newsem("c_%s_0" % e) for e in ENGS}
        self.ninstr = 0
        self.nsem = 5
        self.final = []
        self.dma_toks = {}

    def _newsem(self, name):
        self.nsem = getattr(self, "nsem", 0) + 1
        return self.es.enter_context(self.nc.semaphore(name))

    def _deps(self, reads, writes):
        deps = []
        for b in reads:
            if b.last_write is not None:
                deps.append(b.last_write)
        for b in writes:
            if b.last_write is not None:
                deps.append(b.last_write)
            deps.extend(b.readers)
        return deps

    def _reduce(self, eng, deps, sync_same):
        need = {}
        w = self.waited[eng]
        for sem, val, key in deps:
            if (not sync_same) and key[0] == eng:
                continue
            if w.get(key, 0) >= val:
                continue
            if need.get(key, (None, 0))[1] < val:
                need[key] = (sem, val)
        for key, (sem, val) in need.items():
            w[key] = val
        return list(need.values())

    def op(self, eng, fn, reads=(), writes=(), sync_same=True):
        deps = self._deps(reads, writes)
        waits = self._reduce(eng, deps, sync_same)
        if self.cnt[eng] >= EPOCH:
            self.epoch[eng] += 1
            self.cnt[eng] = 0
            self.esem[eng] = self._newsem("c_%s_%d" % (eng, self.epoch[eng]))
        self.cnt[eng] += 1
        tok = (self.esem[eng], self.cnt[eng], (eng, self.epoch[eng]))
        self.q[eng].append((fn, waits, (self.esem[eng], 1)))
        for b in reads:
            b.readers.append(tok)
        for b in writes:
            b.last_write = tok
            b.readers = []
        self.ninstr += 1
        return tok

    def dma(self, eng, fn, track, reads=(), writes=()):
        deps = self._deps(reads, writes)
        waits = self._reduce(eng, deps, True)
        if track.dsem is None:
            track.dsem = self._newsem("d_" + track.name)
        track.dcount += 16
        tok = (track.dsem, track.dcount, ("dma", id(track)))
        self.q[eng].append((fn, waits, (track.dsem, 16)))
        self.dma_toks[("dma", id(track))] = tok
        for b in reads:
            b.readers.append(tok)
        for b in writes:
            b.last_write = tok
            b.readers = []
        self.ninstr += 1
        return tok

    def finish(self, eng=SP):
        waits = self._reduce(eng, list(self.final), True)
        self.q[eng].append((None, waits, None))

    def barrier(self):
        toks = [(self.esem[e], self.cnt[e], (e, self.epoch[e])) for e in ENGS if self.cnt[e] > 0]
        toks += list(self.dma_toks.values())
        for e in ENGS:
            waits = self._reduce(e, toks, False)
            self.q[e].append((None, waits, None))

    def store(self, out, in_, src, eng=POOL, **kw):
        tok = self.dma(eng, lambda e: e.dma_start(out=out, in_=in_, **kw), src, [src], [])
        self.final.append(tok)
        return tok

    def build(self, block):
        q = self.q

        def runner(ename):
            def body(e):
                for fn, waits, inc in q[ename]:
                    for sem, val in waits:
                        e.wait_ge(sem, val)
                    if fn is not None:
                        fn(e).then_inc(inc[0], inc[1])
            return body
        block.tensor(runner(PE))
        block.scalar(runner(ACT))
        block.vector(runner(DVE))
        block.gpsimd(runner(POOL))
        block.sync(runner(SP))

    def mm(self, out, lhsT, rhs, start, stop, reads, writes):
        return self.op(PE, lambda e: e.matmul(out, lhsT, rhs, start=start, stop=stop), reads, writes,
                       sync_same=False)

    def act(self, out, in_, func, reads, writes, scale=1.0, bias=None):
        if bias is None:
            return self.op(ACT, lambda e: e.activation(out=out, in_=in_, func=func, scale=scale), reads, writes)
        return self.op(ACT, lambda e: e.activation(out=out, in_=in_, func=func, scale=scale, bias=bias),
                       reads, writes)

    def tt(self, eng, out, in0, in1, op, reads, writes):
        return self.op(eng, lambda e: e.tensor_tensor(out=out, in0=in0, in1=in1, op=op), reads, writes)

    def ts(self, eng, out, in0, s1, s2, op0, op1, reads, writes):
        if op1 is None:
            return self.op(eng, lambda e: e.tensor_scalar(out=out, in0=in0, scalar1=s1, scalar2=None, op0=op0),
                           reads, writes)
        return self.op(eng, lambda e: e.tensor_scalar(out=out, in0=in0, scalar1=s1, scalar2=s2, op0=op0, op1=op1),
                       reads, writes)

    def stt(self, out, in0, scalar, in1, op0, op1, reads, writes):
        return self.op(DVE, lambda e: e.scalar_tensor_tensor(out=out, in0=in0, scalar=scalar, in1=in1,
                                                             op0=op0, op1=op1), reads, writes)

    def load(self, out, in_, track, reads=(), writes=None, eng=SP, **kw):
        writes = [track] if writes is None else writes
        return self.dma(eng, lambda e: e.dma_start(out=out, in_=in_, **kw), track, reads, writes)


class Ctx:
    def __init__(self, name):
        self.nc = bass.Bass("TRN2", target_bir_lowering=False)
        self.es = contextlib.ExitStack()
        self.P = Prog(self.nc, self.es)
        self.name = name
        self.outs = []

    def din(self, name, shape, dt=F32):
        return self.nc.dram_tensor(name, list(shape), dt, kind="ExternalInput").ap()

    def dout(self, name, shape, dt=F32):
        return self.nc.dram_tensor(name, list(shape), dt, kind="ExternalOutput").ap()

    def sb(self, name, shape, dt=F32, es=None):
        t = (es or self.es).enter_context(self.nc.sbuf_tensor(name, list(shape), dt))
        return t, Buf(name)

    def ps(self, name, shape=(128, 512), dt=F32, es=None):
        t = (es or self.es).enter_context(self.nc.psum_tensor(name, list(shape), dt))
        return t, Buf(name)

    def done(self):
        self.P.finish()
        with self.nc.Block() as block:
            self.P.build(block)
        self.es.close()
        return self.nc


MODC = 9 * D // NCORE
CONV_PIECE = 4096


def build_prep(conv_len):
    C = Ctx("prep")
    P = C.P
    cst = C.din("cst", [128, KC, 3])
    mw = C.din("mw", [2, KC, 128, MODC])
    mb = C.din("mb", [2, 3, MODC])
    wf = C.din("wf", [128, conv_len])
    m_out = C.dout("m_out", [2, 3, MODC])
    wb = C.dout("wb", [128, conv_len], BF16)

    ct, ctb = C.sb("ct", [128, KC, 3])
    st, stb = C.sb("st", [128, KC, 3])
    P.load(ct[:], cst, ctb)
    P.act(st[:], ct[:], AF.Sigmoid, [ctb], [stb])
    P.tt(DVE, st[:], st[:], ct[:], ALU.mult, [stb, ctb], [stb])
    nslot = 3
    wt = [C.sb("mwt%d" % i, [128, MODC]) for i in range(nslot)]
    bt, btb = C.sb("bt", [3, 2, MODC])
    P.load(bt[:], mb.rearrange("l r c -> r l c"), btb)
    ot, otb = C.sb("ot", [3, 2, MODC])
    pss = [C.ps("mps%d" % i, [128, 512]) for i in range(5)]
    mob = Buf("m_out")
    cols = [(i * 512, min(512, MODC - i * 512)) for i in range(5)]
    it = 0
    for l in range(2):
        for kc in range(KC):
            t, tb = wt[it % nslot]
            it += 1
            P.load(t[:], mw[l, kc], tb)
            for i, (c0, cn) in enumerate(cols):
                P.mm(pss[i][0][0:3, 0:cn], st[:, kc, :], t[:, c0:c0 + cn], kc == 0, kc == KC - 1,
                     [stb, tb], [pss[i][1]])
        for i, (c0, cn) in enumerate(cols):
            P.tt(DVE, ot[:, l, c0:c0 + cn], pss[i][0][0:3, 0:cn], bt[:, l, c0:c0 + cn], ALU.add,
                 [pss[i][1], btb], [otb])
    P.store(m_out.rearrange("l r c -> r l c"), ot[:], otb, eng=SP)
    npiece = conv_len // CONV_PIECE
    cv = [C.sb("cv%d" % i, [128, CONV_PIECE], BF16) for i in range(4)]
    wbb = Buf("wb")
    for i in range(npiece):
        t, tb = cv[i % 4]
        sl = slice(i * CONV_PIECE, (i + 1) * CONV_PIECE)
        P.load(t[:].rearrange("p (a b) -> p a b", b=2048), wf[:, sl].rearrange("p (a b) -> p a b", b=2048), tb,
               eng=POOL)
        P.store(wb[:, sl], t[:], tb, eng=SP)
    return C.done()


NT_CORE = 64 + 4096
BLOCKS = [(0, 64, 0)] + [(64 + i * 512, 512, 1) for i in range(8)]


def build_token(stages):
    C = Ctx("token")
    P = C.P
    nc = C.nc
    zT = C.din("zT", [KC, 128, NT_CORE])
    modv = C.din("modv", [128, 2, 2, 9, KC])
    normg = C.din("normg", [128, 2, 3, KC])
    z_out = C.dout("z_out", [KC, 128, NT_CORE])
    wts = {}
    for s in stages:
        if s[0] == "ffn":
            _, l, i = s
            wts[s] = (C.din("wg_%d_%d" % (l, i), [22, 128, KC, 256], BF16),
                      C.din("wu_%d_%d" % (l, i), [22, 128, KC, 256], BF16),
                      C.din("wd_%d_%d" % (l, i), [KC, 128, FC, 128], BF16))
        elif s[0] == "inproj":
            wts[s] = (C.din("win_%d" % s[1], [22, 128, KC, 256], BF16),
                      C.dout("pT_%d" % s[1], [FC, 128, NT_CORE]))
        elif s[0] == "outproj":
            wts[s] = (C.din("wout_%d" % s[1], [KC, 128, KC, 128], BF16),
                      C.din("oT_%d" % s[1], [KC, 128, NT_CORE]),
                      C.din("zgT_%d" % s[1], [4, 128, NT_CORE]),
                      C.din("ssdg_%d" % s[1], [128, 4]),
                      C.din("ybT_%d" % s[1], [4, 128, NT_CORE]))

    ones, onesb = C.sb("ones", [128, 128])
    P.op(DVE, lambda e: e.memset(ones[:], 1.0), [], [onesb])
    epsT, epsb = C.sb("epsT", [128, 1])
    P.op(DVE, lambda e: e.memset(epsT[:], EPS), [], [epsb])
    mv, mvb = C.sb("mv", [128, 2, 2, 9, KC])
    P.load(mv[:], modv, mvb)
    ng, ngb = C.sb("ng", [128, 2, 3, KC])
    P.load(ng[:], normg, ngb)
    gs, gsb = C.sb("gs", [128, 2, 2, 3, KC])
    hg, hgb = C.sb("hg", [128, 2, 2, 3, KC])
    for l in range(2):
        for g in range(2):
            for j in range(3):
                P.ts(DVE, gs[:, l, g, j, :], mv[:, l, g, 3 * j + 1, :], 1.0, None, ALU.add, None, [mvb], [gsb])
                P.tt(DVE, gs[:, l, g, j, :], gs[:, l, g, j, :], ng[:, l, j, :], ALU.mult, [gsb, ngb], [gsb])
                P.ts(DVE, hg[:, l, g, j, :], mv[:, l, g, 3 * j + 2, :], (1.0 if j == 1 else 0.5), None,
                     ALU.mult, None, [mvb], [hgb])
    sg4 = None
    for s in stages:
        if s[0] == "outproj":
            sg4 = C.sb("sg4_%d" % s[1], [128, 4])
            P.load(sg4[0][:], wts[s][3], sg4[1])

    zt, ztb = C.sb("zt", [128, KC, 512])
    ht, htb = C.sb("ht", [128, KC, 512], BF16)
    actt, actb = C.sb("actt", [128, FC, 512], BF16)
    tmp = [C.sb("tmp%d" % i, [128, 512]) for i in range(3)]
    rstd, rstdb = C.sb("rstd", [128, 512])
    wgs = [C.sb("wgs%d" % i, [128, KC, 256], BF16) for i in range(2)]
    wus = [C.sb("wus%d" % i, [128, KC, 256], BF16) for i in range(2)]
    wds = [C.sb("wds%d" % i, [128, FC, 128], BF16) for i in range(2)]
    stg = [C.sb("stg%d" % i, [128, 512]) for i in range(3)]
    psG = [C.ps("psG%d" % i) for i in range(2)]
    psU = [C.ps("psU%d" % i) for i in range(2)]
    psD = [C.ps("psD%d" % i) for i in range(2)]
    psS = C.ps("psS")
    zob = Buf("z_out")
    cnt = {"w": 0, "d": 0, "t": 0, "s": 0, "g": 0}

    def adaln(l, j, lat, T):
        for kc in range(KC):
            t, tb = tmp[cnt["t"] % 3]
            cnt["t"] += 1
            P.act(t[:, :T], zt[:, kc, :T], AF.Square, [ztb], [tb])
            P.mm(psS[0][:, :T], ones[:], t[:, :T], kc == 0, kc == KC - 1, [onesb, tb], [psS[1]])
        P.act(rstd[:, :T], psS[0][:, :T], AF.Sqrt, [psS[1], epsb], [rstdb], scale=1.0 / D, bias=epsT[:])
        P.op(DVE, lambda e: e.reciprocal(out=rstd[:, :T], in_=rstd[:, :T]), [rstdb], [rstdb])
        for kc in range(KC):
            t, tb = tmp[cnt["t"] % 3]
            cnt["t"] += 1
            P.tt(DVE, t[:, :T], zt[:, kc, :T], rstd[:, :T], ALU.mult, [ztb, rstdb], [tb])
            P.ts(POOL, ht[:, kc, :T], t[:, :T], gs[:, l, lat, j, kc:kc + 1], mv[:, l, lat, 3 * j, kc:kc + 1],
                 ALU.mult, ALU.add, [tb, gsb, mvb], [htb])

    def proj_T(w_ap, nchunk_out, nk, rhs_t, rhs_b, T, consume, slots):
        for oc in range(nchunk_out):
            wt_, wb_ = slots[cnt["d"] % 2]
            cnt["d"] += 1
            P.load(wt_[:, :nk, :], w_ap[oc], wb_)
            ps_, pb_ = psD[oc % 2]
            for k in range(nk):
                P.mm(ps_[:, :T], wt_[:, k, :], rhs_t[:, k, :T], k == 0, k == nk - 1, [wb_, rhs_b], [pb_])
            consume(oc, ps_, pb_)

    def ffn(l, i, lat, T):
        j = 0 if i == 0 else 2
        wg, wu, wd = wts[("ffn", l, i)]
        adaln(l, j, lat, T)
        for g in range(22):
            a, ab = wgs[g % 2]
            u, ub = wus[g % 2]
            P.load(a[:], wg[g], ab)
            P.load(u[:], wu[g], ub)
            for h in range(2):
                fc = g * 2 + h
                pg, pgb = psG[fc % 2]
                pu, pub = psU[fc % 2]
                for k in range(KC):
                    P.mm(pg[:, :T], a[:, k, h * 128:(h + 1) * 128], ht[:, k, :T], k == 0, k == KC - 1,
                         [ab, htb], [pgb])
                for k in range(KC):
                    P.mm(pu[:, :T], u[:, k, h * 128:(h + 1) * 128], ht[:, k, :T], k == 0, k == KC - 1,
                         [ub, htb], [pub])
                t, tb = tmp[cnt["t"] % 3]
                cnt["t"] += 1
                P.act(t[:, :T], pg[:, :T], AF.Silu, [pgb], [tb])
                P.tt(DVE, actt[:, fc, :T], t[:, :T], pu[:, :T], ALU.mult, [tb, pub], [actb])

        def consume(oc, ps_, pb_):
            P.stt(zt[:, oc, :T], ps_[:, :T], hg[:, l, lat, j, oc:oc + 1], zt[:, oc, :T], ALU.mult, ALU.add,
                  [pb_, hgb, ztb], [ztb])
        proj_T(wd, KC, FC, actt, actb, T, consume, wds)

    def inproj(l, lat, T, t0):
        win, pT = wts[("inproj", l)]
        adaln(l, 1, lat, T)
        pTb = pbufs[l]
        for g in range(22):
            a, ab = wgs[g % 2]
            P.load(a[:], win[g], ab)
            for h in range(2):
                oc = g * 2 + h
                ps_, pb_ = psD[oc % 2]
                for k in range(KC):
                    P.mm(ps_[:, :T], a[:, k, h * 128:(h + 1) * 128], ht[:, k, :T], k == 0, k == KC - 1,
                         [ab, htb], [pb_])
                s_, sb_ = stg[cnt["s"] % 3]
                cnt["s"] += 1
                if oc % 2 == 0:
                    P.act(s_[:, :T], ps_[:, :T], AF.Copy, [pb_], [sb_])
                else:
                    P.op(DVE, lambda e, s_=s_, ps_=ps_: e.tensor_copy(out=s_[:, :T], in_=ps_[:, :T]), [pb_], [sb_])
                P.store(pT[oc, :, t0:t0 + T], s_[:, :T], sb_)

    def outproj(l, lat, T, t0):
        wout, oT, zgT, _, ybT = wts[("outproj", l)]
        ssd_t = []
        for c4 in range(4):
            y_, yb_ = stg[cnt["s"] % 3]
            cnt["s"] += 1
            g_, gb_ = tmp[cnt["t"] % 3]
            cnt["t"] += 1
            P.load(y_[:, :T], oT[12 + c4, :, t0:t0 + T], yb_)
            P.load(g_[:, :T], zgT[c4, :, t0:t0 + T], gb_)
            y2_, y2b_ = stg[cnt["s"] % 3]
            cnt["s"] += 1
            P.load(y2_[:, :T], ybT[c4, :, t0:t0 + T], y2b_)
            P.tt(DVE, y_[:, :T], y_[:, :T], y2_[:, :T], ALU.add, [yb_, y2b_], [yb_])
            P.act(g_[:, :T], g_[:, :T], AF.Silu, [gb_], [gb_])
            P.tt(DVE, zsc[0][:, c4, :T], g_[:, :T], y_[:, :T], ALU.mult, [gb_, yb_], [zsc[1]])
            P.act(y_[:, :T], zsc[0][:, c4, :T], AF.Square, [zsc[1]], [yb_])
            P.mm(psS[0][:, :T], ones[:], y_[:, :T], c4 == 0, c4 == 3, [onesb, yb_], [psS[1]])
        P.act(rstd[:, :T], psS[0][:, :T], AF.Sqrt, [psS[1], epsb], [rstdb], scale=1.0 / 512, bias=epsT[:])
        P.op(DVE, lambda e: e.reciprocal(out=rstd[:, :T], in_=rstd[:, :T]), [rstdb], [rstdb])
        for c4 in range(4):
            P.stt(ht[:, 12 + c4, :T], zsc[0][:, c4, :T], sg4[0][:, c4:c4 + 1], rstd[:, :T], ALU.mult, ALU.mult,
                  [zsc[1], sg4[1], rstdb], [htb])
        for kc in range(12):
            y_, yb_ = stg[cnt["s"] % 3]
            cnt["s"] += 1
            P.load(y_[:, :T], oT[kc, :, t0:t0 + T], yb_)
            if kc % 2 == 0:
                P.act(ht[:, kc, :T], y_[:, :T], AF.Copy, [yb_], [htb])
            else:
                P.op(DVE, lambda e, y_=y_, kc=kc: e.tensor_copy(out=ht[:, kc, :T], in_=y_[:, :T]), [yb_], [htb])

        def consume(oc, ps_, pb_):
            P.stt(zt[:, oc, :T], ps_[:, :T], hg[:, l, lat, 1, oc:oc + 1], zt[:, oc, :T], ALU.mult, ALU.add,
                  [pb_, hgb, ztb], [ztb])
        proj_T(wout, KC, KC, ht, htb, T, consume, wds)

    zsc = None
    if any(s[0] == "outproj" for s in stages):
        zsc = C.sb("zsc", [128, 4, 512])
    pbufs = {s[1]: Buf("pT%d" % s[1]) for s in stages if s[0] == "inproj"}

    for (t0, T, lat) in BLOCKS:
        P.load(zt[:, :, :T], zT[:, :, t0:t0 + T].rearrange("k p t -> p k t"), ztb)
        for s in stages:
            if s[0] == "ffn":
                ffn(s[1], s[2], lat, T)
            elif s[0] == "inproj":
                inproj(s[1], lat, T, t0)
            elif s[0] == "outproj":
                outproj(s[1], lat, T, t0)
        P.store(z_out[:, :, t0:t0 + T].rearrange("k p t -> p k t"), zt[:, :, :T], ztb)
    print("token program", stages, "ninstr", P.ninstr, "nsem", P.nsem)
    return C.done()


def lay_w22(w):
    k, n = w.shape
    if n < 22 * 256:
        w = np.concatenate([w, np.zeros((k, 22 * 256 - n), w.dtype)], axis=1)
    return np.ascontiguousarray(w.reshape(KC, 128, 22, 256).transpose(2, 1, 0, 3))


def lay_wd(w):
    nf = w.shape[0] // 128
    return np.ascontiguousarray(w.reshape(nf, 128, KC, 128).transpose(2, 1, 0, 3))


def lay_vec(v):
    lead = v.shape[:-1]
    a = v.reshape(lead + (KC, 128))
    a = np.moveaxis(a, -1, 0)
    return np.ascontiguousarray(a)


def fm(zt_tokens_major):
    t, f = zt_tokens_major.shape
    return np.ascontiguousarray(zt_tokens_major.T.reshape(f // 128, 128, t))


def core_tokens(c):
    q = c % 4
    return np.concatenate([np.arange(q * 64, (q + 1) * 64), CTX + np.arange(q * 4096, (q + 1) * 4096)])


NKT = NTOK // 128
TBLK = [(0, 256)] + [(256 + 512 * i, 512) for i in range(32)]


def build_attn(parts=('dk', 'dq', 'mk', 'mq'), nblk=33, dbg=9):
    C = Ctx("attn")
    P = C.P
    nc = C.nc
    dq = C.din("dq", [128, NTOK]); dk = C.din("dk", [128, NTOK]); dv = C.din("dv", [NTOK, 128])
    cosd = C.din("cosd", [128, NTOK]); sind = C.din("sind", [128, NTOK])
    cosm = C.din("cosm", [96, NTOK]); sinm = C.din("sinm", [96, NTOK])
    cq = C.din("cq", [3, 128, NTOK]); ckv = C.din("ckv", [128, NTOK]); kr = C.din("kr", [32, NTOK])
    wuq = C.din("wuq", [128, 3, 192]); wukv = C.din("wukv", [128, 256])
    vecs = C.din("vecs", [128, 16])
    lamv = C.din("lamv", [64, 4])
    cmat = C.din("cmat", [4, 128, 128])
    o_diff = C.dout("o_diff", [128, NTOK]); o_mla = C.dout("o_mla", [128, NTOK])

    ones, onesb = C.sb("ones", [128, 128])
    P.op(DVE, lambda e: e.memset(ones[:], 1.0), [], [onesb])
    onesh, oneshb = C.sb("onesh", [128, 128], BF16)
    P.op(DVE, lambda e: e.memset(onesh[:], 1.0), [], [oneshb])
    epsT, epsb = C.sb("epsT", [128, 1])
    P.op(DVE, lambda e: e.memset(epsT[:], EPS), [], [epsb])
    vc, vcb = C.sb("vc", [128, 16])
    P.load(vc[:], vecs, vcb)
    cm, cmb = C.sb("cm", [128, 4, 128])
    P.load(cm[:], cmat.rearrange("m p c -> p m c"), cmb)
    lv, lvb = C.sb("lv", [64, 4])
    P.load(lv[:], lamv, lvb)
    wq_f, wq_fb = C.sb("wq_f", [128, 3, 192]); wq, wqb = C.sb("wq", [128, 3, 192], BF16)
    wkv_f, wkv_fb = C.sb("wkv_f", [128, 256]); wkv, wkvb = C.sb("wkv", [128, 256], BF16)
    P.load(wq_f[:], wuq, wq_fb); P.load(wkv_f[:], wukv, wkv_fb)
    P.op(DVE, lambda e: e.tensor_copy(out=wq[:], in_=wq_f[:]), [wq_fb], [wqb])
    P.op(DVE, lambda e: e.tensor_copy(out=wkv[:], in_=wkv_f[:]), [wkv_fb], [wkvb])

    psS = [C.ps("psS%d" % i) for i in range(2)]
    psO = [C.ps("psO%d" % i) for i in range(2)]
    psL = [C.ps("psL%d" % i) for i in range(2)]
    psE = C.ps("psE")
    psX = C.ps("psX")
    NTMP = 8
    tmps = [C.sb("atmp%d" % i, [128, 512]) for i in range(NTMP)]
    cnt = {"t": 0, "p": 0}

    def tmp():
        t = tmps[cnt["t"] % NTMP]
        cnt["t"] += 1
        return t

    l2, l2b = C.sb("l2", [64, 2])
    P.tt(DVE, l2[:, 0:1], lv[:, 0:1], lv[:, 1:2], ALU.mult, [lvb], [l2b])
    P.tt(DVE, l2[:, 1:2], lv[:, 2:3], lv[:, 3:4], ALU.mult, [lvb, l2b], [l2b])
    P.mm(psE[0][:, 0:2], ones[0:64, :], l2[:], True, True, [onesb, l2b], [psE[1]])
    le, leb = C.sb("le", [128, 2])
    P.act(le[:], psE[0][:, 0:2], AF.Exp, [psE[1]], [leb])
    nlam, nlamb = C.sb("nlam", [128, 1])
    P.tt(DVE, nlam[:], le[:, 1:2], le[:, 0:1], ALU.subtract, [leb], [nlamb])
    P.tt(DVE, nlam[:], nlam[:], vc[:, 3:4], ALU.subtract, [nlamb, vcb], [nlamb])
    sgd, sgdb = C.sb("sgd", [128, 1])
    P.tt(DVE, sgd[:], vc[:, 2:3], vc[:, 4:5], ALU.mult, [vcb], [sgdb])

    def norm_rope(src, srcb, R, T, gi, ri, gain_col, cos_ap, sin_ap, out_ap, outb):
        sq, sqb = tmp()
        P.act(sq[:R, :T], src, AF.Square, [srcb], [sqb])
        P.mm(psE[0][:R, :T], cm[:R, gi, :R], sq[:R, :T], True, True, [cmb, sqb], [psE[1]])
        rt, rtb = tmp()
        P.act(rt[:R, :T], psE[0][:R, :T], AF.Sqrt, [psE[1], epsb], [rtb], bias=epsT[:R, :])
        P.op(DVE, lambda e: e.reciprocal(out=rt[:R, :T], in_=rt[:R, :T]), [rtb], [rtb])
        xn, xnb = tmp()
        P.stt(xn[:R, :T], src, vc[:R, gain_col:gain_col + 1], rt[:R, :T], ALU.mult, ALU.mult,
              [srcb, vcb, rtb], [xnb])
        P.mm(psX[0][:R, :T], cm[:R, ri, :R], xn[:R, :T], True, True, [cmb, xnb], [psX[1]])
        ct, ctb = tmp(); st, stb = tmp()
        P.load(ct[:R, :T], cos_ap, ctb)
        P.load(st[:R, :T], sin_ap, stb)
        P.tt(DVE, ct[:R, :T], xn[:R, :T], ct[:R, :T], ALU.mult, [xnb, ctb], [ctb])
        P.tt(DVE, st[:R, :T], psX[0][:R, :T], st[:R, :T], ALU.mult, [psX[1], stb], [stb])
        P.tt(POOL, out_ap, ct[:R, :T], st[:R, :T], ALU.add, [ctb, stb], [outb])

    def attention(streams, q0, T, nkt, epilogue):
        pts = {}
        last = nkt - 1

        def pv(k):
            for si, s in enumerate(streams):
                pt, ptb = pts[(si, k)]
                for (po, pob, lf, lb) in s["pv"]:
                    P.mm(po[:, :T], lf(k), pt[:, :T], k == 0, k == last, [lb, ptb], [pob])
        for k in range(nkt):
            for si, s in enumerate(streams):
                r0, r1 = s["rows"]
                P.mm(psS[si][0][:, :T], s["KT"][r0:r1, k * 128:(k + 1) * 128], s["QT"][r0:r1, :T], True, True,
                     [s["KTb"], s["QTb"]], [psS[si][1]])
            for si, s in enumerate(streams):
                pt, ptb = ptiles[si][k % 2]
                pts[(si, k)] = (pt, ptb)
                P.act(pt[:, :T], psS[si][0][:, :T], AF.Exp, [psS[si][1]], [ptb], scale=s["scale"])
            if k > 0:
                pv(k - 1)
        pv(last)
        epilogue(q0, T)

    ptiles = [[C.sb("pt%d_%d" % (si, j), [128, 512], BF16) for j in range(2)] for si in range(2)]

    with contextlib.ExitStack() as es:
        KT, KTb = C.sb("dKT", [128, NTOK], BF16, es)
        V, Vb = C.sb("dV", [128, NKT, 128], BF16, es)
        QTs = [C.sb("dQT%d" % i, [128, 512], BF16, es) for i in range(2)]
        srcs = [C.sb("dsrc%d" % i, [128, 512], F32, es) for i in range(2)]
        dvr = dv.rearrange("(t p) c -> p t c", p=128)
        for i in range(0, NKT, 13):
            P.load(V[:, i:i + 13, :], dvr[:, i:i + 13, :], Vb, reads=[], writes=[Vb], eng=POOL)
        for bi, (t0, T) in enumerate(TBLK[:nblk] if 'dk' in parts else []):
            s, sb_ = srcs[bi % 2]
            P.load(s[:, :T], dk[:, t0:t0 + T], sb_)
            norm_rope(s[:, :T], sb_, 128, T, 0, 1, 1, cosd[:, t0:t0 + T], sind[:, t0:t0 + T], KT[:, t0:t0 + T], KTb)

        def epi_diff(q0, T):
            r0, r0b = tmp(); r1, r1b = tmp()
            P.op(DVE, lambda e: e.reciprocal(out=r0[:, :T], in_=psL[0][0][:, :T]), [psL[0][1]], [r0b])
            P.op(DVE, lambda e: e.reciprocal(out=r1[:, :T], in_=psL[1][0][:, :T]), [psL[1][1]], [r1b])
            P.tt(DVE, r0[:, :T], psO[0][0][:, :T], r0[:, :T], ALU.mult, [psO[0][1], r0b], [r0b])
            P.tt(DVE, r1[:, :T], psO[1][0][:, :T], r1[:, :T], ALU.mult, [psO[1][1], r1b], [r1b])
            o, ob = tmp()
            P.stt(o[:, :T], r1[:, :T], nlam[:, 0:1], r0[:, :T], ALU.mult, ALU.add, [r1b, nlamb, r0b], [ob])
            sq, sqb = tmp()
            P.act(sq[:, :T], o[:, :T], AF.Square, [ob], [sqb])
            P.mm(psE[0][:, :T], ones[:], sq[:, :T], True, True, [onesb, sqb], [psE[1]])
            rt, rtb = tmp()
            P.act(rt[:, :T], psE[0][:, :T], AF.Sqrt, [psE[1], epsb], [rtb], scale=1.0 / 128, bias=epsT[:])
            P.op(DVE, lambda e: e.reciprocal(out=rt[:, :T], in_=rt[:, :T]), [rtb], [rtb])
            P.stt(o[:, :T], o[:, :T], sgd[:, 0:1], rt[:, :T], ALU.mult, ALU.mult, [ob, sgdb, rtb], [ob])
            P.store(o_diff[:, q0:q0 + T], o[:, :T], ob)

        for bi, (q0, T) in enumerate(TBLK[:nblk] if 'dq' in parts else []):
            s, sb_ = srcs[bi % 2]
            qt, qtb = QTs[bi % 2]
            P.load(s[:, :T], dq[:, q0:q0 + T], sb_)
            norm_rope(s[:, :T], sb_, 128, T, 0, 1, 0, cosd[:, q0:q0 + T], sind[:, q0:q0 + T], qt[:, :T], qtb)
            streams = []
            for m in range(2):
                streams.append(dict(KT=KT, KTb=KTb, rows=(m * 64, (m + 1) * 64), QT=qt, QTb=qtb, scale=0.125,
                                    pv=[(psO[m][0], psO[m][1], (lambda k: V[:, k, :]), Vb),
                                        (psL[m][0], psL[m][1], (lambda k: onesh[:]), oneshb)]))
            attention(streams, q0, T, 2 if bi == 0 else NKT, epi_diff)

    P.barrier()
    with contextlib.ExitStack() as es:
        KTs = [C.sb("mKT%d" % i, [96, NTOK], BF16, es) for i in range(2)]
        Vall, Vallb = C.sb("mV", [128, 2, NKT, 128], BF16, es)
        Vs = [(Vall[:, i], Vallb) for i in range(2)]
        QTs = [[C.sb("mQT%d_%d" % (h, i), [96, 512], BF16, es) for i in range(2)] for h in range(2)]
        cqt = [C.sb("cqt%d" % i, [128, 3, 512], F32, es) for i in range(2)]
        cqn, cqnb = C.sb("cqn", [128, 3, 512], BF16, es)
        ckn, cknb = C.sb("ckn", [128, 512], BF16, es)
        for h in range(2):
            P.op(POOL, lambda e, h=h: e.memset(Vall[:, h, :, 64:128], 1.0), [], [Vallb])
        for bi, (t0, T) in enumerate(TBLK[:nblk] if 'mk' in parts else []):
            c_, cb_ = tmp()
            P.load(c_[:, :T], ckv[:, t0:t0 + T], cb_)
            sq, sqb = tmp()
            P.act(sq[:, :T], c_[:, :T], AF.Square, [cb_], [sqb])
            P.mm(psE[0][:, :T], ones[:], sq[:, :T], True, True, [onesb, sqb], [psE[1]])
            rt, rtb = tmp()
            P.act(rt[:, :T], psE[0][:, :T], AF.Sqrt, [psE[1], epsb], [rtb], scale=1.0 / 128, bias=epsT[:])
            P.op(DVE, lambda e, rt=rt, T=T: e.reciprocal(out=rt[:, :T], in_=rt[:, :T]), [rtb], [rtb])
            P.stt(ckn[:, :T], c_[:, :T], vc[:, 8:9], rt[:, :T], ALU.mult, ALU.mult, [cb_, vcb, rtb], [cknb])
            for h in range(2 if dbg >= 2 else 0):
                s, sb_ = tmp()
                P.mm(psX[0][0:64, :T], wkv[:, h * 128:h * 128 + 64], ckn[:, :T], True, True, [wkvb, cknb], [psX[1]])
                P.load(s[64:96, :T], kr[:, t0:t0 + T], sb_)
                P.op(DVE, lambda e, s=s, T=T: e.tensor_copy(out=s[0:64, :T], in_=psX[0][0:64, :T]), [psX[1]], [sb_])
                if dbg >= 3:
                    norm_rope(s[0:96, :T], sb_, 96, T, 2, 3, 10, cosm[:, t0:t0 + T], sinm[:, t0:t0 + T],
                              KTs[h][0][:, t0:t0 + T], KTs[h][1])
            for sub in range(T // 128 if dbg >= 4 else 0):
                tix = t0 // 128 + sub
                P.mm(psX[0][:, 0:256], ckn[:, sub * 128:(sub + 1) * 128], wkv[:, :], True, True, [cknb, wkvb], [psX[1]])
                if dbg >= 5:
                    P.act(Vall[:, :, tix, 0:64], psX[0][:, 0:256].rearrange("p (h c) -> p h c", h=2)[:, :, 64:128],
                          AF.Copy, [psX[1]], [Vallb])

        osb = [C.sb("mo%d" % i, [128, 512], F32, es) for i in range(2)]
        ocnt = [0]

        def epi_mla(q0, T):
            o, ob = osb[ocnt[0] % 2]
            ocnt[0] += 1
            for h in range(2):
                r, rb = tmp()
                P.op(DVE, lambda e, r=r, h=h: e.reciprocal(out=r[0:64, :T], in_=psO[h][0][64:128, :T]), [psO[h][1]], [rb])
                P.tt(DVE, o[h * 64:(h + 1) * 64, :T], psO[h][0][0:64, :T], r[0:64, :T], ALU.mult, [psO[h][1], rb], [ob])
            P.store(o_mla[:, q0:q0 + T], o[:, :T], ob)

        for bi, (q0, T) in enumerate(TBLK[:nblk] if 'mq' in parts else []):
            c_, cb_ = cqt[bi % 2]
            P.load(c_[:, :, :T], cq[:, :, q0:q0 + T].rearrange("k p t -> p k t"), cb_)
            for kc in range(3):
                sq, sqb = tmp()
                P.act(sq[:, :T], c_[:, kc, :T], AF.Square, [cb_], [sqb])
                P.mm(psE[0][:, :T], ones[:], sq[:, :T], kc == 0, kc == 2, [onesb, sqb], [psE[1]])
            rt, rtb = tmp()
            P.act(rt[:, :T], psE[0][:, :T], AF.Sqrt, [psE[1], epsb], [rtb], scale=1.0 / 384, bias=epsT[:])
            P.op(DVE, lambda e, rt=rt, T=T: e.reciprocal(out=rt[:, :T], in_=rt[:, :T]), [rtb], [rtb])
            for kc in range(3):
                P.stt(cqn[:, kc, :T], c_[:, kc, :T], vc[:, 5 + kc:6 + kc], rt[:, :T], ALU.mult, ALU.mult,
                      [cb_, vcb, rtb], [cqnb])
            streams = []
            for h in range(2):
                for kc in range(3):
                    P.mm(psX[0][0:96, :T], wq[:, kc, h * 96:(h + 1) * 96], cqn[:, kc, :T], kc == 0, kc == 2,
                         [wqb, cqnb], [psX[1]])
                s, sb_ = tmp()
                P.op(DVE, lambda e, s=s, T=T: e.tensor_copy(out=s[0:96, :T], in_=psX[0][0:96, :T]), [psX[1]], [sb_])
                qt, qtb = QTs[h][bi % 2]
                norm_rope(s[0:96, :T], sb_, 96, T, 2, 3, 9, cosm[:, q0:q0 + T], sinm[:, q0:q0 + T], qt[:, :T], qtb)
                streams.append(dict(KT=KTs[h][0], KTb=KTs[h][1], rows=(0, 96), QT=qt, QTb=qtb,
                                    scale=96.0 ** -0.5,
                                    pv=[(psO[h][0], psO[h][1], (lambda k, h=h: Vall[:, h, k, :]), Vallb)]))
            attention(streams, q0, T, 2 if bi == 0 else NKT, epi_mla)
    print("attn program ninstr", P.ninstr, "nsem", P.nsem)
    return C.done()


OFF = dict(dq=0, dk=512, dv=1024, rw=1536, mcq=3456, mckv=3840, mkr=3968, sz=4000, sxbc=4512, sdt=5536)


def rope_tables(dim):
    rows = SEQ // 64
    quarter = dim // 4
    inv = (np.float32(10000.0) ** (-np.arange(quarter, dtype=np.float32) / np.float32(quarter))).astype(np.float32)
    pos_r = np.repeat(np.arange(rows), 64).astype(np.float32)
    pos_c = np.tile(np.arange(64), rows).astype(np.float32)
    ang = np.concatenate([pos_r[:, None] * inv, pos_c[:, None] * inv], axis=-1).astype(np.float32)
    return np.cos(ang).astype(np.float32), np.sin(ang).astype(np.float32)


_CONST = {}


def attn_consts():
    if "a" in _CONST:
        return _CONST["a"]
    cd, sd = rope_tables(64)
    cosd = np.ones((128, NTOK), np.float32); sind = np.zeros((128, NTOK), np.float32)
    for r in range(128):
        cosd[r, CTX:] = cd[:, (r % 64) % 32]; sind[r, CTX:] = sd[:, (r % 64) % 32]
    c32, s32 = rope_tables(32)
    cosm = np.ones((96, NTOK), np.float32); sinm = np.zeros((96, NTOK), np.float32)
    for i in range(32):
        cosm[64 + i, CTX:] = c32[:, i % 16]; sinm[64 + i, CTX:] = s32[:, i % 16]
    cmat = np.zeros((4, 128, 128), np.float32)
    for blk in range(2):
        cmat[0, blk * 64:(blk + 1) * 64, blk * 64:(blk + 1) * 64] = 1.0 / 64
        for m in range(64):
            if m < 32:
                cmat[1, blk * 64 + m + 32, blk * 64 + m] = -1.0
            else:
                cmat[1, blk * 64 + m - 32, blk * 64 + m] = 1.0
    cmat[2, 0:64, 0:64] = 1.0 / 64
    cmat[2, 64:96, 64:96] = 1.0 / 32
    for i in range(32):
        if i < 16:
            cmat[3, 64 + i + 16, 64 + i] = -1.0
        else:
            cmat[3, 64 + i - 16, 64 + i] = 1.0
    _CONST["a"] = (cosd, sind, cosm, sinm, cmat)
    return _CONST["a"]


def attn_inputs(PT, inp, l, j):
    cosd, sind, cosm, sinm, cmat = attn_consts()
    r = lambda o, a, b: np.ascontiguousarray(PT[OFF[o] + a:OFF[o] + b])
    lam_init = 0.8 - 0.6 * math.exp(-0.3 * l)
    vecs = np.zeros((128, 16), np.float32)
    vecs[:, 0] = np.tile(inp["diff_qk_g"][l, 0], 2)
    vecs[:, 1] = np.tile(inp["diff_qk_g"][l, 1], 2)
    vecs[:, 2] = inp["diff_subln_g"][l]
    vecs[:, 3] = lam_init
    vecs[:, 4] = 1.0 - lam_init
    vecs[:, 5:8] = inp["mla_q_norm_g"][l].reshape(3, 128).T
    vecs[:, 8] = inp["mla_kv_norm_g"][l]
    vecs[:96, 9] = np.concatenate([inp["mla_nope_g"][l, 0], inp["mla_rope_g"][l, 0]])
    vecs[:96, 10] = np.concatenate([inp["mla_nope_g"][l, 1], inp["mla_rope_g"][l, 1]])
    wuq = inp["mla_w_uq"][l][:, 2 * j * 96:(2 * j + 2) * 96]
    return {
        "dq": r("dq", j * 128, (j + 1) * 128), "dk": r("dk", j * 128, (j + 1) * 128),
        "dv": np.ascontiguousarray(PT[OFF["dv"] + j * 128:OFF["dv"] + (j + 1) * 128].T),
        "cosd": cosd, "sind": sind, "cosm": cosm, "sinm": sinm,
        "cq": np.ascontiguousarray(PT[OFF["mcq"]:OFF["mcq"] + 384].reshape(3, 128, NTOK)),
        "ckv": r("mckv", 0, 128), "kr": r("mkr", 0, 32),
        "wuq": np.ascontiguousarray(wuq.reshape(3, 128, 192).transpose(1, 0, 2)),
        "wukv": np.ascontiguousarray(inp["mla_w_ukv"][l][:, 2 * j * 128:(2 * j + 2) * 128]),
        "vecs": vecs, "lamv": np.ascontiguousarray(inp["diff_lambda"][l].T), "cmat": cmat,
    }


FWD_ORDER = list(range(NKT))
BWD_ORDER = [1, 0] + list(range(NKT - 1, 1, -1))


def build_ssd(nchunks=NKT):
    C = Ctx("ssd")
    P = C.P
    xs = C.din("xs", [128, NTOK]); Bm = C.din("Bm", [128, NTOK]); Cm = C.din("Cm", [128, NTOK])
    cw = C.din("cw", [128, 3, 5]); cb = C.din("cb", [128, 3])
    dtr = C.din("dtr", [NTOK, 4])
    sv = C.din("sv", [128, 10])
    cmat = C.din("cmat", [8, 128, 128])
    y_f = C.dout("y_f", [NTOK, 128]); y_b = C.dout("y_b", [NTOK, 128])

    cm, cmb = C.sb("cm", [128, 8, 128])
    P.load(cm[:], cmat.rearrange("m p c -> p m c"), cmb)
    identb, identbb = C.sb("identb", [128, 128], BF16)
    P.op(DVE, lambda e: e.tensor_copy(out=identb[:], in_=cm[:, 0, :]), [cmb], [identbb])
    ones, onesb = C.sb("ones", [128, 128])
    P.op(DVE, lambda e: e.memset(ones[:], 1.0), [], [onesb])
    cwt, cwb = C.sb("cwt", [128, 3, 5]); cbt, cbb = C.sb("cbt", [128, 3]); svt, svb = C.sb("svt", [128, 10])
    P.load(cwt[:], cw, cwb); P.load(cbt[:], cb, cbb); P.load(svt[:], sv, svb)
    nexpA, nexpAb = C.sb("nexpA", [128, 4])
    P.act(nexpA[:], svt[:, 4:8], AF.Exp, [svb], [nexpAb])
    P.ts(DVE, nexpA[:], nexpA[:], -1.0, None, ALU.mult, None, [nexpAb], [nexpAb])

    x_tm, x_tmb = C.sb("x_tm", [128, NKT, 128])
    Bf, Bfb = C.sb("Bf", [128, NTOK], BF16)
    Cf, Cfb = C.sb("Cf", [128, NTOK], BF16)
    B_tm, B_tmb = C.sb("B_tm", [128, NKT, 128], BF16)
    dt, dtb = C.sb("dt", [128, NKT, 4])
    adt, adtb = C.sb("adt", [128, NKT, 4])
    P.load(dt[:], dtr.rearrange("(t p) c -> p t c", p=128), dtb)
    for col in range(4):
        P.act(dt[:, :, col], dt[:, :, col], AF.Exp, [dtb, svb], [dtb], bias=svt[:, col:col + 1])
    for col in range(4):
        P.act(dt[:, :, col], dt[:, :, col], AF.Ln, [dtb], [dtb], bias=ones[:, 0:1])
    for col in range(4):
        P.ts(DVE, adt[:, :, col], dt[:, :, col], nexpA[:, col:col + 1], None, ALU.mult, None, [dtb, nexpAb], [adtb])

    psT = C.ps("psT")
    psTb_t = C.es.enter_context(C.nc.psum_tensor("psTb", [128, 1024], BF16)); psTb = (psTb_t, Buf("psTb"))
    psSeg = [C.ps("psSeg%d" % i) for i in range(2)]
    psG = C.ps("psG"); psY = C.ps("psY"); psH = C.ps("psH"); psA = C.ps("psA")

    xin = [C.sb("xin%d" % i, [128, 3, 516]) for i in range(2)]
    acc = [C.sb("cacc%d" % i, [128, 512]) for i in range(3)]
    xc, xcb = C.sb("xc", [128, 512])
    srcs = [xs, Bm, Cm]
    for bi, (t0, T) in enumerate(TBLK):
        seg0, seg1 = (0, CTX) if t0 < CTX else (CTX, NTOK)
        lo, hi = max(seg0, t0 - 2), min(seg1, t0 + T + 2)
        xi, xib = xin[bi % 2]
        P.op(POOL, lambda e, xi=xi: e.memset(xi[:], 0.0), [], [xib])
        for i3 in range(3):
            P.load(xi[:, i3, lo - (t0 - 2):hi - (t0 - 2)], srcs[i3][:, lo:hi], xib)
        for i3 in range(3):
            a, ab = acc[i3]
            P.ts(DVE, a[:, :T], xi[:, i3, 0:T], cwt[:, i3, 0:1], None, ALU.mult, None, [xib, cwb], [ab])
            for i in range(1, 5):
                P.stt(a[:, :T], xi[:, i3, i:i + T], cwt[:, i3, i:i + 1], a[:, :T], ALU.mult, ALU.add,
                      [xib, cwb, ab], [ab])
            if i3 == 0:
                P.act(xc[:, :T], a[:, :T], AF.Silu, [ab, cbb], [xcb], bias=cbt[:, 0:1])
            elif i3 == 1:
                P.act(Bf[:, t0:t0 + T], a[:, :T], AF.Silu, [ab, cbb], [Bfb], bias=cbt[:, 1:2])
            else:
                P.act(Cf[:, t0:t0 + T], a[:, :T], AF.Silu, [ab, cbb], [Cfb], bias=cbt[:, 2:3])
        for sub in range(T // 128):
            ci = t0 // 128 + sub
            P.op(PE, lambda e, sub=sub: e.transpose(psT[0][:, 0:128], xc[:, sub * 128:(sub + 1) * 128], cm[:, 0, :]),
                 [xcb, cmb], [psT[1]], sync_same=False)
            P.op(DVE, lambda e, ci=ci: e.tensor_copy(out=x_tm[:, ci, :], in_=psT[0][:, 0:128]), [psT[1]], [x_tmb])
            P.op(PE, lambda e, ci=ci: e.transpose(psTb[0][:, 0:128], Bf[:, ci * 128:(ci + 1) * 128], identb[:]),
                 [Bfb, identbb], [psTb[1]], sync_same=False)
            P.act(B_tm[:, ci, :], psTb[0][:, 0:128], AF.Copy, [psTb[1]], [B_tmb])

    hT = [C.sb("hT%d" % h, [128, 64]) for h in range(2)]
    hTh = [C.sb("hTh%d" % h, [128, 64], BF16) for h in range(2)]
    NS = 3
    abc = [[C.sb("abc%d_%d" % (h, i), [128, 128]) for i in range(NS)] for h in range(2)]
    dec = [[C.sb("dec%d_%d" % (h, i), [128, 128]) for i in range(NS)] for h in range(2)]
    sct = [[C.sb("sct%d_%d" % (h, i), [128, 128], BF16) for i in range(NS)] for h in range(2)]
    bw = [[C.sb("bw%d_%d" % (h, i), [128, 128], BF16) for i in range(NS)] for h in range(2)]
    xdt = [[C.sb("xdt%d_%d" % (h, i), [128, 64], BF16) for i in range(NS)] for h in range(2)]
    sm = [C.sb("sm%d" % i, [128, 8]) for i in range(NS)]
    ysb = [C.sb("ysb%d" % i, [128, 128]) for i in range(NS)]
    yo = [C.sb("yo%d" % i, [128, 128]) for i in range(NS)]
    it = 0
    for d in range(2):
        order = (FWD_ORDER if d == 0 else BWD_ORDER)[:nchunks]
        Ui, nUi, Mi = (1, 2, 5) if d == 0 else (3, 4, 6)
        yout = y_f if d == 0 else y_b
        for h in range(2):
            P.op(DVE, lambda e, h=h: e.memset(hT[h][0][:], 0.0), [], [hT[h][1]])
            P.op(POOL, lambda e, h=h: e.memset(hTh[h][0][:], 0.0), [], [hTh[h][1]])
        for c in order:
            sl = it % NS
            it += 1
            cs = slice(c * 128, (c + 1) * 128)
            a2 = adt[:, c, 2 * d:2 * d + 2]
            P.mm(psA[0][:, 0:2], cm[:, Ui, :], a2, True, True, [cmb, adtb], [psA[1]])
            P.mm(psA[0][:, 2:4], ones[:], a2, True, True, [onesb, adtb], [psA[1]])
            s_, sb_ = sm[sl]
            P.act(s_[:, 0:4], psA[0][:, 0:4], AF.Exp, [psA[1]], [sb_])
            P.tt(DVE, s_[:, 6:8], psA[0][:, 2:4], psA[0][:, 0:2], ALU.subtract, [psA[1]], [sb_]) if False else None
            P.act(s_[:, 6:8], psA[0][:, 0:2], AF.Copy, [psA[1]], [sb_])
            P.tt(DVE, s_[:, 6:8], psA[0][:, 2:4], s_[:, 6:8], ALU.subtract, [psA[1], sb_], [sb_])
            P.act(s_[:, 4:6], s_[:, 6:8], AF.Exp, [sb_], [sb_])
            P.mm(psG[0][:, 0:128], Bf[:, cs], Cf[:, cs], True, True, [Bfb, Cfb], [psG[1]])
            for h in range(2):
                ab_, abb_ = abc[h][sl]
                P.ts(DVE, ab_[:], ones[:], adt[:, c, 2 * d + h:2 * d + h + 1], None, ALU.mult, None, [onesb, adtb], [abb_])
                sg, sgb = psSeg[h]
                P.mm(sg[:, 0:128], ab_[:], cm[:, Ui, :], True, False, [abb_, cmb], [sgb])
                P.mm(sg[:, 0:128], cm[:, nUi, :], ab_[:], False, False, [abb_, cmb], [sgb])
                P.mm(sg[:, 0:128], cm[:, 0, :], cm[:, Mi, :], False, True, [cmb], [sgb])
                de, deb = dec[h][sl]
                P.act(de[:], sg[:, 0:128], AF.Exp, [sgb], [deb])
                sc, scb = sct[h][sl]
                P.tt(DVE, sc[:], psG[0][:, 0:128], de[:], ALU.mult, [psG[1], deb], [scb])
                xd, xdb = xdt[h][sl]
                P.ts(POOL, xd[:], x_tm[:, c, h * 64:(h + 1) * 64], dt[:, c, 2 * d + h:2 * d + h + 1], None, ALU.mult, None,
                     [x_tmb, dtb], [xdb])
                bw_, bwb_ = bw[h][sl]
                P.ts(POOL, bw_[:], B_tm[:, c, :], s_[:, 4 + h:5 + h], None, ALU.mult, None, [B_tmb, sb_], [bwb_])
                P.mm(psY[0][:, h * 64:(h + 1) * 64], sc[:], xd[:], True, True, [scb, xdb], [psY[1]])
                P.mm(psY[0][:, 128 + h * 64:128 + (h + 1) * 64], Cf[:, cs], hTh[h][0][:], True, True,
                     [Cfb, hTh[h][1]], [psY[1]])
                P.mm(psH[0][:, h * 64:(h + 1) * 64], bw_[:], xd[:], True, True, [bwb_, xdb], [psH[1]])
            ys, ysb_ = ysb[sl]
            P.act(ys[:], psY[0][:, 0:128], AF.Copy, [psY[1]], [ysb_])
            yo_, yob_ = yo[sl]
            for h in range(2):
                hs = slice(h * 64, (h + 1) * 64)
                P.stt(yo_[:, hs], psY[0][:, 128 + h * 64:128 + (h + 1) * 64], s_[:, h:h + 1], ys[:, hs], ALU.mult, ALU.add,
                      [psY[1], sb_, ysb_], [yob_])
                if d == 0:
                    P.stt(yo_[:, hs], x_tm[:, c, hs], svt[:, 8 + h:9 + h], yo_[:, hs], ALU.mult, ALU.add,
                          [x_tmb, svb, yob_], [yob_])
                P.stt(hT[h][0][:], hT[h][0][:], s_[:, 2 + h:3 + h], psH[0][:, hs], ALU.mult, ALU.add,
                      [hT[h][1], sb_, psH[1]], [hT[h][1]])
                P.op(POOL, lambda e, h=h: e.tensor_copy(out=hTh[h][0][:], in_=hT[h][0][:]), [hT[h][1]], [hTh[h][1]])
            P.store(yout[cs, :], yo_[:], yob_, eng=SP)
    print("ssd program ninstr", P.ninstr, "nsem", P.nsem)
    return C.done()


def ssd_consts():
    if "s" in _CONST:
        return _CONST["s"]
    cmat = np.zeros((8, 128, 128), np.float32)
    idx = np.arange(128)
    cmat[0] = np.eye(128, dtype=np.float32)
    U = (idx[:, None] <= idx[None, :]).astype(np.float32)
    cmat[1] = U; cmat[2] = -U; cmat[3] = U.T; cmat[4] = -U.T
    cmat[5] = np.where(idx[None, :] >= idx[:, None], 0.0, -30000.0)
    cmat[6] = np.where(idx[None, :] <= idx[:, None], 0.0, -30000.0)
    _CONST["s"] = cmat
    return cmat


def ssd_inputs(PT, inp, l, j):
    gi = j // 2
    o = OFF["sxbc"]
    cols = [np.arange(2 * j * 64, (2 * j + 2) * 64), 512 + gi * 128 + np.arange(128), 768 + gi * 128 + np.arange(128)]
    cwv = np.stack([inp["ssd_conv_w"][l][:, cc].T for cc in cols], axis=1)
    cbv = np.stack([inp["ssd_conv_b"][l][cc] for cc in cols], axis=1)
    sv = np.zeros((128, 10), np.float32)
    hh = [2 * j, 2 * j + 1]
    for d in range(2):
        for h in range(2):
            sv[:, 2 * d + h] = inp["ssd_dt_bias"][l][d, hh[h]]
            sv[:, 4 + 2 * d + h] = inp["ssd_a_log"][l][d, hh[h]]
    sv[:, 8] = inp["ssd_d"][l][hh[0]]; sv[:, 9] = inp["ssd_d"][l][hh[1]]
    dcols = [OFF["sdt"] + d * 8 + hh[h] for d in range(2) for h in range(2)]
    return {
        "xs": np.ascontiguousarray(PT[o + cols[0]]), "Bm": np.ascontiguousarray(PT[o + cols[1]]),
        "Cm": np.ascontiguousarray(PT[o + cols[2]]),
        "cw": np.ascontiguousarray(cwv), "cb": np.ascontiguousarray(cbv),
        "dtr": np.ascontiguousarray(PT[dcols].T), "sv": sv, "cmat": ssd_consts(),
    }


RC = 64
DECAY_SCALE = 0.606531
LN_EPS = 64e-5


def rwkv_dir_blocks(d):
    out = []
    if d == 0:
        for (t0, T) in TBLK:
            out.append((t0, T, [o for o in range(0, T, RC)]))
    else:
        out.append((0, 256, [o for o in range(256 - RC, -1, -RC)]))
        for (t0, T) in reversed(TBLK[1:]):
            out.append((t0, T, [o for o in range(T - RC, -1, -RC)]))
    return out


def build_rwkv(nsteps=None, phases=(1, 2, 3), dbg=9, dbg_scr=False):
    C = Ctx("rwkv")
    P = C.P
    nc = C.nc
    names = ["r", "k", "v", "lrw", "lra", "gd"]
    src = {n: C.din("rw_" + n, [128, NTOK]) for n in names}
    cw = C.din("cw", [128, 6, 3])
    vec = C.din("vec", [128, 12])
    wup = C.din("wup", [128, 128]); aup = C.din("aup", [128, 128]); gup = C.din("gup", [128, 128])
    cmat = C.din("cmat", [3, 128, 128])
    msk = C.din("msk", [64, 2, 640])
    o_rwkv = C.dout("o_rwkv", [128, NTOK])
    snames = ["r", "v", "kk", "lw0", "lw1", "kt0", "kt1", "be0", "be1", "bonus", "g", "yf", "yb"]
    scr = {n: nc.dram_tensor("scr_" + n, [128, NTOK], F32, kind=("ExternalOutput" if dbg_scr else "Internal")).ap()
           for n in snames}
    scrb = {n: Buf("scr_" + n) for n in snames}

    cm, cmb = C.sb("cm", [128, 3, 128]); P.load(cm[:], cmat.rearrange("m p c -> p m c"), cmb)
    mk, mkb = C.sb("mk", [64, 2, 640]); P.load(mk[:], msk, mkb)
    vc, vcb = C.sb("vc", [128, 12]); P.load(vc[:], vec, vcb)
    P.ts(DVE, vc[:, 6:7], vc[:, 5:6], -1.0, 1.0, ALU.mult, ALU.add, [vcb], [vcb])
    cwt, cwb = C.sb("cwt", [128, 6, 3]); P.load(cwt[:], cw, cwb)
    wu, wub = C.sb("wu", [128, 128]); P.load(wu[:], wup, wub)
    au, aub = C.sb("au", [128, 128]); P.load(au[:], aup, aub)
    gu, gub = C.sb("gu", [128, 128]); P.load(gu[:], gup, gub)
    ones, onesb = C.sb("ones", [128, 64]); P.op(DVE, lambda e: e.memset(ones[:], 1.0), [], [onesb])
    lneps, lnepsb = C.sb("lneps", [128, 1]); P.op(DVE, lambda e: e.memset(lneps[:], LN_EPS), [], [lnepsb])
    idr, idrb = C.sb("idr", [64, 2, 64])
    for h in range(2):
        P.op(DVE, lambda e, h=h: e.tensor_copy(out=idr[:, h, :], in_=cm[0:64, 0, 0:64]), [cmb], [idrb])

    ps = [C.ps("rps%d" % i) for i in range(8)]

    if 1 in phases:
        with contextlib.ExitStack() as es:
            xin = [C.sb("rxin%d" % i, [128, 6, 514], F32, es) for i in range(2)]
            cv = {n: C.sb("rcv_" + n, [128, 512], F32, es) for n in names}
            NT1 = 10
            t1 = [C.sb("rt1_%d" % i, [128, 512], F32, es) for i in range(NT1)]
            c1 = [0]

            def tmp():
                t = t1[c1[0] % NT1]
                c1[0] += 1
                return t

            def put(name, t, tb, t0, T):
                tok = P.dma(POOL, lambda e: e.dma_start(out=scr[name][:, t0:t0 + T], in_=t[:, :T]), tb, [tb], [scrb[name]])
                return tok

            for bi, (t0, T) in enumerate(TBLK):
                seg0, seg1 = (0, CTX) if t0 < CTX else (CTX, NTOK)
                lo, hi = max(seg0, t0 - 1), min(seg1, t0 + T + 1)
                xi, xib = xin[bi % 2]
                P.op(POOL, lambda e, xi=xi: e.memset(xi[:], 0.0), [], [xib])
                for i6, n in enumerate(names):
                    P.load(xi[:, i6, lo - (t0 - 1):hi - (t0 - 1)], src[n][:, lo:hi], xib)
                for i6, n in enumerate(names):
                    a, ab = cv[n]
                    P.ts(DVE, a[:, :T], xi[:, i6, 0:T], cwt[:, i6, 0:1], None, ALU.mult, None, [xib, cwb], [ab])
                    for i in range(1, 3):
                        P.stt(a[:, :T], xi[:, i6, i:i + T], cwt[:, i6, i:i + 1], a[:, :T], ALU.mult, ALU.add,
                              [xib, cwb, ab], [ab])
                r_, rb_ = cv["r"]; k_, kb_ = cv["k"]; v_, vb_ = cv["v"]
                put("r", r_, rb_, t0, T); put("v", v_, vb_, t0, T)
                tw, twb = tmp()
                P.act(tw[:, :T], cv["lrw"][0][:, :T], AF.Tanh, [cv["lrw"][1]], [twb])
                a_d = []
                for d in range(2):
                    hs = slice(d * 64, (d + 1) * 64)
                    P.mm(ps[d][0][:, :T], wu[hs, :], tw[hs, :T], True, True, [wub, twb], [ps[d][1]])
                    lw, lwb = tmp()
                    P.act(lw[:, :T], ps[d][0][:, :T], AF.Sigmoid, [ps[d][1], vcb], [lwb], bias=vc[:, d:d + 1])
                    P.ts(DVE, lw[:, :T], lw[:, :T], -DECAY_SCALE, None, ALU.mult, None, [lwb], [lwb])
                    put("lw%d" % d, lw, lwb, t0, T)
                    P.mm(ps[2 + d][0][:, :T], au[hs, :], cv["lra"][0][hs, :T], True, True, [aub, cv["lra"][1]],
                         [ps[2 + d][1]])
                    a, ab = tmp()
                    P.act(a[:, :T], ps[2 + d][0][:, :T], AF.Sigmoid, [ps[2 + d][1], vcb], [ab], bias=vc[:, 2 + d:3 + d])
                    a_d.append((a, ab))
                kq, kqb = tmp()
                P.ts(DVE, kq[:, :T], k_[:, :T], vc[:, 4:5], None, ALU.mult, None, [kb_, vcb], [kqb])
                sq, sqb = tmp()
                P.act(sq[:, :T], kq[:, :T], AF.Square, [kqb], [sqb])
                P.mm(ps[4][0][:, :T], cm[:, 1, :], sq[:, :T], True, True, [cmb, sqb], [ps[4][1]])
                P.act(sq[:, :T], ps[4][0][:, :T], AF.Sqrt, [ps[4][1]], [sqb])
                P.ts(DVE, sq[:, :T], sq[:, :T], 1e-12, None, ALU.max, None, [sqb], [sqb])
                P.op(DVE, lambda e, sq=sq, T=T: e.reciprocal(out=sq[:, :T], in_=sq[:, :T]), [sqb], [sqb])
                P.tt(DVE, kq[:, :T], kq[:, :T], sq[:, :T], ALU.mult, [kqb, sqb], [kqb])
                put("kk", kq, kqb, t0, T)
                for d in range(2):
                    a, ab = a_d[d]
                    u, ub = tmp()
                    P.ts(DVE, u[:, :T], a[:, :T], vc[:, 5:6], vc[:, 6:7], ALU.mult, ALU.add, [ab, vcb], [ub])
                    P.tt(POOL, u[:, :T], u[:, :T], k_[:, :T], ALU.mult, [ub, kb_], [ub])
                    put("kt%d" % d, u, ub, t0, T)
                    be, beb = tmp()
                    P.tt(POOL, be[:, :T], a[:, :T], kq[:, :T], ALU.mult, [ab, kqb], [beb])
                    put("be%d" % d, be, beb, t0, T)
                rk, rkb = tmp()
                P.stt(rk[:, :T], r_[:, :T], vc[:, 7:8], k_[:, :T], ALU.mult, ALU.mult, [rb_, vcb, kb_], [rkb])
                P.mm(ps[5][0][:, :T], cm[:, 1, :], rk[:, :T], True, True, [cmb, rkb], [ps[5][1]])
                P.tt(DVE, rk[:, :T], ps[5][0][:, :T], v_[:, :T], ALU.mult, [ps[5][1], vb_], [rkb])
                put("bonus", rk, rkb, t0, T)
                sg, sgb = tmp()
                P.act(sg[:, :T], cv["gd"][0][:, :T], AF.Sigmoid, [cv["gd"][1]], [sgb])
                P.mm(ps[6][0][:, :T], gu[:], sg[:, :T], True, True, [gub, sgb], [ps[6][1]])
                P.act(sg[:, :T], ps[6][0][:, :T], AF.Copy, [ps[6][1]], [sgb])
                put("g", sg, sgb, t0, T)

    P.barrier()
    if 2 in phases:
        with contextlib.ExitStack() as es:
            ST = [C.sb("ST%d" % d, [64, 128], F32, es) for d in range(2)]
            blk = [[C.sb("rblk%d_%d" % (d, i), [64, 6, 2, 512], F32, es) for i in range(2)] for d in range(2)]
            YB = [[C.sb("rYB%d_%d" % (d, i), [64, 2, 512], F32, es) for i in range(2)] for d in range(2)]

            def mk2(name, shape, n=2):
                return [[C.sb("%s%d_%d" % (name, d, i), shape, F32, es) for i in range(n)] for d in range(2)]
            CL = mk2("rCL", [64, 3, 2, 64]); EE = mk2("rEE", [64, 4, 2, 64]); ET = mk2("rET", [64, 2])
            KR = mk2("rKR", [64, 2, 128]); BK = mk2("rBK", [64, 2, 2, 64]); KB2 = mk2("rKB2", [64, 2, 2, 64])
            PW = mk2("rPW", [64, 512]); MS = mk2("rMS", [64, 128]); TR = mk2("rTR", [64, 384])
            ZS = mk2("rZS", [64, 128]); PP = mk2("rPP", [64, 256], 4); XS = mk2("rXS", [64, 128]); SAn = mk2("rSAn", [64, 128])
            for d in range(2):
                P.op(DVE, lambda e, d=d: e.memset(ST[d][0][:], 0.0), [], [ST[d][1]])
            seqs = []
            for d in range(2):
                lst = []
                for bidx, (t0, T, offs) in enumerate(rwkv_dir_blocks(d)):
                    for oi, o in enumerate(offs):
                        lst.append((bidx, t0, T, o, oi == 0, oi == len(offs) - 1))
                seqs.append(lst)
            nst = len(seqs[0]) if nsteps is None else nsteps
            snm = [["r", "v", "kk", "lw0", "kt0", "be0"], ["r", "v", "kk", "lw1", "kt1", "be1"]]
            ppc = [0, 0]
            id64 = cm[0:64, 0, 0:64]
            for i in range(nst):
                sl = i % 2
                for d in range(2):
                    bidx, t0, T, o, first, last = seqs[d][i]
                    b_, bb_ = blk[d][bidx % 2]
                    if first:
                        for i6, n in enumerate(snm[d]):
                            P.load(b_[:, i6, :, :T], scr[n][:, t0:t0 + T].rearrange("(h c) t -> c h t", h=2), bb_,
                                   reads=[scrb[n]], writes=[bb_])
                    cs = slice(o, o + RC)
                    cl, clb = CL[d][sl]; ee, eeb = EE[d][sl]; et, etb = ET[d][sl]
                    lw = b_[:, 3, :, cs]
                    for h in range(2):
                        P.op(DVE, lambda e, cl=cl, b_=b_, cs=cs, h=h: e.tensor_tensor_scan(
                            out=cl[:, 0, h, :], data0=ones[0:64, :], data1=b_[:, 3, h, cs], initial=0.0,
                            op0=ALU.mult, op1=ALU.add), [onesb, bb_], [clb])
                    if d == 1:
                        for h in range(2):
                            P.ts(DVE, cl[:, 2, h, :], cl[:, 0, h, :], -1.0, cl[:, 0, h, 63:64], ALU.mult, ALU.add, [clb], [clb])
                        P.tt(DVE, cl[:, 1], cl[:, 2], lw, ALU.add, [clb, bb_], [clb])
                        cin, cex = cl[:, 1], cl[:, 2]
                    else:
                        P.tt(DVE, cl[:, 1], cl[:, 0], lw, ALU.subtract, [clb, bb_], [clb])
                        cin, cex = cl[:, 0], cl[:, 1]
                    P.act(ee[:, 0], cin, AF.Exp, [clb], [eeb])
                    P.act(ee[:, 1], cex, AF.Exp, [clb], [eeb])
                    P.act(ee[:, 2], cin, AF.Exp, [clb], [eeb], scale=-1.0)
                    for h in range(2):
                        tot = cl[:, 0, h, 63:64]
                        P.act(ee[:, 3, h, :], cin[:, h, :], AF.Exp, [clb], [eeb], scale=-1.0, bias=tot)
                        P.act(et[:, h:h + 1], tot, AF.Exp, [clb], [etb])
                    kr, krb = KR[d][sl]; bk, bkb = BK[d][sl]; kb2, kb2b = KB2[d][sl]
                    P.tt(DVE, kr[:, :, 0:64], b_[:, 2, :, cs], ee[:, 1], ALU.mult, [bb_, eeb], [krb])
                    P.tt(POOL, kr[:, :, 64:128], b_[:, 0, :, cs], ee[:, 0], ALU.mult, [bb_, eeb], [krb])
                    P.tt(DVE, bk[:, 0], b_[:, 5, :, cs], ee[:, 2], ALU.mult, [bb_, eeb], [bkb])
                    P.tt(POOL, bk[:, 1], b_[:, 4, :, cs], ee[:, 2], ALU.mult, [bb_, eeb], [bkb])
                    P.tt(DVE, kb2[:, 0], b_[:, 4, :, cs], ee[:, 3], ALU.mult, [bb_, eeb], [kb2b])
                    P.tt(POOL, kb2[:, 1], b_[:, 5, :, cs], ee[:, 3], ALU.mult, [bb_, eeb], [kb2b])
                for d in range(2 if dbg >= 2 else 0):
                    bidx, t0, T, o, first, last = seqs[d][i]
                    b_, bb_ = blk[d][bidx % 2]
                    cs = slice(o, o + RC)
                    kr, krb = KR[d][sl]; bk, bkb = BK[d][sl]; kb2, kb2b = KB2[d][sl]
                    pa, pab = ps[4 * d + 0]; pb, pbb = ps[4 * d + 1]; pd, pdb = ps[4 * d + 3]
                    for h in range(2):
                        P.mm(pa[0:64, h * 128:(h + 1) * 128], bk[:, 0, h, :], kr[:, h, :], True, True, [bkb, krb], [pab])
                        P.mm(pa[0:64, 256 + h * 128:256 + (h + 1) * 128], bk[:, 1, h, :], kr[:, h, :], True, True,
                             [bkb, krb], [pab])
                        P.mm(pb[0:64, h * 64:(h + 1) * 64], kr[:, h, 0:64], bk[:, 0, h, :], True, True, [krb, bkb], [pbb])
                    for h in range(2):
                        for (srcap, rb, c0) in [(kb2[:, 0, h, :], kb2b, 64 + h * 64), (kb2[:, 1, h, :], kb2b, 192 + h * 64),
                                                (b_[:, 1, h, cs], bb_, 320 + h * 64)]:
                            P.op(PE, lambda e, pd=pd, srcap=srcap, c0=c0: e.transpose(pd[0:64, c0:c0 + 64], srcap, id64),
                                 [rb, cmb], [pdb], sync_same=False)
                    pw, pwb = PW[d][sl]; ms, msb = MS[d][sl]; tr, trb = TR[d][sl]; zs, zsb = ZS[d][sl]
                    P.tt(DVE, pw[:], pa[0:64, :], mk[:, d, 0:512], ALU.mult, [pab, mkb], [pwb])
                    P.tt(DVE, ms[:], pb[0:64, 0:128], mk[:, d, 512:640], ALU.mult, [pbb, mkb], [msb])
                    P.act(tr[:], pd[0:64, 64:448], AF.Copy, [pdb], [trb])
                    mt_v = pw[:, 0:256].rearrange("p (h x) -> p h x", h=2)[:, :, 0:64]
                    P.tt(POOL, zs[:].rearrange("p (h x) -> p h x", h=2), idr[:], mt_v, ALU.subtract, [idrb, pwb], [zsb])
                cur = []
                for d in range(2):
                    pw, pwb = PW[d][sl]; ms, msb = MS[d][sl]
                    cur.append(([pw[:, h * 128:h * 128 + 64] for h in range(2)], pwb,
                                [ms[:, h * 64:(h + 1) * 64] for h in range(2)], msb))
                for kk_ in range(1, 6 if dbg >= 3 else 1):
                    for d in range(2):
                        pb, pbb = ps[4 * d + 1]
                        Pm, Pmb, PTm, PTmb = cur[d]
                        zs, zsb = ZS[d][sl]
                        for h in range(2):
                            P.mm(pb[0:64, 128 + h * 64:128 + (h + 1) * 64], PTm[h], Pm[h], True, True, [PTmb, Pmb], [pbb])
                            P.mm(pb[0:64, 256 + h * 64:256 + (h + 1) * 64], Pm[h], PTm[h], True, True, [PTmb, Pmb], [pbb])
                        pp, ppb = PP[d][ppc[d] % 4]
                        ppc[d] += 1
                        P.op(DVE, lambda e, pp=pp, pb=pb: e.tensor_copy(out=pp[:], in_=pb[0:64, 128:384]), [pbb], [ppb])
                        cur[d] = ([pp[:, h * 64:(h + 1) * 64] for h in range(2)], ppb,
                                  [pp[:, 128 + h * 64:128 + (h + 1) * 64] for h in range(2)], ppb)
                        for h in range(2):
                            P.mm(pb[0:64, 384 + h * 64:384 + (h + 1) * 64], cur[d][2][h], zs[:, h * 64:(h + 1) * 64],
                                 True, True, [ppb, zsb], [pbb])
                        P.tt(DVE, zs[:], zs[:], pb[0:64, 384:512], ALU.add, [zsb, pbb], [zsb])
                for d in range(2 if dbg >= 4 else 0):
                    bidx, t0, T, o, first, last = seqs[d][i]
                    kr, krb = KR[d][sl]; pw, pwb = PW[d][sl]; tr, trb = TR[d][sl]; zs, zsb = ZS[d][sl]
                    et, etb = ET[d][sl]
                    st, stb = ST[d]
                    pc, pcb = ps[4 * d + 2]
                    xs_, xsb_ = XS[d][sl]; sa, sab = SAn[d][sl]
                    for h in range(2):
                        hs = slice(h * 64, (h + 1) * 64)
                        P.mm(pc[0:64, hs], kr[:, h, 0:64], st[:, hs], True, False, [krb, stb], [pcb])
                        P.mm(pc[0:64, hs], pw[:, 256 + h * 128:256 + h * 128 + 64], tr[:, 256 + h * 64:256 + (h + 1) * 64],
                             False, True, [pwb, trb], [pcb])
                    P.op(DVE, lambda e, xs_=xs_, pc=pc: e.tensor_copy(out=xs_[:], in_=pc[0:64, 0:128]), [pcb], [xsb_])
                    for h in range(2):
                        hs = slice(h * 64, (h + 1) * 64)
                        P.mm(pc[0:64, 128 + h * 64:128 + (h + 1) * 64], zs[:, hs], xs_[:, hs], True, True, [zsb, xsb_], [pcb])
                    P.ts(DVE, sa[:], pc[0:64, 128:256], -1.0, None, ALU.mult, None, [pcb], [sab])
                    for h in range(2):
                        hs = slice(h * 64, (h + 1) * 64)
                        yo_ = pc[0:64, 384 + h * 64:384 + (h + 1) * 64]
                        P.mm(yo_, st[:, hs], kr[:, h, 64:128], True, False, [stb, krb], [pcb])
                        P.mm(yo_, tr[:, 256 + h * 64:256 + (h + 1) * 64],
                             pw[:, 256 + h * 128 + 64:256 + (h + 1) * 128], False, False, [trb, pwb], [pcb])
                        P.mm(yo_, sa[:, hs], pw[:, h * 128 + 64:(h + 1) * 128], False, True, [sab, pwb], [pcb])
                    for h in range(2):
                        hs = slice(h * 64, (h + 1) * 64)
                        su_ = pc[0:64, 256 + h * 64:256 + (h + 1) * 64]
                        P.mm(su_, tr[:, hs], tr[:, 256 + h * 64:256 + (h + 1) * 64], True, False, [trb], [pcb])
                        P.mm(su_, tr[:, 128 + h * 64:128 + (h + 1) * 64], sa[:, hs], False, True, [trb, sab], [pcb])
                    for h in range(2):
                        hs = slice(h * 64, (h + 1) * 64)
                        P.stt(st[:, hs], st[:, hs], et[:, h:h + 1], pc[0:64, 256 + h * 64:256 + (h + 1) * 64], ALU.mult, ALU.add,
                              [stb, etb, pcb], [stb])
                    yb, ybb = YB[d][bidx % 2]
                    P.op(DVE, lambda e, yb=yb, pc=pc, o=o: e.tensor_copy(
                        out=yb[:, :, o:o + RC], in_=pc[0:64, 384:512].rearrange("p (h x) -> p h x", h=2)), [pcb], [ybb])
                    if last:
                        nm = "yf" if d == 0 else "yb"
                        P.dma(SP, lambda e, nm=nm, t0=t0, T=T, yb=yb: e.dma_start(
                            out=scr[nm][:, t0:t0 + T].rearrange("(h c) t -> c h t", h=2), in_=yb[:, :, :T]),
                            ybb, [ybb], [scrb[nm]])

    P.barrier()
    if 3 in phases:
        with contextlib.ExitStack() as es:
            NT3 = 8
            t3 = [C.sb("rt3_%d" % i, [128, 512], F32, es) for i in range(NT3)]
            c3 = [0]

            def tmp3():
                t = t3[c3[0] % NT3]
                c3[0] += 1
                return t
            for bi, (t0, T) in enumerate(TBLK):
                yf, yfb = tmp3(); yb, ybb = tmp3(); bo, bob = tmp3(); g, gb = tmp3()
                P.load(yf[:, :T], scr["yf"][:, t0:t0 + T], yfb, reads=[scrb["yf"]], writes=[yfb])
                P.load(yb[:, :T], scr["yb"][:, t0:t0 + T], ybb, reads=[scrb["yb"]], writes=[ybb])
                P.load(bo[:, :T], scr["bonus"][:, t0:t0 + T], bob, reads=[scrb["bonus"]], writes=[bob])
                P.load(g[:, :T], scr["g"][:, t0:t0 + T], gb, reads=[scrb["g"]], writes=[gb])
                P.tt(DVE, yf[:, :T], yf[:, :T], yb[:, :T], ALU.add, [yfb, ybb], [yfb])
                P.mm(ps[0][0][:, :T], cm[:, 2, :], yf[:, :T], True, True, [cmb, yfb], [ps[0][1]])
                P.tt(DVE, yf[:, :T], yf[:, :T], ps[0][0][:, :T], ALU.subtract, [yfb, ps[0][1]], [yfb])
                P.act(yb[:, :T], yf[:, :T], AF.Square, [yfb], [ybb])
                P.mm(ps[1][0][:, :T], cm[:, 2, :], yb[:, :T], True, True, [cmb, ybb], [ps[1][1]])
                P.act(yb[:, :T], ps[1][0][:, :T], AF.Sqrt, [ps[1][1], lnepsb], [ybb], bias=lneps[:])
                P.op(DVE, lambda e, yb=yb, T=T: e.reciprocal(out=yb[:, :T], in_=yb[:, :T]), [ybb], [ybb])
                P.stt(yf[:, :T], yf[:, :T], vc[:, 8:9], yb[:, :T], ALU.mult, ALU.mult, [yfb, vcb, ybb], [yfb])
                P.stt(yf[:, :T], yf[:, :T], vc[:, 9:10], bo[:, :T], ALU.add, ALU.add, [yfb, vcb, bob], [yfb])
                P.tt(DVE, yf[:, :T], yf[:, :T], g[:, :T], ALU.mult, [yfb, gb], [yfb])
                P.store(o_rwkv[:, t0:t0 + T], yf[:, :T], yfb, eng=SP)
    print("rwkv program ninstr", P.ninstr, "nsem", P.nsem)
    return C.done()


def rwkv_consts():
    if "r" in _CONST:
        return _CONST["r"]
    cmat = np.zeros((3, 128, 128), np.float32)
    cmat[0] = np.eye(128, dtype=np.float32)
    for b in range(2):
        cmat[1, b * 64:(b + 1) * 64, b * 64:(b + 1) * 64] = 1.0
        cmat[2, b * 64:(b + 1) * 64, b * 64:(b + 1) * 64] = 1.0 / 64
    j = np.arange(64)[:, None]; t = np.arange(64)[None, :]
    msk = np.zeros((64, 2, 640), np.float32)
    strict = [(t > j), (t < j)]; incl = [(t >= j), (t <= j)]
    for d in range(2):
        for s in range(8):
            msk[:, d, s * 64:(s + 1) * 64] = (strict[d] if s % 2 == 0 else incl[d])
        mm = (t < j) if d == 0 else (t > j)
        for h in range(2):
            msk[:, d, 512 + h * 64:512 + (h + 1) * 64] = mm
    _CONST["r"] = (cmat, msk)
    return _CONST["r"]


def rwkv_inputs(PT, inp, l, j):
    cmat, msk = rwkv_consts()
    o = OFF["rw"]
    hc = np.arange(2 * j * 64, (2 * j + 2) * 64)
    rows = {"r": o + hc, "k": o + 512 + hc, "v": o + 1024 + hc, "lrw": o + 1536 + np.arange(128),
            "lra": o + 1536 + 128 + np.arange(128), "gd": o + 1536 + 256 + np.arange(128)}
    d = {"rw_" + n: np.ascontiguousarray(PT[rows[n]]) for n in rows}
    sw = inp["rwkv_shift_w"][l]
    d["cw"] = np.ascontiguousarray(np.stack([sw[:, rows[n] - o].T for n in ["r", "k", "v", "lrw", "lra", "gd"]], axis=1))
    vec = np.zeros((128, 12), np.float32)
    vec[:, 0] = inp["rwkv_w0"][l][0, hc]; vec[:, 1] = inp["rwkv_w0"][l][1, hc]
    vec[:, 2] = inp["rwkv_a0"][l][0, hc]; vec[:, 3] = inp["rwkv_a0"][l][1, hc]
    vec[:, 4] = inp["rwkv_k_k"][l][hc]; vec[:, 5] = inp["rwkv_k_a"][l][hc]
    vec[:, 7] = inp["rwkv_r_k"][l].reshape(-1)[hc]
    vec[:, 8] = inp["rwkv_ln_g"][l][hc]; vec[:, 9] = inp["rwkv_ln_b"][l][hc]
    d["vec"] = vec
    d["wup"] = np.ascontiguousarray(np.concatenate([inp["rwkv_w_up"][l][0][:, hc], inp["rwkv_w_up"][l][1][:, hc]], 0))
    d["aup"] = np.ascontiguousarray(np.concatenate([inp["rwkv_a_up"][l][0][:, hc], inp["rwkv_a_up"][l][1][:, hc]], 0))
    d["gup"] = np.ascontiguousarray(inp["rwkv_g_up"][l][:, hc])
    d["cmat"] = cmat; d["msk"] = msk
    return d


_PROGS = {}


def _prog(key, fn):
    if key not in _PROGS:
        _PROGS[key] = fn()
    return _PROGS[key]


def _run(nc, in_maps):
    return run_bass_kernel_spmd(nc, in_maps, core_ids=list(range(NCORE))).results


def _big_weights(inp):
    lst = []
    for l in range(2):
        for i in range(2):
            lst += [("wg_%d_%d" % (l, i), inp["ffn_w_gate"][l, i]), ("wu_%d_%d" % (l, i), inp["ffn_w_up"][l, i]),
                    ("wd_%d_%d" % (l, i), inp["ffn_w_down"][l, i])]
        lst += [("win_%d" % l, inp["w_in"][l]), ("wout_%d" % l, inp["w_out"][l])]
    return lst


def kernel(**inp):
    inp = {k: np.asarray(v) for k, v in inp.items()}
    big = _big_weights(inp)
    per_core = sum(w.size for _, w in big) // NCORE
    L = -(-per_core // (128 * CONV_PIECE)) * CONV_PIECE
    cs = np.concatenate([inp["c"], inp["c_ctx"][None]], 0)
    cst = np.ascontiguousarray(cs.reshape(3, KC, 128).transpose(2, 1, 0))
    in_maps = []
    for c in range(NCORE):
        flat = np.zeros(128 * L, np.float32)
        o = 0
        for _, w in big:
            r = w.shape[0] // NCORE
            sl = w[c * r:(c + 1) * r].ravel()
            flat[o:o + sl.size] = sl
            o += sl.size
        cols = slice(c * MODC, (c + 1) * MODC)
        in_maps.append({
            "cst": cst,
            "mw": np.ascontiguousarray(inp["mod_w"][:, :, cols].reshape(2, KC, 128, MODC)),
            "mb": np.ascontiguousarray(np.broadcast_to(inp["mod_b"].reshape(2, 1, -1)[:, :, cols], (2, 3, MODC))),
            "wf": flat.reshape(128, L),
        })
    res = _run(_prog(("prep", L), lambda: build_prep(L)), in_maps)
    del in_maps
    m = np.concatenate([r["m_out"] for r in res], axis=2)
    m9 = m.reshape(2, 3, 9, D)
    wbf = {}
    flats = [np.asarray(r["wb"]).reshape(-1) for r in res]
    o = 0
    for name, w in big:
        r = w.shape[0] // NCORE
        n = r * w.shape[1]
        wbf[name] = np.concatenate([f[o:o + n].reshape(r, w.shape[1]) for f in flats], axis=0)
        o += n
    del flats, res
    wl = {}
    for name, w in wbf.items():
        if name.startswith("wd_") or name.startswith("wout_"):
            wl[name] = lay_wd(w)
        else:
            wl[name] = lay_w22(w)
    del wbf
    normg = np.ascontiguousarray(lay_vec(inp["norm_g"]))
    modv = []
    for b in range(2):
        mv = np.zeros((128, 2, 2, 9, KC), np.float32)
        for l in range(2):
            mv[:, l, 0] = lay_vec(m9[l, 2])
            mv[:, l, 1] = lay_vec(m9[l, b])
        modv.append(mv)

    z_full = np.concatenate([inp["ctx"], inp["x"]], axis=1)
    toks = [core_tokens(c) for c in range(NCORE)]
    zT = [fm(z_full[c // 4][toks[c]]) for c in range(NCORE)]
    del z_full

    def token_launch(stages, zT, extra):
        nc = _prog(("token",) + tuple(stages), lambda: build_token(stages))
        in_maps = []
        for c in range(NCORE):
            d = {"zT": zT[c], "modv": modv[c // 4], "normg": normg}
            for s in stages:
                if s[0] == "ffn":
                    for p in ("wg", "wu", "wd"):
                        nm = "%s_%d_%d" % (p, s[1], s[2])
                        d[nm] = wl[nm]
                elif s[0] == "inproj":
                    d["win_%d" % s[1]] = wl["win_%d" % s[1]]
                elif s[0] == "outproj":
                    d["wout_%d" % s[1]] = wl["wout_%d" % s[1]]
            d.update(extra[c])
            in_maps.append(d)
        return _run(nc, in_maps)

    def assemble_PT(res, l):
        PT = np.zeros((2, INP, NTOK), np.float32)
        for c in range(NCORE):
            PT[c // 4][:, toks[c]] = np.asarray(res[c]["pT_%d" % l]).reshape(INP, NT_CORE)
        return PT

    def mixers(PT, l):
        oT = np.zeros((2, D, NTOK), np.float32)
        ybT = np.zeros((2, 512, NTOK), np.float32)
        ra = _run(_prog("attn", build_attn), [attn_inputs(PT[c // 4], inp, l, c % 4) for c in range(NCORE)])
        for c in range(NCORE):
            b, j = c // 4, c % 4
            oT[b, j * 128:(j + 1) * 128] = ra[c]["o_diff"]
            oT[b, 1024 + j * 128:1024 + (j + 1) * 128] = ra[c]["o_mla"]
        del ra
        rr = _run(_prog("rwkv", build_rwkv), [rwkv_inputs(PT[c // 4], inp, l, c % 4) for c in range(NCORE)])
        for c in range(NCORE):
            b, j = c // 4, c % 4
            oT[b, 512 + j * 128:512 + (j + 1) * 128] = rr[c]["o_rwkv"]
        del rr
        rs = _run(_prog("ssd", build_ssd), [ssd_inputs(PT[c // 4], inp, l, c % 4) for c in range(NCORE)])
        for c in range(NCORE):
            b, j = c // 4, c % 4
            oT[b, 1536 + j * 128:1536 + (j + 1) * 128] = np.asarray(rs[c]["y_f"]).T
            ybT[b, j * 128:(j + 1) * 128] = np.asarray(rs[c]["y_b"]).T
        return oT, ybT

    def outproj_extra(oT, ybT, PT, l):
        ex = []
        sg = np.ascontiguousarray(inp["ssd_norm_g"][l].reshape(4, 128).T)
        for c in range(NCORE):
            b = c // 4
            ex.append({
                "oT_%d" % l: np.ascontiguousarray(oT[b][:, toks[c]].reshape(KC, 128, NT_CORE)),
                "ybT_%d" % l: np.ascontiguousarray(ybT[b][:, toks[c]].reshape(4, 128, NT_CORE)),
                "zgT_%d" % l: np.ascontiguousarray(PT[b][OFF["sz"]:OFF["sz"] + 512][:, toks[c]].reshape(4, 128, NT_CORE)),
                "ssdg_%d" % l: sg,
            })
        return ex

    res = token_launch([("ffn", 0, 0), ("inproj", 0)], zT, [{} for _ in range(NCORE)])
    zT = [np.asarray(r["z_out"]) for r in res]
    PT = assemble_PT(res, 0)
    del res
    oT, ybT = mixers(PT, 0)
    res = token_launch([("outproj", 0), ("ffn", 0, 1), ("ffn", 1, 0), ("inproj", 1)], zT, outproj_extra(oT, ybT, PT, 0))
    zT = [np.asarray(r["z_out"]) for r in res]
    PT = assemble_PT(res, 1)
    del res
    oT, ybT = mixers(PT, 1)
    res = token_launch([("outproj", 1), ("ffn", 1, 1)], zT, outproj_extra(oT, ybT, PT, 1))
    out = np.zeros((2, SEQ, D), np.float32)
    for c in range(NCORE):
        b, q = c // 4, c % 4
        zc = np.asarray(res[c]["z_out"]).reshape(D, NT_CORE)
        out[b, q * 4096:(q + 1) * 4096] = zc[:, 64:].T
    return out
```

```python
import contextlib
import math
import numpy as np
import ml_dtypes
import concourse.bass as bass
import concourse.mybir as mybir
from concourse.bass_utils import run_bass_kernel_spmd

F32 = mybir.dt.float32
BF16 = mybir.dt.bfloat16
AF = mybir.ActivationFunctionType
ALU = mybir.AluOpType
NPBF = ml_dtypes.bfloat16

PE, ACT, DVE, POOL, SP = "tensor", "scalar", "vector", "gpsimd", "sync"
ENGS = [PE, ACT, DVE, POOL, SP]
EPOCH = 30000

D = 2048
KC = 16
FF = 5632
FC = 44
NCORE = 8
SEQ = 16384
CTX = 256
NTOK = SEQ + CTX
IN_COLS = 5552
INP = 5632
EPS = 1e-6


class Buf:
    __slots__ = ("name", "last_write", "readers", "dsem", "dcount")

    def __init__(self, name=""):
        self.name = name
        self.last_write = None
        self.readers = []
        self.dsem = None
        self.dcount = 0


class Prog:
    def __init__(self, nc, es):
        self.nc = nc
        self.es = es
        self.q = {e: [] for e in ENGS}
        self.cnt = {e: 0 for e in ENGS}
        self.epoch = {e: 0 for e in ENGS}
        self.waited = {e: {} for e in ENGS}
        self.esem = {e: self._newsem("c_%s_0" % e) for e in ENGS}
        self.ninstr = 0
        self.nsem = 5
        self.final = []
        self.dma_toks = {}

    def _newsem(self, name):
        self.nsem = getattr(self, "nsem", 0) + 1
        return self.es.enter_context(self.nc.semaphore(name))

    def _deps(self, reads, writes):
        deps = []
        for b in reads:
            if b.last_write is not None:
                deps.append(b.last_write)
        for b in writes:
            if b.last_write is not None:
                deps.append(b.last_write)
            deps.extend(b.readers)
        return deps

    def _reduce(self, eng, deps, sync_same):
        need = {}
        w = self.waited[eng]
        for sem, val, key in deps:
            if (not sync_same) and key[0] == eng:
                continue
            if w.get(key, 0) >= val:
                continue
            if need.get(key, (None, 0))[1] < val:
                need[key] = (sem, val)
        for key, (sem, val) in need.items():
            w[key] = val
        return list(need.values())

    def op(self, eng, fn, reads=(), writes=(), sync_same=True):
        deps = self._deps(reads, writes)
        waits = self._reduce(eng, deps, sync_same)
        if self.cnt[eng] >= EPOCH:
            self.epoch[eng] += 1
            self.cnt[eng] = 0
            self.esem[eng] = self._newsem("c_%s_%d" % (eng, self.epoch[eng]))
        self.cnt[eng] += 1
        tok = (self.esem[eng], self.cnt[eng], (eng, self.epoch[eng]))
        self.q[eng].append((fn, waits, (self.esem[eng], 1)))
        for b in reads:
            b.readers.append(tok)
        for b in writes:
            b.last_write = tok
            b.readers = []
        self.ninstr += 1
        return tok

    def dma(self, eng, fn, track, reads=(), writes=()):
        deps = self._deps(reads, writes)
        waits = self._reduce(eng, deps, True)
        if track.dsem is None:
            track.dsem = self._newsem("d_" + track.name)
        track.dcount += 16
        tok = (track.dsem, track.dcount, ("dma", id(track)))
        self.q[eng].append((fn, waits, (track.dsem, 16)))
        self.dma_toks[("dma", id(track))] = tok
        for b in reads:
            b.readers.append(tok)
        for b in writes:
            b.last_write = tok
            b.readers = []
        self.ninstr += 1
        return tok

    def finish(self, eng=SP):
        waits = self._reduce(eng, list(self.final), True)
        self.q[eng].append((None, waits, None))

    def barrier(self):
        toks = [(self.esem[e], self.cnt[e], (e, self.epoch[e])) for e in ENGS if self.cnt[e] > 0]
        toks += list(self.dma_toks.values())
        for e in ENGS:
            waits = self._reduce(e, toks, False)
            self.q[e].append((None, waits, None))

    def store(self, out, in_, src, eng=POOL, **kw):
        tok = self.dma(eng, lambda e: e.dma_start(out=out, in_=in_, **kw), src, [src], [])
        self.final.append(tok)
        return tok

    def build(self, block):
        q = self.q

        def runner(ename):
            def body(e):
                for fn, waits, inc in q[ename]:
                    for sem, val in waits:
                        e.wait_ge(sem, val)
                    if fn is not None:
                        fn(e).then_inc(inc[0], inc[1])
            return body
        block.tensor(runner(PE))
        block.scalar(runner(ACT))
        block.vector(runner(DVE))
        block.gpsimd(runner(POOL))
        block.sync(runner(SP))

    def mm(self, out, lhsT, rhs, start, stop, reads, writes):
        return self.op(PE, lambda e: e.matmul(out, lhsT, rhs, start=start, stop=stop), reads, writes,
                       sync_same=False)

    def act(self, out, in_, func, reads, writes, scale=1.0, bias=None):
        if bias is None:
            return self.op(ACT, lambda e: e.activation(out=out, in_=in_, func=func, scale=scale), reads, writes)
        return self.op(ACT, lambda e: e.activation(out=out, in_=in_, func=func, scale=scale, bias=bias),
                       reads, writes)

    def tt(self, eng, out, in0, in1, op, reads, writes):
        return self.op(eng, lambda e: e.tensor_tensor(out=out, in0=in0, in1=in1, op=op), reads, writes)

    def ts(self, eng, out, in0, s1, s2, op0, op1, reads, writes):
        if op1 is None:
            return self.op(eng, lambda e: e.tensor_scalar(out=out, in0=in0, scalar1=s1, scalar2=None, op0=op0),
                           reads, writes)
        return self.op(eng, lambda e: e.tensor_scalar(out=out, in0=in0, scalar1=s1, scalar2=s2, op0=op0, op1=op1),
                       reads, writes)

    def stt(self, out, in0, scalar, in1, op0, op1, reads, writes):
        return self.op(DVE, lambda e: e.scalar_tensor_tensor(out=out, in0=in0, scalar=scalar, in1=in1,
                                                             op0=op0, op1=op1), reads, writes)

    def load(self, out, in_, track, reads=(), writes=None, eng=SP, **kw):
        writes = [track] if writes is None else writes
        return self.dma(eng, lambda e: e.dma_start(out=out, in_=in_, **kw), track, reads, writes)


class Ctx:
    def __init__(self, name):
        self.nc = bass.Bass("TRN2", target_bir_lowering=False)
        self.es = contextlib.ExitStack()
        self.P = Prog(self.nc, self.es)
        self.name = name
        self.outs = []

    def din(self, name, shape, dt=F32):
        return self.nc.dram_tensor(name, list(shape), dt, kind="ExternalInput").ap()

    def dout(self, name, shape, dt=F32):
        return self.nc.dram_tensor(name, list(shape), dt, kind="ExternalOutput").ap()

    def sb(self, name, shape, dt=F32, es=None):
        t = (es or self.es).enter_context(self.nc.sbuf_tensor(name, list(shape), dt))
        return t, Buf(name)

    def ps(self, name, shape=(128, 512), dt=F32, es=None):
        t = (es or self.es).enter_context(self.nc.psum_tensor(name, list(shape), dt))
        return t, Buf(name)

    def done(self):
        self.P.finish()
        with self.nc.Block() as block:
            self.P.build(block)
        self.es.close()
        return self.nc


MODC = 9 * D // NCORE
CONV_PIECE = 4096


def build_prep(conv_len):
    C = Ctx("prep")
    P = C.P
    cst = C.din("cst", [128, KC, 3])
    mw = C.din("mw", [2, KC, 128, MODC])
    mb = C.din("mb", [2, 3, MODC])
    wf = C.din("wf", [128, conv_len])
    m_out = C.dout("m_out", [2, 3, MODC])
    wb = C.dout("wb", [128, conv_len], BF16)

    ct, ctb = C.sb("ct", [128, KC, 3])
    st, stb = C.sb("st", [128, KC, 3])
    P.load(ct[:], cst, ctb)
    P.act(st[:], ct[:], AF.Sigmoid, [ctb], [stb])
    P.tt(DVE, st[:], st[:], ct[:], ALU.mult, [stb, ctb], [stb])
    nslot = 3
    wt = [C.sb("mwt%d" % i, [128, MODC]) for i in range(nslot)]
    bt, btb = C.sb("bt", [3, 2, MODC])
    P.load(bt[:], mb.rearrange("l r c -> r l c"), btb)
    ot, otb = C.sb("ot", [3, 2, MODC])
    pss = [C.ps("mps%d" % i, [128, 512]) for i in range(5)]
    mob = Buf("m_out")
    cols = [(i * 512, min(512, MODC - i * 512)) for i in range(5)]
    it = 0
    for l in range(2):
        for kc in range(KC):
            t, tb = wt[it % nslot]
            it += 1
            P.load(t[:], mw[l, kc], tb)
            for i, (c0, cn) in enumerate(cols):
                P.mm(pss[i][0][0:3, 0:cn], st[:, kc, :], t[:, c0:c0 + cn], kc == 0, kc == KC - 1,
                     [stb, tb], [pss[i][1]])
        for i, (c0, cn) in enumerate(cols):
            P.tt(DVE, ot[:, l, c0:c0 + cn], pss[i][0][0:3, 0:cn], bt[:, l, c0:c0 + cn], ALU.add,
                 [pss[i][1], btb], [otb])
    P.store(m_out.rearrange("l r c -> r l c"), ot[:], otb, eng=SP)
    npiece = conv_len // CONV_PIECE
    cv = [C.sb("cv%d" % i, [128, CONV_PIECE], BF16) for i in range(4)]
    wbb = Buf("wb")
    for i in range(npiece):
        t, tb = cv[i % 4]
        sl = slice(i * CONV_PIECE, (i + 1) * CONV_PIECE)
        P.load(t[:].rearrange("p (a b) -> p a b", b=2048), wf[:, sl].rearrange("p (a b) -> p a b", b=2048), tb,
               eng=POOL)
        P.store(wb[:, sl], t[:], tb, eng=SP)
    return C.done()


NT_CORE = 64 + 4096
BLOCKS = [(0, 64, 0)] + [(64 + i * 512, 512, 1) for i in range(8)]


def build_token(stages):
    C = Ctx("token")
    P = C.P
    nc = C.nc
    zT = C.din("zT", [KC, 128, NT_CORE])
    modv = C.din("modv", [128, 2, 2, 9, KC])
    normg = C.din("normg", [128, 2, 3, KC])
    z_out = C.dout("z_out", [KC, 128, NT_CORE])
    wts = {}
    for s in stages:
        if s[0] == "ffn":
            _, l, i = s
            wts[s] = (C.din("wg_%d_%d" % (l, i), [22, 128, KC, 256], BF16),
                      C.din("wu_%d_%d" % (l, i), [22, 128, KC, 256], BF16),
                      C.din("wd_%d_%d" % (l, i), [KC, 128, FC, 128], BF16))
        elif s[0] == "inproj":
            wts[s] = (C.din("win_%d" % s[1], [22, 128, KC, 256], BF16),
                      C.dout("pT_%d" % s[1], [FC, 128, NT_CORE]))
        elif s[0] == "outproj":
            wts[s] = (C.din("wout_%d" % s[1], [KC, 128, KC, 128], BF16),
                      C.din("oT_%d" % s[1], [KC, 128, NT_CORE]),
                      C.din("zgT_%d" % s[1], [4, 128, NT_CORE]),
                      C.din("ssdg_%d" % s[1], [128, 4]),
                      C.din("ybT_%d" % s[1], [4, 128, NT_CORE]))

    ones, onesb = C.sb("ones", [128, 128])
    P.op(DVE, lambda e: e.memset(ones[:], 1.0), [], [onesb])
    epsT, epsb = C.sb("epsT", [128, 1])
    P.op(DVE, lambda e: e.memset(epsT[:], EPS), [], [epsb])
    mv, mvb = C.sb("mv", [128, 2, 2, 9, KC])
    P.load(mv[:], modv, mvb)
    ng, ngb = C.sb("ng", [128, 2, 3, KC])
    P.load(ng[:], normg, ngb)
    gs, gsb = C.sb("gs", [128, 2, 2, 3, KC])
    hg, hgb = C.sb("hg", [128, 2, 2, 3, KC])
    for l in range(2):
        for g in range(2):
            for j in range(3):
                P.ts(DVE, gs[:, l, g, j, :], mv[:, l, g, 3 * j + 1, :], 1.0, None, ALU.add, None, [mvb], [gsb])
                P.tt(DVE, gs[:, l, g, j, :], gs[:, l, g, j, :], ng[:, l, j, :], ALU.mult, [gsb, ngb], [gsb])
                P.ts(DVE, hg[:, l, g, j, :], mv[:, l, g, 3 * j + 2, :], (1.0 if j == 1 else 0.5), None,
                     ALU.mult, None, [mvb], [hgb])
    sg4 = None
    for s in stages:
        if s[0] == "outproj":
            sg4 = C.sb("sg4_%d" % s[1], [128, 4])
            P.load(sg4[0][:], wts[s][3], sg4[1])

    zt, ztb = C.sb("zt", [128, KC, 512])
    ht, htb = C.sb("ht", [128, KC, 512], BF16)
    actt, actb = C.sb("actt", [128, FC, 512], BF16)
    tmp = [C.sb("tmp%d" % i, [128, 512]) for i in range(3)]
    rstd, rstdb = C.sb("rstd", [128, 512])
    wgs = [C.sb("wgs%d" % i, [128, KC, 256], BF16) for i in range(2)]
    wus = [C.sb("wus%d" % i, [128, KC, 256], BF16) for i in range(2)]
    wds = [C.sb("wds%d" % i, [128, FC, 128], BF16) for i in range(2)]
    stg = [C.sb("stg%d" % i, [128, 512]) for i in range(3)]
    psG = [C.ps("psG%d" % i) for i in range(2)]
    psU = [C.ps("psU%d" % i) for i in range(2)]
    psD = [C.ps("psD%d" % i) for i in range(2)]
    psS = C.ps("psS")
    zob = Buf("z_out")
    cnt = {"w": 0, "d": 0, "t": 0, "s": 0, "g": 0}

    def adaln(l, j, lat, T):
        for kc in range(KC):
            t, tb = tmp[cnt["t"] % 3]
            cnt["t"] += 1
            P.act(t[:, :T], zt[:, kc, :T], AF.Square, [ztb], [tb])
            P.mm(psS[0][:, :T], ones[:], t[:, :T], kc == 0, kc == KC - 1, [onesb, tb], [psS[1]])
        P.act(rstd[:, :T], psS[0][:, :T], AF.Sqrt, [psS[1], epsb], [rstdb], scale=1.0 / D, bias=epsT[:])
        P.op(DVE, lambda e: e.reciprocal(out=rstd[:, :T], in_=rstd[:, :T]), [rstdb], [rstdb])
        for kc in range(KC):
            t, tb = tmp[cnt["t"] % 3]
            cnt["t"] += 1
            P.tt(DVE, t[:, :T], zt[:, kc, :T], rstd[:, :T], ALU.mult, [ztb, rstdb], [tb])
            P.ts(POOL, ht[:, kc, :T], t[:, :T], gs[:, l, lat, j, kc:kc + 1], mv[:, l, lat, 3 * j, kc:kc + 1],
                 ALU.mult, ALU.add, [tb, gsb, mvb], [htb])

    def proj_T(w_ap, nchunk_out, nk, rhs_t, rhs_b, T, consume, slots):
        for oc in range(nchunk_out):
            wt_, wb_ = slots[cnt["d"] % 2]
            cnt["d"] += 1
            P.load(wt_[:, :nk, :], w_ap[oc], wb_)
            ps_, pb_ = psD[oc % 2]
            for k in range(nk):
                P.mm(ps_[:, :T], wt_[:, k, :], rhs_t[:, k, :T], k == 0, k == nk - 1, [wb_, rhs_b], [pb_])
            consume(oc, ps_, pb_)

    def ffn(l, i, lat, T):
        j = 0 if i == 0 else 2
        wg, wu, wd = wts[("ffn", l, i)]
        adaln(l, j, lat, T)
        for g in range(22):
            a, ab = wgs[g % 2]
            u, ub = wus[g % 2]
            P.load(a[:], wg[g], ab)
            P.load(u[:], wu[g], ub)
            for h in range(2):
                fc = g * 2 + h
                pg, pgb = psG[fc % 2]
                pu, pub = psU[fc % 2]
                for k in range(KC):
                    P.mm(pg[:, :T], a[:, k, h * 128:(h + 1) * 128], ht[:, k, :T], k == 0, k == KC - 1,
                         [ab, htb], [pgb])
                for k in range(KC):
                    P.mm(pu[:, :T], u[:, k, h * 128:(h + 1) * 128], ht[:, k, :T], k == 0, k == KC - 1,
                         [ub, htb], [pub])
                t, tb = tmp[cnt["t"] % 3]
                cnt["t"] += 1
                P.act(t[:, :T], pg[:, :T], AF.Silu, [pgb], [tb])
                P.tt(DVE, actt[:, fc, :T], t[:, :T], pu[:, :T], ALU.mult, [tb, pub], [actb])

        def consume(oc, ps_, pb_):
            P.stt(zt[:, oc, :T], ps_[:, :T], hg[:, l, lat, j, oc:oc + 1], zt[:, oc, :T], ALU.mult, ALU.add,
                  [pb_, hgb, ztb], [ztb])
        proj_T(wd, KC, FC, actt, actb, T, consume, wds)

    def inproj(l, lat, T, t0):
        win, pT = wts[("inproj", l)]
        adaln(l, 1, lat, T)
        pTb = pbufs[l]
        for g in range(22):
            a, ab = wgs[g % 2]
            P.load(a[:], win[g], ab)
            for h in range(2):
                oc = g * 2 + h
                ps_, pb_ = psD[oc % 2]
                for k in range(KC):
                    P.mm(ps_[:, :T], a[:, k, h * 128:(h + 1) * 128], ht[:, k, :T], k == 0, k == KC - 1,
                         [ab, htb], [pb_])
                s_, sb_ = stg[cnt["s"] % 3]
                cnt["s"] += 1
                if oc % 2 == 0:
                    P.act(s_[:, :T], ps_[:, :T], AF.Copy, [pb_], [sb_])
                else:
                    P.op(DVE, lambda e, s_=s_, ps_=ps_: e.tensor_copy(out=s_[:, :T], in_=ps_[:, :T]), [pb_], [sb_])
                P.store(pT[oc, :, t0:t0 + T], s_[:, :T], sb_)

    def outproj(l, lat, T, t0):
        wout, oT, zgT, _, ybT = wts[("outproj", l)]
        ssd_t = []
        for c4 in range(4):
            y_, yb_ = stg[cnt["s"] % 3]
            cnt["s"] += 1
            g_, gb_ = tmp[cnt["t"] % 3]
            cnt["t"] += 1
            P.load(y_[:, :T], oT[12 + c4, :, t0:t0 + T], yb_)
            P.load(g_[:, :T], zgT[c4, :, t0:t0 + T], gb_)
            y2_, y2b_ = stg[cnt["s"] % 3]
            cnt["s"] += 1
            P.load(y2_[:, :T], ybT[c4, :, t0:t0 + T], y2b_)
            P.tt(DVE, y_[:, :T], y_[:, :T], y2_[:, :T], ALU.add, [yb_, y2b_], [yb_])
            P.act(g_[:, :T], g_[:, :T], AF.Silu, [gb_], [gb_])
            P.tt(DVE, zsc[0][:, c4, :T], g_[:, :T], y_[:, :T], ALU.mult, [gb_, yb_], [zsc[1]])
            P.act(y_[:, :T], zsc[0][:, c4, :T], AF.Square, [zsc[1]], [yb_])
            P.mm(psS[0][:, :T], ones[:], y_[:, :T], c4 == 0, c4 == 3, [onesb, yb_], [psS[1]])
        P.act(rstd[:, :T], psS[0][:, :T], AF.Sqrt, [psS[1], epsb], [rstdb], scale=1.0 / 512, bias=epsT[:])
        P.op(DVE, lambda e: e.reciprocal(out=rstd[:, :T], in_=rstd[:, :T]), [rstdb], [rstdb])
        for c4 in range(4):
            P.stt(ht[:, 12 + c4, :T], zsc[0][:, c4, :T], sg4[0][:, c4:c4 + 1], rstd[:, :T], ALU.mult, ALU.mult,
                  [zsc[1], sg4[1], rstdb], [htb])
        for kc in range(12):
            y_, yb_ = stg[cnt["s"] % 3]
            cnt["s"] += 1
            P.load(y_[:, :T], oT[kc, :, t0:t0 + T], yb_)
            if kc % 2 == 0:
                P.act(ht[:, kc, :T], y_[:, :T], AF.Copy, [yb_], [htb])
            else:
                P.op(DVE, lambda e, y_=y_, kc=kc: e.tensor_copy(out=ht[:, kc, :T], in_=y_[:, :T]), [yb_], [htb])

        def consume(oc, ps_, pb_):
            P.stt(zt[:, oc, :T], ps_[:, :T], hg[:, l, lat, 1, oc:oc + 1], zt[:, oc, :T], ALU.mult, ALU.add,
                  [pb_, hgb, ztb], [ztb])
        proj_T(wout, KC, KC, ht, htb, T, consume, wds)

    zsc = None
    if any(s[0] == "outproj" for s in stages):
        zsc = C.sb("zsc", [128, 4, 512])
    pbufs = {s[1]: Buf("pT%d" % s[1]) for s in stages if s[0] == "inproj"}

    for (t0, T, lat) in BLOCKS:
        P.load(zt[:, :, :T], zT[:, :, t0:t0 + T].rearrange("k p t -> p k t"), ztb)
        for s in stages:
            if s[0] == "ffn":
                ffn(s[1], s[2], lat, T)
            elif s[0] == "inproj":
                inproj(s[1], lat, T, t0)
            elif s[0] == "outproj":
                outproj(s[1], lat, T, t0)
        P.store(z_out[:, :, t0:t0 + T].rearrange("k p t -> p k t"), zt[:, :, :T], ztb)
    print("token program", stages, "ninstr", P.ninstr, "nsem", P.nsem)
    return C.done()


def lay_w22(w):
    k, n = w.shape
    if n < 22 * 256:
        w = np.concatenate([w, np.zeros((k, 22 * 256 - n), w.dtype)], axis=1)
    return np.ascontiguousarray(w.reshape(KC, 128, 22, 256).transpose(2, 1, 0, 3))


def lay_wd(w):
    nf = w.shape[0] // 128
    return np.ascontiguousarray(w.reshape(nf, 128, KC, 128).transpose(2, 1, 0, 3))


def lay_vec(v):
    lead = v.shape[:-1]
    a = v.reshape(lead + (KC, 128))
    a = np.moveaxis(a, -1, 0)
    return np.ascontiguousarray(a)


def fm(zt_tokens_major):
    t, f = zt_tokens_major.shape
    return np.ascontiguousarray(zt_tokens_major.T.reshape(f // 128, 128, t))


def core_tokens(c):
    q = c % 4
    return np.concatenate([np.arange(q * 64, (q + 1) * 64), CTX + np.arange(q * 4096, (q + 1) * 4096)])


NKT = NTOK // 128
TBLK = [(0, 256)] + [(256 + 512 * i, 512) for i in range(32)]


def build_attn(parts=('dk', 'dq', 'mk', 'mq'), nblk=33, dbg=9):
    C = Ctx("attn")
    P = C.P
    nc = C.nc
    dq = C.din("dq", [128, NTOK]); dk = C.din("dk", [128, NTOK]); dv = C.din("dv", [NTOK, 128])
    cosd = C.din("cosd", [128, NTOK]); sind = C.din("sind", [128, NTOK])
    cosm = C.din("cosm", [96, NTOK]); sinm = C.din("sinm", [96, NTOK])
    cq = C.din("cq", [3, 128, NTOK]); ckv = C.din("ckv", [128, NTOK]); kr = C.din("kr", [32, NTOK])
    wuq = C.din("wuq", [128, 3, 192]); wukv = C.din("wukv", [128, 256])
    vecs = C.din("vecs", [128, 16])
    lamv = C.din("lamv", [64, 4])
    cmat = C.din("cmat", [4, 128, 128])
    o_diff = C.dout("o_diff", [128, NTOK]); o_mla = C.dout("o_mla", [128, NTOK])

    ones, onesb = C.sb("ones", [128, 128])
    P.op(DVE, lambda e: e.memset(ones[:], 1.0), [], [onesb])
    onesh, oneshb = C.sb("onesh", [128, 128], BF16)
    P.op(DVE, lambda e: e.memset(onesh[:], 1.0), [], [oneshb])
    epsT, epsb = C.sb("epsT", [128, 1])
    P.op(DVE, lambda e: e.memset(epsT[:], EPS), [], [epsb])
    vc, vcb = C.sb("vc", [128, 16])
    P.load(vc[:], vecs, vcb)
    cm, cmb = C.sb("cm", [128, 4, 128])
    P.load(cm[:], cmat.rearrange("m p c -> p m c"), cmb)
    lv, lvb = C.sb("lv", [64, 4])
    P.load(lv[:], lamv, lvb)
    wq_f, wq_fb = C.sb("wq_f", [128, 3, 192]); wq, wqb = C.sb("wq", [128, 3, 192], BF16)
    wkv_f, wkv_fb = C.sb("wkv_f", [128, 256]); wkv, wkvb = C.sb("wkv", [128, 256], BF16)
    P.load(wq_f[:], wuq, wq_fb); P.load(wkv_f[:], wukv, wkv_fb)
    P.op(DVE, lambda e: e.tensor_copy(out=wq[:], in_=wq_f[:]), [wq_fb], [wqb])
    P.op(DVE, lambda e: e.tensor_copy(out=wkv[:], in_=wkv_f[:]), [wkv_fb], [wkvb])

    psS = [C.ps("psS%d" % i) for i in range(2)]
    psO = [C.ps("psO%d" % i) for i in range(2)]
    psL = [C.ps("psL%d" % i) for i in range(2)]
    psE = C.ps("psE")
    psX = C.ps("psX")
    NTMP = 8
    tmps = [C.sb("atmp%d" % i, [128, 512]) for i in range(NTMP)]
    cnt = {"t": 0, "p": 0}

    def tmp():
        t = tmps[cnt["t"] % NTMP]
        cnt["t"] += 1
        return t

    l2, l2b = C.sb("l2", [64, 2])
    P.tt(DVE, l2[:, 0:1], lv[:, 0:1], lv[:, 1:2], ALU.mult, [lvb], [l2b])
    P.tt(DVE, l2[:, 1:2], lv[:, 2:3], lv[:, 3:4], ALU.mult, [lvb, l2b], [l2b])
    P.mm(psE[0][:, 0:2], ones[0:64, :], l2[:], True, True, [onesb, l2b], [psE[1]])
    le, leb = C.sb("le", [128, 2])
    P.act(le[:], psE[0][:, 0:2], AF.Exp, [psE[1]], [leb])
    nlam, nlamb = C.sb("nlam", [128, 1])
    P.tt(DVE, nlam[:], le[:, 1:2], le[:, 0:1], ALU.subtract, [leb], [nlamb])
    P.tt(DVE, nlam[:], nlam[:], vc[:, 3:4], ALU.subtract, [nlamb, vcb], [nlamb])
    sgd, sgdb = C.sb("sgd", [128, 1])
    P.tt(DVE, sgd[:], vc[:, 2:3], vc[:, 4:5], ALU.mult, [vcb], [sgdb])

    def norm_rope(src, srcb, R, T, gi, ri, gain_col, cos_ap, sin_ap, out_ap, outb):
        sq, sqb = tmp()
        P.act(sq[:R, :T], src, AF.Square, [srcb], [sqb])
        P.mm(psE[0][:R, :T], cm[:R, gi, :R], sq[:R, :T], True, True, [cmb, sqb], [psE[1]])
        rt, rtb = tmp()
        P.act(rt[:R, :T], psE[0][:R, :T], AF.Sqrt, [psE[1], epsb], [rtb], bias=epsT[:R, :])
        P.op(DVE, lambda e: e.reciprocal(out=rt[:R, :T], in_=rt[:R, :T]), [rtb], [rtb])
        xn, xnb = tmp()
        P.stt(xn[:R, :T], src, vc[:R, gain_col:gain_col + 1], rt[:R, :T], ALU.mult, ALU.mult,
              [srcb, vcb, rtb], [xnb])
        P.mm(psX[0][:R, :T], cm[:R, ri, :R], xn[:R, :T], True, True, [cmb, xnb], [psX[1]])
        ct, ctb = tmp(); st, stb = tmp()
        P.load(ct[:R, :T], cos_ap, ctb)
        P.load(st[:R, :T], sin_ap, stb)
        P.tt(DVE, ct[:R, :T], xn[:R, :T], ct[:R, :T], ALU.mult, [xnb, ctb], [ctb])
        P.tt(DVE, st[:R, :T], psX[0][:R, :T], st[:R, :T], ALU.mult, [psX[1], stb], [stb])
        P.tt(POOL, out_ap, ct[:R, :T], st[:R, :T], ALU.add, [ctb, stb], [outb])

    def attention(streams, q0, T, nkt, epilogue):
        pts = {}
        last = nkt - 1

        def pv(k):
            for si, s in enumerate(streams):
                pt, ptb = pts[(si, k)]
                for (po, pob, lf, lb) in s["pv"]:
                    P.mm(po[:, :T], lf(k), pt[:, :T], k == 0, k == last, [lb, ptb], [pob])
        sb2 = [[psS[0], psE], [psS[1], psX]]
        for k in range(nkt):
            for si, s in enumerate(streams):
                r0, r1 = s["rows"]
                sbk, sbkb = sb2[si][k % 2]
                P.mm(sbk[:, :T], s["KT"][r0:r1, k * 128:(k + 1) * 128], s["QT"][r0:r1, :T], True, True,
                     [s["KTb"], s["QTb"]], [sbkb])
            for si, s in enumerate(streams):
                pt, ptb = ptiles[si][k % 3]
                pts[(si, k)] = (pt, ptb)
                sbk, sbkb = sb2[si][k % 2]
                P.act(pt[:, :T], sbk[:, :T], AF.Exp, [sbkb], [ptb], scale=s["scale"])
            if k > 0:
                pv(k - 1)
        pv(last)
        epilogue(q0, T)

    ptiles = [[C.sb("pt%d_%d" % (si, j), [128, 512], BF16) for j in range(3)] for si in range(2)]

    with contextlib.ExitStack() as es:
        KT, KTb = C.sb("dKT", [128, NTOK], BF16, es)
        V, Vb = C.sb("dV", [128, NKT, 128], BF16, es)
        QTs = [C.sb("dQT%d" % i, [128, 512], BF16, es) for i in range(2)]
        srcs = [C.sb("dsrc%d" % i, [128, 512], F32, es) for i in range(2)]
        dvr = dv.rearrange("(t p) c -> p t c", p=128)
        for i in range(0, NKT, 13):
            P.load(V[:, i:i + 13, :], dvr[:, i:i + 13, :], Vb, reads=[], writes=[Vb], eng=POOL)
        for bi, (t0, T) in enumerate(TBLK[:nblk] if 'dk' in parts else []):
            s, sb_ = srcs[bi % 2]
            P.load(s[:, :T], dk[:, t0:t0 + T], sb_)
            norm_rope(s[:, :T], sb_, 128, T, 0, 1, 1, cosd[:, t0:t0 + T], sind[:, t0:t0 + T], KT[:, t0:t0 + T], KTb)

        def epi_diff(q0, T):
            r0, r0b = tmp(); r1, r1b = tmp()
            P.op(DVE, lambda e: e.reciprocal(out=r0[:, :T], in_=psL[0][0][:, :T]), [psL[0][1]], [r0b])
            P.op(DVE, lambda e: e.reciprocal(out=r1[:, :T], in_=psL[1][0][:, :T]), [psL[1][1]], [r1b])
            P.tt(DVE, r0[:, :T], psO[0][0][:, :T], r0[:, :T], ALU.mult, [psO[0][1], r0b], [r0b])
            P.tt(DVE, r1[:, :T], psO[1][0][:, :T], r1[:, :T], ALU.mult, [psO[1][1], r1b], [r1b])
            o, ob = tmp()
            P.stt(o[:, :T], r1[:, :T], nlam[:, 0:1], r0[:, :T], ALU.mult, ALU.add, [r1b, nlamb, r0b], [ob])
            sq, sqb = tmp()
            P.act(sq[:, :T], o[:, :T], AF.Square, [ob], [sqb])
            P.mm(psE[0][:, :T], ones[:], sq[:, :T], True, True, [onesb, sqb], [psE[1]])
            rt, rtb = tmp()
            P.act(rt[:, :T], psE[0][:, :T], AF.Sqrt, [psE[1], epsb], [rtb], scale=1.0 / 128, bias=epsT[:])
            P.op(DVE, lambda e: e.reciprocal(out=rt[:, :T], in_=rt[:, :T]), [rtb], [rtb])
            P.stt(o[:, :T], o[:, :T], sgd[:, 0:1], rt[:, :T], ALU.mult, ALU.mult, [ob, sgdb, rtb], [ob])
            P.store(o_diff[:, q0:q0 + T], o[:, :T], ob)

        for bi, (q0, T) in enumerate(TBLK[:nblk] if 'dq' in parts else []):
            s, sb_ = srcs[bi % 2]
            qt, qtb = QTs[bi % 2]
            P.load(s[:, :T], dq[:, q0:q0 + T], sb_)
            norm_rope(s[:, :T], sb_, 128, T, 0, 1, 0, cosd[:, q0:q0 + T], sind[:, q0:q0 + T], qt[:, :T], qtb)
            streams = []
            for m in range(2):
                streams.append(dict(KT=KT, KTb=KTb, rows=(m * 64, (m + 1) * 64), QT=qt, QTb=qtb, scale=0.125,
                                    pv=[(psO[m][0], psO[m][1], (lambda k: V[:, k, :]), Vb),
                                        (psL[m][0], psL[m][1], (lambda k: onesh[:]), oneshb)]))
            attention(streams, q0, T, 2 if bi == 0 else NKT, epi_diff)

    P.barrier()
    with contextlib.ExitStack() as es:
        KTs = [C.sb("mKT%d" % i, [96, NTOK], BF16, es) for i in range(2)]
        Vall, Vallb = C.sb("mV", [128, 2, NKT, 128], BF16, es)
        Vs = [(Vall[:, i], Vallb) for i in range(2)]
        QTs = [[C.sb("mQT%d_%d" % (h, i), [96, 512], BF16, es) for i in range(2)] for h in range(2)]
        cqt = [C.sb("cqt%d" % i, [128, 3, 512], F32, es) for i in range(2)]
        cqn, cqnb = C.sb("cqn", [128, 3, 512], BF16, es)
        ckn, cknb = C.sb("ckn", [128, 512], BF16, es)
        for h in range(2):
            P.op(POOL, lambda e, h=h: e.memset(Vall[:, h, :, 64:128], 1.0), [], [Vallb])
        for bi, (t0, T) in enumerate(TBLK[:nblk] if 'mk' in parts else []):
            c_, cb_ = tmp()
            P.load(c_[:, :T], ckv[:, t0:t0 + T], cb_)
            sq, sqb = tmp()
            P.act(sq[:, :T], c_[:, :T], AF.Square, [cb_], [sqb])
            P.mm(psE[0][:, :T], ones[:], sq[:, :T], True, True, [onesb, sqb], [psE[1]])
            rt, rtb = tmp()
            P.act(rt[:, :T], psE[0][:, :T], AF.Sqrt, [psE[1], epsb], [rtb], scale=1.0 / 128, bias=epsT[:])
            P.op(DVE, lambda e, rt=rt, T=T: e.reciprocal(out=rt[:, :T], in_=rt[:, :T]), [rtb], [rtb])
            P.stt(ckn[:, :T], c_[:, :T], vc[:, 8:9], rt[:, :T], ALU.mult, ALU.mult, [cb_, vcb, rtb], [cknb])
            for h in range(2 if dbg >= 2 else 0):
                s, sb_ = tmp()
                P.mm(psX[0][0:64, :T], wkv[:, h * 128:h * 128 + 64], ckn[:, :T], True, True, [wkvb, cknb], [psX[1]])
                P.load(s[64:96, :T], kr[:, t0:t0 + T], sb_)
                P.op(DVE, lambda e, s=s, T=T: e.tensor_copy(out=s[0:64, :T], in_=psX[0][0:64, :T]), [psX[1]], [sb_])
                if dbg >= 3:
                    norm_rope(s[0:96, :T], sb_, 96, T, 2, 3, 10, cosm[:, t0:t0 + T], sinm[:, t0:t0 + T],
                              KTs[h][0][:, t0:t0 + T], KTs[h][1])
            for sub in range(T // 128 if dbg >= 4 else 0):
                tix = t0 // 128 + sub
                P.mm(psX[0][:, 0:256], ckn[:, sub * 128:(sub + 1) * 128], wkv[:, :], True, True, [cknb, wkvb], [psX[1]])
                if dbg >= 5:
                    P.act(Vall[:, :, tix, 0:64], psX[0][:, 0:256].rearrange("p (h c) -> p h c", h=2)[:, :, 64:128],
                          AF.Copy, [psX[1]], [Vallb])

        osb = [C.sb("mo%d" % i, [128, 512], F32, es) for i in range(2)]
        ocnt = [0]

        def epi_mla(q0, T):
            o, ob = osb[ocnt[0] % 2]
            ocnt[0] += 1
            for h in range(2):
                r, rb = tmp()
                P.op(DVE, lambda e, r=r, h=h: e.reciprocal(out=r[0:64, :T], in_=psO[h][0][64:128, :T]), [psO[h][1]], [rb])
                P.tt(DVE, o[h * 64:(h + 1) * 64, :T], psO[h][0][0:64, :T], r[0:64, :T], ALU.mult, [psO[h][1], rb], [ob])
            P.store(o_mla[:, q0:q0 + T], o[:, :T], ob)

        for bi, (q0, T) in enumerate(TBLK[:nblk] if 'mq' in parts else []):
            c_, cb_ = cqt[bi % 2]
            P.load(c_[:, :, :T], cq[:, :, q0:q0 + T].rearrange("k p t -> p k t"), cb_)
            for kc in range(3):
                sq, sqb = tmp()
                P.act(sq[:, :T], c_[:, kc, :T], AF.Square, [cb_], [sqb])
                P.mm(psE[0][:, :T], ones[:], sq[:, :T], kc == 0, kc == 2, [onesb, sqb], [psE[1]])
            rt, rtb = tmp()
            P.act(rt[:, :T], psE[0][:, :T], AF.Sqrt, [psE[1], epsb], [rtb], scale=1.0 / 384, bias=epsT[:])
            P.op(DVE, lambda e, rt=rt, T=T: e.reciprocal(out=rt[:, :T], in_=rt[:, :T]), [rtb], [rtb])
            for kc in range(3):
                P.stt(cqn[:, kc, :T], c_[:, kc, :T], vc[:, 5 + kc:6 + kc], rt[:, :T], ALU.mult, ALU.mult,
                      [cb_, vcb, rtb], [cqnb])
            streams = []
            for h in range(2):
                for kc in range(3):
                    P.mm(psX[0][0:96, :T], wq[:, kc, h * 96:(h + 1) * 96], cqn[:, kc, :T], kc == 0, kc == 2,
                         [wqb, cqnb], [psX[1]])
                s, sb_ = tmp()
                P.op(DVE, lambda e, s=s, T=T: e.tensor_copy(out=s[0:96, :T], in_=psX[0][0:96, :T]), [psX[1]], [sb_])
                qt, qtb = QTs[h][bi % 2]
                norm_rope(s[0:96, :T], sb_, 96, T, 2, 3, 9, cosm[:, q0:q0 + T], sinm[:, q0:q0 + T], qt[:, :T], qtb)
                streams.append(dict(KT=KTs[h][0], KTb=KTs[h][1], rows=(0, 96), QT=qt, QTb=qtb,
                                    scale=96.0 ** -0.5,
                                    pv=[(psO[h][0], psO[h][1], (lambda k, h=h: Vall[:, h, k, :]), Vallb)]))
            attention(streams, q0, T, 2 if bi == 0 else NKT, epi_mla)
    print("attn program ninstr", P.ninstr, "nsem", P.nsem)
    return C.done()


OFF = dict(dq=0, dk=512, dv=1024, rw=1536, mcq=3456, mckv=3840, mkr=3968, sz=4000, sxbc=4512, sdt=5536)


def rope_tables(dim):
    rows = SEQ // 64
    quarter = dim // 4
    inv = (np.float32(10000.0) ** (-np.arange(quarter, dtype=np.float32) / np.float32(quarter))).astype(np.float32)
    pos_r = np.repeat(np.arange(rows), 64).astype(np.float32)
    pos_c = np.tile(np.arange(64), rows).astype(np.float32)
    ang = np.concatenate([pos_r[:, None] * inv, pos_c[:, None] * inv], axis=-1).astype(np.float32)
    return np.cos(ang).astype(np.float32), np.sin(ang).astype(np.float32)


_CONST = {}


def attn_consts():
    if "a" in _CONST:
        return _CONST["a"]
    cd, sd = rope_tables(64)
    cosd = np.ones((128, NTOK), np.float32); sind = np.zeros((128, NTOK), np.float32)
    for r in range(128):
        cosd[r, CTX:] = cd[:, (r % 64) % 32]; sind[r, CTX:] = sd[:, (r % 64) % 32]
    c32, s32 = rope_tables(32)
    cosm = np.ones((96, NTOK), np.float32); sinm = np.zeros((96, NTOK), np.float32)
    for i in range(32):
        cosm[64 + i, CTX:] = c32[:, i % 16]; sinm[64 + i, CTX:] = s32[:, i % 16]
    cmat = np.zeros((4, 128, 128), np.float32)
    for blk in range(2):
        cmat[0, blk * 64:(blk + 1) * 64, blk * 64:(blk + 1) * 64] = 1.0 / 64
        for m in range(64):
            if m < 32:
                cmat[1, blk * 64 + m + 32, blk * 64 + m] = -1.0
            else:
                cmat[1, blk * 64 + m - 32, blk * 64 + m] = 1.0
    cmat[2, 0:64, 0:64] = 1.0 / 64
    cmat[2, 64:96, 64:96] = 1.0 / 32
    for i in range(32):
        if i < 16:
            cmat[3, 64 + i + 16, 64 + i] = -1.0
        else:
            cmat[3, 64 + i - 16, 64 + i] = 1.0
    _CONST["a"] = (cosd, sind, cosm, sinm, cmat)
    return _CONST["a"]


def attn_inputs(PT, inp, l, j):
    cosd, sind, cosm, sinm, cmat = attn_consts()
    r = lambda o, a, b: np.ascontiguousarray(PT[OFF[o] + a:OFF[o] + b])
    lam_init = 0.8 - 0.6 * math.exp(-0.3 * l)
    vecs = np.zeros((128, 16), np.float32)
    vecs[:, 0] = np.tile(inp["diff_qk_g"][l, 0], 2)
    vecs[:, 1] = np.tile(inp["diff_qk_g"][l, 1], 2)
    vecs[:, 2] = inp["diff_subln_g"][l]
    vecs[:, 3] = lam_init
    vecs[:, 4] = 1.0 - lam_init
    vecs[:, 5:8] = inp["mla_q_norm_g"][l].reshape(3, 128).T
    vecs[:, 8] = inp["mla_kv_norm_g"][l]
    vecs[:96, 9] = np.concatenate([inp["mla_nope_g"][l, 0], inp["mla_rope_g"][l, 0]])
    vecs[:96, 10] = np.concatenate([inp["mla_nope_g"][l, 1], inp["mla_rope_g"][l, 1]])
    wuq = inp["mla_w_uq"][l][:, 2 * j * 96:(2 * j + 2) * 96]
    return {
        "dq": r("dq", j * 128, (j + 1) * 128), "dk": r("dk", j * 128, (j + 1) * 128),
        "dv": np.ascontiguousarray(PT[OFF["dv"] + j * 128:OFF["dv"] + (j + 1) * 128].T),
        "cosd": cosd, "sind": sind, "cosm": cosm, "sinm": sinm,
        "cq": np.ascontiguousarray(PT[OFF["mcq"]:OFF["mcq"] + 384].reshape(3, 128, NTOK)),
        "ckv": r("mckv", 0, 128), "kr": r("mkr", 0, 32),
        "wuq": np.ascontiguousarray(wuq.reshape(3, 128, 192).transpose(1, 0, 2)),
        "wukv": np.ascontiguousarray(inp["mla_w_ukv"][l][:, 2 * j * 128:(2 * j + 2) * 128]),
        "vecs": vecs, "lamv": np.ascontiguousarray(inp["diff_lambda"][l].T), "cmat": cmat,
    }


FWD_ORDER = list(range(NKT))
BWD_ORDER = [1, 0] + list(range(NKT - 1, 1, -1))


def build_ssd(nchunks=NKT):
    C = Ctx("ssd")
    P = C.P
    xs = C.din("xs", [128, NTOK]); Bm = C.din("Bm", [128, NTOK]); Cm = C.din("Cm", [128, NTOK])
    cw = C.din("cw", [128, 3, 5]); cb = C.din("cb", [128, 3])
    dtr = C.din("dtr", [NTOK, 4])
    sv = C.din("sv", [128, 10])
    cmat = C.din("cmat", [8, 128, 128])
    y_f = C.dout("y_f", [NTOK, 128]); y_b = C.dout("y_b", [NTOK, 128])

    cm, cmb = C.sb("cm", [128, 8, 128])
    P.load(cm[:], cmat.rearrange("m p c -> p m c"), cmb)
    identb, identbb = C.sb("identb", [128, 128], BF16)
    P.op(DVE, lambda e: e.tensor_copy(out=identb[:], in_=cm[:, 0, :]), [cmb], [identbb])
    ones, onesb = C.sb("ones", [128, 128])
    P.op(DVE, lambda e: e.memset(ones[:], 1.0), [], [onesb])
    cwt, cwb = C.sb("cwt", [128, 3, 5]); cbt, cbb = C.sb("cbt", [128, 3]); svt, svb = C.sb("svt", [128, 10])
    P.load(cwt[:], cw, cwb); P.load(cbt[:], cb, cbb); P.load(svt[:], sv, svb)
    nexpA, nexpAb = C.sb("nexpA", [128, 4])
    P.act(nexpA[:], svt[:, 4:8], AF.Exp, [svb], [nexpAb])
    P.ts(DVE, nexpA[:], nexpA[:], -1.0, None, ALU.mult, None, [nexpAb], [nexpAb])

    x_tm, x_tmb = C.sb("x_tm", [128, NKT, 128])
    Bf, Bfb = C.sb("Bf", [128, NTOK], BF16)
    Cf, Cfb = C.sb("Cf", [128, NTOK], BF16)
    B_tm, B_tmb = C.sb("B_tm", [128, NKT, 128], BF16)
    dt, dtb = C.sb("dt", [128, NKT, 4])
    adt, adtb = C.sb("adt", [128, NKT, 4])
    P.load(dt[:], dtr.rearrange("(t p) c -> p t c", p=128), dtb)
    for col in range(4):
        P.act(dt[:, :, col], dt[:, :, col], AF.Exp, [dtb, svb], [dtb], bias=svt[:, col:col + 1])
    for col in range(4):
        P.act(dt[:, :, col], dt[:, :, col], AF.Ln, [dtb], [dtb], bias=ones[:, 0:1])
    for col in range(4):
        P.ts(DVE, adt[:, :, col], dt[:, :, col], nexpA[:, col:col + 1], None, ALU.mult, None, [dtb, nexpAb], [adtb])

    psT = C.ps("psT")
    psTb_t = C.es.enter_context(C.nc.psum_tensor("psTb", [128, 1024], BF16)); psTb = (psTb_t, Buf("psTb"))
    psSeg = [C.ps("psSeg%d" % i) for i in range(2)]
    psG = C.ps("psG"); psY = C.ps("psY"); psH = C.ps("psH"); psA = C.ps("psA")

    xin = [C.sb("xin%d" % i, [128, 3, 516]) for i in range(2)]
    acc = [C.sb("cacc%d" % i, [128, 512]) for i in range(3)]
    xc, xcb = C.sb("xc", [128, 512])
    srcs = [xs, Bm, Cm]
    for bi, (t0, T) in enumerate(TBLK):
        seg0, seg1 = (0, CTX) if t0 < CTX else (CTX, NTOK)
        lo, hi = max(seg0, t0 - 2), min(seg1, t0 + T + 2)
        xi, xib = xin[bi % 2]
        P.op(POOL, lambda e, xi=xi: e.memset(xi[:], 0.0), [], [xib])
        for i3 in range(3):
            P.load(xi[:, i3, lo - (t0 - 2):hi - (t0 - 2)], srcs[i3][:, lo:hi], xib)
        for i3 in range(3):
            a, ab = acc[i3]
            P.ts(DVE, a[:, :T], xi[:, i3, 0:T], cwt[:, i3, 0:1], None, ALU.mult, None, [xib, cwb], [ab])
            for i in range(1, 5):
                P.stt(a[:, :T], xi[:, i3, i:i + T], cwt[:, i3, i:i + 1], a[:, :T], ALU.mult, ALU.add,
                      [xib, cwb, ab], [ab])
            if i3 == 0:
                P.act(xc[:, :T], a[:, :T], AF.Silu, [ab, cbb], [xcb], bias=cbt[:, 0:1])
            elif i3 == 1:
                P.act(Bf[:, t0:t0 + T], a[:, :T], AF.Silu, [ab, cbb], [Bfb], bias=cbt[:, 1:2])
            else:
                P.act(Cf[:, t0:t0 + T], a[:, :T], AF.Silu, [ab, cbb], [Cfb], bias=cbt[:, 2:3])
        for sub in range(T // 128):
            ci = t0 // 128 + sub
            P.op(PE, lambda e, sub=sub: e.transpose(psT[0][:, 0:128], xc[:, sub * 128:(sub + 1) * 128], cm[:, 0, :]),
                 [xcb, cmb], [psT[1]], sync_same=False)
            P.op(DVE, lambda e, ci=ci: e.tensor_copy(out=x_tm[:, ci, :], in_=psT[0][:, 0:128]), [psT[1]], [x_tmb])
            P.op(PE, lambda e, ci=ci: e.transpose(psTb[0][:, 0:128], Bf[:, ci * 128:(ci + 1) * 128], identb[:]),
                 [Bfb, identbb], [psTb[1]], sync_same=False)
            P.act(B_tm[:, ci, :], psTb[0][:, 0:128], AF.Copy, [psTb[1]], [B_tmb])

    hT = [C.sb("hT%d" % h, [128, 64]) for h in range(2)]
    hTh = [C.sb("hTh%d" % h, [128, 64], BF16) for h in range(2)]
    NS = 3
    abc = [[C.sb("abc%d_%d" % (h, i), [128, 128]) for i in range(NS)] for h in range(2)]
    dec = [[C.sb("dec%d_%d" % (h, i), [128, 128]) for i in range(NS)] for h in range(2)]
    sct = [[C.sb("sct%d_%d" % (h, i), [128, 128], BF16) for i in range(NS)] for h in range(2)]
    bw = [[C.sb("bw%d_%d" % (h, i), [128, 128], BF16) for i in range(NS)] for h in range(2)]
    xdt = [[C.sb("xdt%d_%d" % (h, i), [128, 64], BF16) for i in range(NS)] for h in range(2)]
    sm = [C.sb("sm%d" % i, [128, 8]) for i in range(NS)]
    ysb = [C.sb("ysb%d" % i, [128, 128]) for i in range(NS)]
    yo = [C.sb("yo%d" % i, [128, 128]) for i in range(NS)]
    it = 0
    for d in range(2):
        order = (FWD_ORDER if d == 0 else BWD_ORDER)[:nchunks]
        Ui, nUi, Mi = (1, 2, 5) if d == 0 else (3, 4, 6)
        yout = y_f if d == 0 else y_b
        for h in range(2):
            P.op(DVE, lambda e, h=h: e.memset(hT[h][0][:], 0.0), [], [hT[h][1]])
            P.op(POOL, lambda e, h=h: e.memset(hTh[h][0][:], 0.0), [], [hTh[h][1]])
        for c in order:
            sl = it % NS
            it += 1
            cs = slice(c * 128, (c + 1) * 128)
            a2 = adt[:, c, 2 * d:2 * d + 2]
            P.mm(psA[0][:, 0:2], cm[:, Ui, :], a2, True, True, [cmb, adtb], [psA[1]])
            P.mm(psA[0][:, 2:4], ones[:], a2, True, True, [onesb, adtb], [psA[1]])
            s_, sb_ = sm[sl]
            P.act(s_[:, 0:4], psA[0][:, 0:4], AF.Exp, [psA[1]], [sb_])
            P.tt(DVE, s_[:, 6:8], psA[0][:, 2:4], psA[0][:, 0:2], ALU.subtract, [psA[1]], [sb_]) if False else None
            P.act(s_[:, 6:8], psA[0][:, 0:2], AF.Copy, [psA[1]], [sb_])
            P.tt(DVE, s_[:, 6:8], psA[0][:, 2:4], s_[:, 6:8], ALU.subtract, [psA[1], sb_], [sb_])
            P.act(s_[:, 4:6], s_[:, 6:8], AF.Exp, [sb_], [sb_])
            P.mm(psG[0][:, 0:128], Bf[:, cs], Cf[:, cs], True, True, [Bfb, Cfb], [psG[1]])
            for h in range(2):
                ab_, abb_ = abc[h][sl]
                P.ts(DVE, ab_[:], ones[:], adt[:, c, 2 * d + h:2 * d + h + 1], None, ALU.mult, None, [onesb, adtb], [abb_])
                sg, sgb = psSeg[h]
                P.mm(sg[:, 0:128], ab_[:], cm[:, Ui, :], True, False, [abb_, cmb], [sgb])
                P.mm(sg[:, 0:128], cm[:, nUi, :], ab_[:], False, False, [abb_, cmb], [sgb])
                P.mm(sg[:, 0:128], cm[:, 0, :], cm[:, Mi, :], False, True, [cmb], [sgb])
                de, deb = dec[h][sl]
                P.act(de[:], sg[:, 0:128], AF.Exp, [sgb], [deb])
                sc, scb = sct[h][sl]
                P.tt(DVE, sc[:], psG[0][:, 0:128], de[:], ALU.mult, [psG[1], deb], [scb])
                xd, xdb = xdt[h][sl]
                P.ts(POOL, xd[:], x_tm[:, c, h * 64:(h + 1) * 64], dt[:, c, 2 * d + h:2 * d + h + 1], None, ALU.mult, None,
                     [x_tmb, dtb], [xdb])
                bw_, bwb_ = bw[h][sl]
                P.ts(POOL, bw_[:], B_tm[:, c, :], s_[:, 4 + h:5 + h], None, ALU.mult, None, [B_tmb, sb_], [bwb_])
                P.mm(psY[0][:, h * 64:(h + 1) * 64], sc[:], xd[:], True, True, [scb, xdb], [psY[1]])
                P.mm(psY[0][:, 128 + h * 64:128 + (h + 1) * 64], Cf[:, cs], hTh[h][0][:], True, True,
                     [Cfb, hTh[h][1]], [psY[1]])
                P.mm(psH[0][:, h * 64:(h + 1) * 64], bw_[:], xd[:], True, True, [bwb_, xdb], [psH[1]])
            ys, ysb_ = ysb[sl]
            P.act(ys[:], psY[0][:, 0:128], AF.Copy, [psY[1]], [ysb_])
            yo_, yob_ = yo[sl]
            for h in range(2):
                hs = slice(h * 64, (h + 1) * 64)
                P.stt(yo_[:, hs], psY[0][:, 128 + h * 64:128 + (h + 1) * 64], s_[:, h:h + 1], ys[:, hs], ALU.mult, ALU.add,
                      [psY[1], sb_, ysb_], [yob_])
                if d == 0:
                    P.stt(yo_[:, hs], x_tm[:, c, hs], svt[:, 8 + h:9 + h], yo_[:, hs], ALU.mult, ALU.add,
                          [x_tmb, svb, yob_], [yob_])
                P.stt(hT[h][0][:], hT[h][0][:], s_[:, 2 + h:3 + h], psH[0][:, hs], ALU.mult, ALU.add,
                      [hT[h][1], sb_, psH[1]], [hT[h][1]])
                P.op(POOL, lambda e, h=h: e.tensor_copy(out=hTh[h][0][:], in_=hT[h][0][:]), [hT[h][1]], [hTh[h][1]])
            P.store(yout[cs, :], yo_[:], yob_, eng=SP)
    print("ssd program ninstr", P.ninstr, "nsem", P.nsem)
    return C.done()


def ssd_consts():
    if "s" in _CONST:
        return _CONST["s"]
    cmat = np.zeros((8, 128, 128), np.float32)
    idx = np.arange(128)
    cmat[0] = np.eye(128, dtype=np.float32)
    U = (idx[:, None] <= idx[None, :]).astype(np.float32)
    cmat[1] = U; cmat[2] = -U; cmat[3] = U.T; cmat[4] = -U.T
    cmat[5] = np.where(idx[None, :] >= idx[:, None], 0.0, -30000.0)
    cmat[6] = np.where(idx[None, :] <= idx[:, None], 0.0, -30000.0)
    _CONST["s"] = cmat
    return cmat


def ssd_inputs(PT, inp, l, j):
    gi = j // 2
    o = OFF["sxbc"]
    cols = [np.arange(2 * j * 64, (2 * j + 2) * 64), 512 + gi * 128 + np.arange(128), 768 + gi * 128 + np.arange(128)]
    cwv = np.stack([inp["ssd_conv_w"][l][:, cc].T for cc in cols], axis=1)
    cbv = np.stack([inp["ssd_conv_b"][l][cc] for cc in cols], axis=1)
    sv = np.zeros((128, 10), np.float32)
    hh = [2 * j, 2 * j + 1]
    for d in range(2):
        for h in range(2):
            sv[:, 2 * d + h] = inp["ssd_dt_bias"][l][d, hh[h]]
            sv[:, 4 + 2 * d + h] = inp["ssd_a_log"][l][d, hh[h]]
    sv[:, 8] = inp["ssd_d"][l][hh[0]]; sv[:, 9] = inp["ssd_d"][l][hh[1]]
    dcols = [OFF["sdt"] + d * 8 + hh[h] for d in range(2) for h in range(2)]
    return {
        "xs": np.ascontiguousarray(PT[o + cols[0]]), "Bm": np.ascontiguousarray(PT[o + cols[1]]),
        "Cm": np.ascontiguousarray(PT[o + cols[2]]),
        "cw": np.ascontiguousarray(cwv), "cb": np.ascontiguousarray(cbv),
        "dtr": np.ascontiguousarray(PT[dcols].T), "sv": sv, "cmat": ssd_consts(),
    }


RC = 64
DECAY_SCALE = 0.606531
LN_EPS = 64e-5


def rwkv_dir_blocks(d):
    out = []
    if d == 0:
        for (t0, T) in TBLK:
            out.append((t0, T, [o for o in range(0, T, RC)]))
    else:
        out.append((0, 256, [o for o in range(256 - RC, -1, -RC)]))
        for (t0, T) in reversed(TBLK[1:]):
            out.append((t0, T, [o for o in range(T - RC, -1, -RC)]))
    return out


def build_rwkv(nsteps=None, phases=(1, 2, 3), dbg=9, dbg_scr=False):
    C = Ctx("rwkv")
    P = C.P
    nc = C.nc
    names = ["r", "k", "v", "lrw", "lra", "gd"]
    src = {n: C.din("rw_" + n, [128, NTOK]) for n in names}
    cw = C.din("cw", [128, 6, 3])
    vec = C.din("vec", [128, 12])
    wup = C.din("wup", [128, 128]); aup = C.din("aup", [128, 128]); gup = C.din("gup", [128, 128])
    cmat = C.din("cmat", [3, 128, 128])
    msk = C.din("msk", [64, 2, 640])
    o_rwkv = C.dout("o_rwkv", [128, NTOK])
    snames = ["r", "v", "kk", "lw0", "lw1", "kt0", "kt1", "be0", "be1", "bonus", "g", "yf", "yb"]
    scr = {n: nc.dram_tensor("scr_" + n, [128, NTOK], F32, kind=("ExternalOutput" if dbg_scr else "Internal")).ap()
           for n in snames}
    scrb = {n: Buf("scr_" + n) for n in snames}

    cm, cmb = C.sb("cm", [128, 3, 128]); P.load(cm[:], cmat.rearrange("m p c -> p m c"), cmb)
    mk, mkb = C.sb("mk", [64, 2, 640]); P.load(mk[:], msk, mkb)
    vc, vcb = C.sb("vc", [128, 12]); P.load(vc[:], vec, vcb)
    P.ts(DVE, vc[:, 6:7], vc[:, 5:6], -1.0, 1.0, ALU.mult, ALU.add, [vcb], [vcb])
    cwt, cwb = C.sb("cwt", [128, 6, 3]); P.load(cwt[:], cw, cwb)
    wu, wub = C.sb("wu", [128, 128]); P.load(wu[:], wup, wub)
    au, aub = C.sb("au", [128, 128]); P.load(au[:], aup, aub)
    gu, gub = C.sb("gu", [128, 128]); P.load(gu[:], gup, gub)
    ones, onesb = C.sb("ones", [128, 64]); P.op(DVE, lambda e: e.memset(ones[:], 1.0), [], [onesb])
    lneps, lnepsb = C.sb("lneps", [128, 1]); P.op(DVE, lambda e: e.memset(lneps[:], LN_EPS), [], [lnepsb])
    idr, idrb = C.sb("idr", [64, 2, 64])
    for h in range(2):
        P.op(DVE, lambda e, h=h: e.tensor_copy(out=idr[:, h, :], in_=cm[0:64, 0, 0:64]), [cmb], [idrb])

    ps = [C.ps("rps%d" % i) for i in range(8)]

    if 1 in phases:
        with contextlib.ExitStack() as es:
            xin = [C.sb("rxin%d" % i, [128, 6, 514], F32, es) for i in range(2)]
            cv = {n: C.sb("rcv_" + n, [128, 512], F32, es) for n in names}
            NT1 = 10
            t1 = [C.sb("rt1_%d" % i, [128, 512], F32, es) for i in range(NT1)]
            c1 = [0]

            def tmp():
                t = t1[c1[0] % NT1]
                c1[0] += 1
                return t

            def put(name, t, tb, t0, T):
                tok = P.dma(POOL, lambda e: e.dma_start(out=scr[name][:, t0:t0 + T], in_=t[:, :T]), tb, [tb], [scrb[name]])
                return tok

            for bi, (t0, T) in enumerate(TBLK):
                seg0, seg1 = (0, CTX) if t0 < CTX else (CTX, NTOK)
                lo, hi = max(seg0, t0 - 1), min(seg1, t0 + T + 1)
                xi, xib = xin[bi % 2]
                P.op(POOL, lambda e, xi=xi: e.memset(xi[:], 0.0), [], [xib])
                for i6, n in enumerate(names):
                    P.load(xi[:, i6, lo - (t0 - 1):hi - (t0 - 1)], src[n][:, lo:hi], xib)
                for i6, n in enumerate(names):
                    a, ab = cv[n]
                    P.ts(DVE, a[:, :T], xi[:, i6, 0:T], cwt[:, i6, 0:1], None, ALU.mult, None, [xib, cwb], [ab])
                    for i in range(1, 3):
                        P.stt(a[:, :T], xi[:, i6, i:i + T], cwt[:, i6, i:i + 1], a[:, :T], ALU.mult, ALU.add,
                              [xib, cwb, ab], [ab])
                r_, rb_ = cv["r"]; k_, kb_ = cv["k"]; v_, vb_ = cv["v"]
                put("r", r_, rb_, t0, T); put("v", v_, vb_, t0, T)
                tw, twb = tmp()
                P.act(tw[:, :T], cv["lrw"][0][:, :T], AF.Tanh, [cv["lrw"][1]], [twb])
                a_d = []
                for d in range(2):
                    hs = slice(d * 64, (d + 1) * 64)
                    P.mm(ps[d][0][:, :T], wu[hs, :], tw[hs, :T], True, True, [wub, twb], [ps[d][1]])
                    lw, lwb = tmp()
                    P.act(lw[:, :T], ps[d][0][:, :T], AF.Sigmoid, [ps[d][1], vcb], [lwb], bias=vc[:, d:d + 1])
                    P.ts(DVE, lw[:, :T], lw[:, :T], -DECAY_SCALE, None, ALU.mult, None, [lwb], [lwb])
                    put("lw%d" % d, lw, lwb, t0, T)
                    P.mm(ps[2 + d][0][:, :T], au[hs, :], cv["lra"][0][hs, :T], True, True, [aub, cv["lra"][1]],
                         [ps[2 + d][1]])
                    a, ab = tmp()
                    P.act(a[:, :T], ps[2 + d][0][:, :T], AF.Sigmoid, [ps[2 + d][1], vcb], [ab], bias=vc[:, 2 + d:3 + d])
                    a_d.append((a, ab))
                kq, kqb = tmp()
                P.ts(DVE, kq[:, :T], k_[:, :T], vc[:, 4:5], None, ALU.mult, None, [kb_, vcb], [kqb])
                sq, sqb = tmp()
                P.act(sq[:, :T], kq[:, :T], AF.Square, [kqb], [sqb])
                P.mm(ps[4][0][:, :T], cm[:, 1, :], sq[:, :T], True, True, [cmb, sqb], [ps[4][1]])
                P.act(sq[:, :T], ps[4][0][:, :T], AF.Sqrt, [ps[4][1]], [sqb])
                P.ts(DVE, sq[:, :T], sq[:, :T], 1e-12, None, ALU.max, None, [sqb], [sqb])
                P.op(DVE, lambda e, sq=sq, T=T: e.reciprocal(out=sq[:, :T], in_=sq[:, :T]), [sqb], [sqb])
                P.tt(DVE, kq[:, :T], kq[:, :T], sq[:, :T], ALU.mult, [kqb, sqb], [kqb])
                put("kk", kq, kqb, t0, T)
                for d in range(2):
                    a, ab = a_d[d]
                    u, ub = tmp()
                    P.ts(DVE, u[:, :T], a[:, :T], vc[:, 5:6], vc[:, 6:7], ALU.mult, ALU.add, [ab, vcb], [ub])
                    P.tt(POOL, u[:, :T], u[:, :T], k_[:, :T], ALU.mult, [ub, kb_], [ub])
                    put("kt%d" % d, u, ub, t0, T)
                    be, beb = tmp()
                    P.tt(POOL, be[:, :T], a[:, :T], kq[:, :T], ALU.mult, [ab, kqb], [beb])
                    put("be%d" % d, be, beb, t0, T)
                rk, rkb = tmp()
                P.stt(rk[:, :T], r_[:, :T], vc[:, 7:8], k_[:, :T], ALU.mult, ALU.mult, [rb_, vcb, kb_], [rkb])
                P.mm(ps[5][0][:, :T], cm[:, 1, :], rk[:, :T], True, True, [cmb, rkb], [ps[5][1]])
                P.tt(DVE, rk[:, :T], ps[5][0][:, :T], v_[:, :T], ALU.mult, [ps[5][1], vb_], [rkb])
                put("bonus", rk, rkb, t0, T)
                sg, sgb = tmp()
                P.act(sg[:, :T], cv["gd"][0][:, :T], AF.Sigmoid, [cv["gd"][1]], [sgb])
                P.mm(ps[6][0][:, :T], gu[:], sg[:, :T], True, True, [gub, sgb], [ps[6][1]])
                P.act(sg[:, :T], ps[6][0][:, :T], AF.Copy, [ps[6][1]], [sgb])
                put("g", sg, sgb, t0, T)

    P.barrier()
    if 2 in phases:
        with contextlib.ExitStack() as es:
            ST = [C.sb("ST%d" % d, [64, 128], F32, es) for d in range(2)]
            blk = [[C.sb("rblk%d_%d" % (d, i), [64, 6, 2, 512], F32, es) for i in range(2)] for d in range(2)]
            YB = [[C.sb("rYB%d_%d" % (d, i), [64, 2, 512], F32, es) for i in range(2)] for d in range(2)]

            def mk2(name, shape, n=2):
                return [[C.sb("%s%d_%d" % (name, d, i), shape, F32, es) for i in range(n)] for d in range(2)]
            CL = mk2("rCL", [64, 3, 2, 64], 3); EE = mk2("rEE", [64, 4, 2, 64], 3); ET = mk2("rET", [64, 2], 3)
            KR = mk2("rKR", [64, 2, 128], 3); BK = mk2("rBK", [64, 2, 2, 64], 3); KB2 = mk2("rKB2", [64, 2, 2, 64], 3)
            PW = mk2("rPW", [64, 512]); MS = mk2("rMS", [64, 128]); TR = mk2("rTR", [64, 384])
            ZS = mk2("rZS", [64, 128]); PP = mk2("rPP", [64, 256], 4); XS = mk2("rXS", [64, 128]); SAn = mk2("rSAn", [64, 128])
            for d in range(2):
                P.op(DVE, lambda e, d=d: e.memset(ST[d][0][:], 0.0), [], [ST[d][1]])
            seqs = []
            for d in range(2):
                lst = []
                for bidx, (t0, T, offs) in enumerate(rwkv_dir_blocks(d)):
                    for oi, o in enumerate(offs):
                        lst.append((bidx, t0, T, o, oi == 0, oi == len(offs) - 1))
                seqs.append(lst)
            nst = len(seqs[0]) if nsteps is None else nsteps
            snm = [["r", "v", "kk", "lw0", "kt0", "be0"], ["r", "v", "kk", "lw1", "kt1", "be1"]]
            ppc = [0, 0]
            id64 = cm[0:64, 0, 0:64]
            def stageAB(i):
                sl = i % 3
                for d in range(2):
                    bidx, t0, T, o, first, last = seqs[d][i]
                    b_, bb_ = blk[d][bidx % 2]
                    if first:
                        for i6, n in enumerate(snm[d]):
                            P.load(b_[:, i6, :, :T], scr[n][:, t0:t0 + T].rearrange("(h c) t -> c h t", h=2), bb_,
                                   reads=[scrb[n]], writes=[bb_])
                    cs = slice(o, o + RC)
                    cl, clb = CL[d][sl]; ee, eeb = EE[d][sl]; et, etb = ET[d][sl]
                    lw = b_[:, 3, :, cs]
                    for h in range(2):
                        P.op(DVE, lambda e, cl=cl, b_=b_, cs=cs, h=h: e.tensor_tensor_scan(
                            out=cl[:, 0, h, :], data0=ones[0:64, :], data1=b_[:, 3, h, cs], initial=0.0,
                            op0=ALU.mult, op1=ALU.add), [onesb, bb_], [clb])
                    if d == 1:
                        for h in range(2):
                            P.ts(DVE, cl[:, 2, h, :], cl[:, 0, h, :], -1.0, cl[:, 0, h, 63:64], ALU.mult, ALU.add, [clb], [clb])
                        P.tt(DVE, cl[:, 1], cl[:, 2], lw, ALU.add, [clb, bb_], [clb])
                        cin, cex = cl[:, 1], cl[:, 2]
                    else:
                        P.tt(DVE, cl[:, 1], cl[:, 0], lw, ALU.subtract, [clb, bb_], [clb])
                        cin, cex = cl[:, 0], cl[:, 1]
                    P.act(ee[:, 0], cin, AF.Exp, [clb], [eeb])
                    P.act(ee[:, 1], cex, AF.Exp, [clb], [eeb])
                    P.act(ee[:, 2], cin, AF.Exp, [clb], [eeb], scale=-1.0)
                    for h in range(2):
                        tot = cl[:, 0, h, 63:64]
                        P.act(ee[:, 3, h, :], cin[:, h, :], AF.Exp, [clb], [eeb], scale=-1.0, bias=tot)
                        P.act(et[:, h:h + 1], tot, AF.Exp, [clb], [etb])
                    kr, krb = KR[d][sl]; bk, bkb = BK[d][sl]; kb2, kb2b = KB2[d][sl]
                    P.tt(DVE, kr[:, :, 0:64], b_[:, 2, :, cs], ee[:, 1], ALU.mult, [bb_, eeb], [krb])
                    P.tt(POOL, kr[:, :, 64:128], b_[:, 0, :, cs], ee[:, 0], ALU.mult, [bb_, eeb], [krb])
                    P.tt(DVE, bk[:, 0], b_[:, 5, :, cs], ee[:, 2], ALU.mult, [bb_, eeb], [bkb])
                    P.tt(POOL, bk[:, 1], b_[:, 4, :, cs], ee[:, 2], ALU.mult, [bb_, eeb], [bkb])
                    P.tt(DVE, kb2[:, 0], b_[:, 4, :, cs], ee[:, 3], ALU.mult, [bb_, eeb], [kb2b])
                    P.tt(POOL, kb2[:, 1], b_[:, 5, :, cs], ee[:, 3], ALU.mult, [bb_, eeb], [kb2b])

            def stageC(i):
                sl = i % 2
                for d in range(2):
                    bidx, t0, T, o, first, last = seqs[d][i]
                    b_, bb_ = blk[d][bidx % 2]
                    cs = slice(o, o + RC)
                    kr, krb = KR[d][i % 3]; bk, bkb = BK[d][i % 3]; kb2, kb2b = KB2[d][i % 3]
                    pa, pab = ps[4 * d + 0]; pb, pbb = ps[4 * d + 1]; pd, pdb = ps[4 * d + 3]
                    for h in range(2):
                        P.mm(pa[0:64, h * 128:(h + 1) * 128], bk[:, 0, h, :], kr[:, h, :], True, True, [bkb, krb], [pab])
                        P.mm(pa[0:64, 256 + h * 128:256 + (h + 1) * 128], bk[:, 1, h, :], kr[:, h, :], True, True,
                             [bkb, krb], [pab])
                        P.mm(pb[0:64, h * 64:(h + 1) * 64], kr[:, h, 0:64], bk[:, 0, h, :], True, True, [krb, bkb], [pbM[d]])
                    for h in range(2):
                        for (srcap, rb, c0) in [(kb2[:, 0, h, :], kb2b, 64 + h * 64), (kb2[:, 1, h, :], kb2b, 192 + h * 64),
                                                (b_[:, 1, h, cs], bb_, 320 + h * 64)]:
                            P.op(PE, lambda e, pd=pd, srcap=srcap, c0=c0: e.transpose(pd[0:64, c0:c0 + 64], srcap, id64),
                                 [rb, cmb], [pdb], sync_same=False)
                for d in range(2):
                    pa, pab = ps[4 * d + 0]; pb, pbb = ps[4 * d + 1]; pd, pdb = ps[4 * d + 3]
                    pw, pwb = PW[d][sl]; ms, msb = MS[d][sl]; tr, trb = TR[d][sl]; zs, zsb = ZS[d][sl]
                    P.tt(DVE, pw[:], pa[0:64, :], mk[:, d, 0:512], ALU.mult, [pab, mkb], [pwb])
                    P.tt(DVE, ms[:], pb[0:64, 0:128], mk[:, d, 512:640], ALU.mult, [pbM[d], mkb], [msb])
                    P.act(tr[:], pd[0:64, 64:448], AF.Copy, [pdb], [trb])
                    mt_v = pw[:, 0:256].rearrange("p (h x) -> p h x", h=2)[:, :, 0:64]
                    P.tt(POOL, zs[:].rearrange("p (h x) -> p h x", h=2), idr[:], mt_v, ALU.subtract, [idrb, pwb], [zsb])

            def stageD(i):
                sl = i % 2
                cur = []
                for d in range(2):
                    pw, pwb = PW[d][sl]; ms, msb = MS[d][sl]
                    cur.append(([pw[:, h * 128:h * 128 + 64] for h in range(2)], pwb,
                                [ms[:, h * 64:(h + 1) * 64] for h in range(2)], msb))
                for kk_ in range(1, 6):
                    for d in range(2):
                        pb, pbb = ps[4 * d + 1]
                        Pm, Pmb, PTm, PTmb = cur[d]
                        for h in range(2):
                            P.mm(pb[0:64, 128 + h * 64:128 + (h + 1) * 64], PTm[h], Pm[h], True, True, [PTmb, Pmb], [pbb])
                            P.mm(pb[0:64, 256 + h * 64:256 + (h + 1) * 64], Pm[h], PTm[h], True, True, [PTmb, Pmb], [pbb])
                    yield
                    for d in range(2):
                        pb, pbb = ps[4 * d + 1]
                        pp, ppb = PP[d][ppc[d] % 4]
                        ppc[d] += 1
                        P.op(DVE, lambda e, pp=pp, pb=pb: e.tensor_copy(out=pp[:], in_=pb[0:64, 128:384]), [pbb], [ppb])
                        cur[d] = ([pp[:, h * 64:(h + 1) * 64] for h in range(2)], ppb,
                                  [pp[:, 128 + h * 64:128 + (h + 1) * 64] for h in range(2)], ppb)
                    yield
                    for d in range(2):
                        pb, pbb = ps[4 * d + 1]
                        zs, zsb = ZS[d][sl]
                        for h in range(2):
                            P.mm(pb[0:64, 384 + h * 64:384 + (h + 1) * 64], cur[d][2][h], zs[:, h * 64:(h + 1) * 64],
                                 True, True, [cur[d][1], zsb], [pbZ[d]])
                    yield
                    for d in range(2):
                        pb, pbb = ps[4 * d + 1]
                        zs, zsb = ZS[d][sl]
                        P.tt(DVE, zs[:], zs[:], pb[0:64, 384:512], ALU.add, [zsb, pbZ[d]], [zsb])
                    yield

            def stageE(i):
                sl = i % 2
                R = range(2)
                info = [seqs[d][i] for d in R]
                for d in R:
                    kr, krb = KR[d][i % 3]; pw, pwb = PW[d][sl]; tr, trb = TR[d][sl]
                    st, stb = ST[d]; pc, pcb = ps[4 * d + 2]
                    for h in range(2):
                        hs = slice(h * 64, (h + 1) * 64)
                        P.mm(pc[0:64, hs], kr[:, h, 0:64], st[:, hs], True, False, [krb, stb], [pcb])
                        P.mm(pc[0:64, hs], pw[:, 256 + h * 128:256 + h * 128 + 64], tr[:, 256 + h * 64:256 + (h + 1) * 64],
                             False, True, [pwb, trb], [pcb])
                yield
                for d in R:
                    pc, pcb = ps[4 * d + 2]; xs_, xsb_ = XS[d][sl]
                    P.op(DVE, lambda e, xs_=xs_, pc=pc: e.tensor_copy(out=xs_[:], in_=pc[0:64, 0:128]), [pcb], [xsb_])
                yield
                for d in R:
                    pc, pcb = ps[4 * d + 2]; xs_, xsb_ = XS[d][sl]; zs, zsb = ZS[d][sl]
                    for h in range(2):
                        hs = slice(h * 64, (h + 1) * 64)
                        P.mm(pc[0:64, 128 + h * 64:128 + (h + 1) * 64], zs[:, hs], xs_[:, hs], True, True, [zsb, xsb_], [pcb])
                yield
                for d in R:
                    pc, pcb = ps[4 * d + 2]; sa, sab = SAn[d][sl]
                    P.ts(DVE, sa[:], pc[0:64, 128:256], -1.0, None, ALU.mult, None, [pcb], [sab])
                yield
                for d in R:
                    kr, krb = KR[d][i % 3]; pw, pwb = PW[d][sl]; tr, trb = TR[d][sl]
                    st, stb = ST[d]; pc, pcb = ps[4 * d + 2]; sa, sab = SAn[d][sl]
                    for h in range(2):
                        hs = slice(h * 64, (h + 1) * 64)
                        yo_ = pc[0:64, 384 + h * 64:384 + (h + 1) * 64]
                        P.mm(yo_, st[:, hs], kr[:, h, 64:128], True, False, [stb, krb], [pcb])
                        P.mm(yo_, tr[:, 256 + h * 64:256 + (h + 1) * 64],
                             pw[:, 256 + h * 128 + 64:256 + (h + 1) * 128], False, False, [trb, pwb], [pcb])
                        P.mm(yo_, sa[:, hs], pw[:, h * 128 + 64:(h + 1) * 128], False, True, [sab, pwb], [pcb])
                    for h in range(2):
                        hs = slice(h * 64, (h + 1) * 64)
                        su_ = pc[0:64, 256 + h * 64:256 + (h + 1) * 64]
                        P.mm(su_, tr[:, hs], tr[:, 256 + h * 64:256 + (h + 1) * 64], True, False, [trb], [pcb])
                        P.mm(su_, tr[:, 128 + h * 64:128 + (h + 1) * 64], sa[:, hs], False, True, [trb, sab], [pcb])
                yield
                for d in R:
                    bidx, t0, T, o, first, last = info[d]
                    et, etb = ET[d][i % 3]; st, stb = ST[d]; pc, pcb = ps[4 * d + 2]
                    for h in range(2):
                        hs = slice(h * 64, (h + 1) * 64)
                        P.stt(st[:, hs], st[:, hs], et[:, h:h + 1], pc[0:64, 256 + h * 64:256 + (h + 1) * 64], ALU.mult, ALU.add,
                              [stb, etb, pcb], [stb])
                    yb, ybb = YB[d][bidx % 2]
                    P.op(DVE, lambda e, yb=yb, pc=pc, o=o: e.tensor_copy(
                        out=yb[:, :, o:o + RC], in_=pc[0:64, 384:512].rearrange("p (h x) -> p h x", h=2)), [pcb], [ybb])
                    if last:
                        nm = "yf" if d == 0 else "yb"
                        P.dma(SP, lambda e, nm=nm, t0=t0, T=T, yb=yb: e.dma_start(
                            out=scr[nm][:, t0:t0 + T].rearrange("(h c) t -> c h t", h=2), in_=yb[:, :, :T]),
                            ybb, [ybb], [scrb[nm]])

            pbM = [Buf("pbM%d" % d) for d in range(2)]
            pbZ = [Buf("pbZ%d" % d) for d in range(2)]
            def drain(*gens):
                gens = [g for g in gens if g is not None]
                while gens:
                    for g in list(gens):
                        try:
                            next(g)
                        except StopIteration:
                            gens.remove(g)

            stageAB(0)
            stageC(0)
            for i in range(nst):
                if i + 1 < nst:
                    stageAB(i + 1)
                drain(stageD(i))
                if i + 1 < nst:
                    stageC(i + 1)
                drain(stageE(i))

    P.barrier()
    if 3 in phases:
        with contextlib.ExitStack() as es:
            NT3 = 8
            t3 = [C.sb("rt3_%d" % i, [128, 512], F32, es) for i in range(NT3)]
            c3 = [0]

            def tmp3():
                t = t3[c3[0] % NT3]
                c3[0] += 1
                return t
            for bi, (t0, T) in enumerate(TBLK):
                yf, yfb = tmp3(); yb, ybb = tmp3(); bo, bob = tmp3(); g, gb = tmp3()
                P.load(yf[:, :T], scr["yf"][:, t0:t0 + T], yfb, reads=[scrb["yf"]], writes=[yfb])
                P.load(yb[:, :T], scr["yb"][:, t0:t0 + T], ybb, reads=[scrb["yb"]], writes=[ybb])
                P.load(bo[:, :T], scr["bonus"][:, t0:t0 + T], bob, reads=[scrb["bonus"]], writes=[bob])
                P.load(g[:, :T], scr["g"][:, t0:t0 + T], gb, reads=[scrb["g"]], writes=[gb])
                P.tt(DVE, yf[:, :T], yf[:, :T], yb[:, :T], ALU.add, [yfb, ybb], [yfb])
                P.mm(ps[0][0][:, :T], cm[:, 2, :], yf[:, :T], True, True, [cmb, yfb], [ps[0][1]])
                P.tt(DVE, yf[:, :T], yf[:, :T], ps[0][0][:, :T], ALU.subtract, [yfb, ps[0][1]], [yfb])
                P.act(yb[:, :T], yf[:, :T], AF.Square, [yfb], [ybb])
                P.mm(ps[1][0][:, :T], cm[:, 2, :], yb[:, :T], True, True, [cmb, ybb], [ps[1][1]])
                P.act(yb[:, :T], ps[1][0][:, :T], AF.Sqrt, [ps[1][1], lnepsb], [ybb], bias=lneps[:])
                P.op(DVE, lambda e, yb=yb, T=T: e.reciprocal(out=yb[:, :T], in_=yb[:, :T]), [ybb], [ybb])
                P.stt(yf[:, :T], yf[:, :T], vc[:, 8:9], yb[:, :T], ALU.mult, ALU.mult, [yfb, vcb, ybb], [yfb])
                P.stt(yf[:, :T], yf[:, :T], vc[:, 9:10], bo[:, :T], ALU.add, ALU.add, [yfb, vcb, bob], [yfb])
                P.tt(DVE, yf[:, :T], yf[:, :T], g[:, :T], ALU.mult, [yfb, gb], [yfb])
                P.store(o_rwkv[:, t0:t0 + T], yf[:, :T], yfb, eng=SP)
    print("rwkv program ninstr", P.ninstr, "nsem", P.nsem)
    return C.done()


def rwkv_consts():
    if "r" in _CONST:
        return _CONST["r"]
    cmat = np.zeros((3, 128, 128), np.float32)
    cmat[0] = np.eye(128, dtype=np.float32)
    for b in range(2):
        cmat[1, b * 64:(b + 1) * 64, b * 64:(b + 1) * 64] = 1.0
        cmat[2, b * 64:(b + 1) * 64, b * 64:(b + 1) * 64] = 1.0 / 64
    j = np.arange(64)[:, None]; t = np.arange(64)[None, :]
    msk = np.zeros((64, 2, 640), np.float32)
    strict = [(t > j), (t < j)]; incl = [(t >= j), (t <= j)]
    for d in range(2):
        for s in range(8):
            msk[:, d, s * 64:(s + 1) * 64] = (strict[d] if s % 2 == 0 else incl[d])
        mm = (t < j) if d == 0 else (t > j)
        for h in range(2):
            msk[:, d, 512 + h * 64:512 + (h + 1) * 64] = mm
    _CONST["r"] = (cmat, msk)
    return _CONST["r"]


def rwkv_inputs(PT, inp, l, j):
    cmat, msk = rwkv_consts()
    o = OFF["rw"]
    hc = np.arange(2 * j * 64, (2 * j + 2) * 64)
    rows = {"r": o + hc, "k": o + 512 + hc, "v": o + 1024 + hc, "lrw": o + 1536 + np.arange(128),
            "lra": o + 1536 + 128 + np.arange(128), "gd": o + 1536 + 256 + np.arange(128)}
    d = {"rw_" + n: np.ascontiguousarray(PT[rows[n]]) for n in rows}
    sw = inp["rwkv_shift_w"][l]
    d["cw"] = np.ascontiguousarray(np.stack([sw[:, rows[n] - o].T for n in ["r", "k", "v", "lrw", "lra", "gd"]], axis=1))
    vec = np.zeros((128, 12), np.float32)
    vec[:, 0] = inp["rwkv_w0"][l][0, hc]; vec[:, 1] = inp["rwkv_w0"][l][1, hc]
    vec[:, 2] = inp["rwkv_a0"][l][0, hc]; vec[:, 3] = inp["rwkv_a0"][l][1, hc]
    vec[:, 4] = inp["rwkv_k_k"][l][hc]; vec[:, 5] = inp["rwkv_k_a"][l][hc]
    vec[:, 7] = inp["rwkv_r_k"][l].reshape(-1)[hc]
    vec[:, 8] = inp["rwkv_ln_g"][l][hc]; vec[:, 9] = inp["rwkv_ln_b"][l][hc]
    d["vec"] = vec
    d["wup"] = np.ascontiguousarray(np.concatenate([inp["rwkv_w_up"][l][0][:, hc], inp["rwkv_w_up"][l][1][:, hc]], 0))
    d["aup"] = np.ascontiguousarray(np.concatenate([inp["rwkv_a_up"][l][0][:, hc], inp["rwkv_a_up"][l][1][:, hc]], 0))
    d["gup"] = np.ascontiguousarray(inp["rwkv_g_up"][l][:, hc])
    d["cmat"] = cmat; d["msk"] = msk
    return d


_PROGS = {}


def _prog(key, fn):
    if key not in _PROGS:
        _PROGS[key] = fn()
    return _PROGS[key]


def _run(nc, in_maps):
    return run_bass_kernel_spmd(nc, in_maps, core_ids=list(range(NCORE))).results


def _big_weights(inp):
    lst = []
    for l in range(2):
        for i in range(2):
            lst += [("wg_%d_%d" % (l, i), inp["ffn_w_gate"][l, i]), ("wu_%d_%d" % (l, i), inp["ffn_w_up"][l, i]),
                    ("wd_%d_%d" % (l, i), inp["ffn_w_down"][l, i])]
        lst += [("win_%d" % l, inp["w_in"][l]), ("wout_%d" % l, inp["w_out"][l])]
    return lst


def kernel(**inp):
    inp = {k: np.asarray(v) for k, v in inp.items()}
    big = _big_weights(inp)
    per_core = sum(w.size for _, w in big) // NCORE
    L = -(-per_core // (128 * CONV_PIECE)) * CONV_PIECE
    cs = np.concatenate([inp["c"], inp["c_ctx"][None]], 0)
    cst = np.ascontiguousarray(cs.reshape(3, KC, 128).transpose(2, 1, 0))
    in_maps = []
    for c in range(NCORE):
        flat = np.zeros(128 * L, np.float32)
        o = 0
        for _, w in big:
            r = w.shape[0] // NCORE
            sl = w[c * r:(c + 1) * r].ravel()
            flat[o:o + sl.size] = sl
            o += sl.size
        cols = slice(c * MODC, (c + 1) * MODC)
        in_maps.append({
            "cst": cst,
            "mw": np.ascontiguousarray(inp["mod_w"][:, :, cols].reshape(2, KC, 128, MODC)),
            "mb": np.ascontiguousarray(np.broadcast_to(inp["mod_b"].reshape(2, 1, -1)[:, :, cols], (2, 3, MODC))),
            "wf": flat.reshape(128, L),
        })
    res = _run(_prog(("prep", L), lambda: build_prep(L)), in_maps)
    del in_maps
    m = np.concatenate([r["m_out"] for r in res], axis=2)
    m9 = m.reshape(2, 3, 9, D)
    wbf = {}
    flats = [np.asarray(r["wb"]).reshape(-1) for r in res]
    o = 0
    for name, w in big:
        r = w.shape[0] // NCORE
        n = r * w.shape[1]
        wbf[name] = np.concatenate([f[o:o + n].reshape(r, w.shape[1]) for f in flats], axis=0)
        o += n
    del flats, res
    wl = {}
    for name, w in wbf.items():
        if name.startswith("wd_") or name.startswith("wout_"):
            wl[name] = lay_wd(w)
        else:
            wl[name] = lay_w22(w)
    del wbf
    normg = np.ascontiguousarray(lay_vec(inp["norm_g"]))
    modv = []
    for b in range(2):
        mv = np.zeros((128, 2, 2, 9, KC), np.float32)
        for l in range(2):
            mv[:, l, 0] = lay_vec(m9[l, 2])
            mv[:, l, 1] = lay_vec(m9[l, b])
        modv.append(mv)

    z_full = np.concatenate([inp["ctx"], inp["x"]], axis=1)
    toks = [core_tokens(c) for c in range(NCORE)]
    zT = [fm(z_full[c // 4][toks[c]]) for c in range(NCORE)]
    del z_full

    def token_launch(stages, zT, extra):
        nc = _prog(("token",) + tuple(stages), lambda: build_token(stages))
        in_maps = []
        for c in range(NCORE):
            d = {"zT": zT[c], "modv": modv[c // 4], "normg": normg}
            for s in stages:
                if s[0] == "ffn":
                    for p in ("wg", "wu", "wd"):
                        nm = "%s_%d_%d" % (p, s[1], s[2])
                        d[nm] = wl[nm]
                elif s[0] == "inproj":
                    d["win_%d" % s[1]] = wl["win_%d" % s[1]]
                elif s[0] == "outproj":
                    d["wout_%d" % s[1]] = wl["wout_%d" % s[1]]
            d.update(extra[c])
            in_maps.append(d)
        return _run(nc, in_maps)

    def assemble_PT(res, l):
        PT = np.zeros((2, INP, NTOK), np.float32)
        for c in range(NCORE):
            PT[c // 4][:, toks[c]] = np.asarray(res[c]["pT_%d" % l]).reshape(INP, NT_CORE)
        return PT

    def mixers(PT, l):
        oT = np.zeros((2, D, NTOK), np.float32)
        ybT = np.zeros((2, 512, NTOK), np.float32)
        ra = _run(_prog("attn", build_attn), [attn_inputs(PT[c // 4], inp, l, c % 4) for c in range(NCORE)])
        for c in range(NCORE):
            b, j = c // 4, c % 4
            oT[b, j * 128:(j + 1) * 128] = ra[c]["o_diff"]
            oT[b, 1024 + j * 128:1024 + (j + 1) * 128] = ra[c]["o_mla"]
        del ra
        rr = _run(_prog("rwkv", build_rwkv), [rwkv_inputs(PT[c // 4], inp, l, c % 4) for c in range(NCORE)])
        for c in range(NCORE):
            b, j = c // 4, c % 4
            oT[b, 512 + j * 128:512 + (j + 1) * 128] = rr[c]["o_rwkv"]
        del rr
        rs = _run(_prog("ssd", build_ssd), [ssd_inputs(PT[c // 4], inp, l, c % 4) for c in range(NCORE)])
        for c in range(NCORE):
            b, j = c // 4, c % 4
            oT[b, 1536 + j * 128:1536 + (j + 1) * 128] = np.asarray(rs[c]["y_f"]).T
            ybT[b, j * 128:(j + 1) * 128] = np.asarray(rs[c]["y_b"]).T
        return oT, ybT

    def outproj_extra(oT, ybT, PT, l):
        ex = []
        sg = np.ascontiguousarray(inp["ssd_norm_g"][l].reshape(4, 128).T)
        for c in range(NCORE):
            b = c // 4
            ex.append({
                "oT_%d" % l: np.ascontiguousarray(oT[b][:, toks[c]].reshape(KC, 128, NT_CORE)),
                "ybT_%d" % l: np.ascontiguousarray(ybT[b][:, toks[c]].reshape(4, 128, NT_CORE)),
                "zgT_%d" % l: np.ascontiguousarray(PT[b][OFF["sz"]:OFF["sz"] + 512][:, toks[c]].reshape(4, 128, NT_CORE)),
                "ssdg_%d" % l: sg,
            })
        return ex

    res = token_launch([("ffn", 0, 0), ("inproj", 0)], zT, [{} for _ in range(NCORE)])
    zT = [np.asarray(r["z_out"]) for r in res]
    PT = assemble_PT(res, 0)
    del res
    oT, ybT = mixers(PT, 0)
    res = token_launch([("outproj", 0), ("ffn", 0, 1), ("ffn", 1, 0), ("inproj", 1)], zT, outproj_extra(oT, ybT, PT, 0))
    zT = [np.asarray(r["z_out"]) for r in res]
    PT = assemble_PT(res, 1)
    del res
    oT, ybT = mixers(PT, 1)
    res = token_launch([("outproj", 1), ("ffn", 1, 1)], zT, outproj_extra(oT, ybT, PT, 1))
    out = np.zeros((2, SEQ, D), np.float32)
    for c in range(NCORE):
        b, q = c // 4, c % 4
        zc = np.asarray(res[c]["z_out"]).reshape(D, NT_CORE)
        out[b, q * 4096:(q + 1) * 4096] = zc[:, 64:].T
    return out
```

```python
import contextlib
import math
import numpy as np
import ml_dtypes
import concourse.bass as bass
import concourse.mybir as mybir
from concourse.bass_utils import run_bass_kernel_spmd

F32 = mybir.dt.float32
BF16 = mybir.dt.bfloat16
AF = mybir.ActivationFunctionType
ALU = mybir.AluOpType
NPBF = ml_dtypes.bfloat16

PE, ACT, DVE, POOL, SP = "tensor", "scalar", "vector", "gpsimd", "sync"
ENGS = [PE, ACT, DVE, POOL, SP]
EPOCH = 30000

D = 2048
KC = 16
FF = 5632
FC = 44
NCORE = 8
SEQ = 16384
CTX = 256
NTOK = SEQ + CTX
IN_COLS = 5552
INP = 5632
EPS = 1e-6


class Buf:
    __slots__ = ("name", "last_write", "readers", "dsem", "dcount")

    def __init__(self, name=""):
        self.name = name
        self.last_write = None
        self.readers = []
        self.dsem = None
        self.dcount = 0


class Prog:
    def __init__(self, nc, es):
        self.nc = nc
        self.es = es
        self.q = {e: [] for e in ENGS}
        self.cnt = {e: 0 for e in ENGS}
        self.epoch = {e: 0 for e in ENGS}
        self.waited = {e: {} for e in ENGS}
        self.esem = {e: self._newsem("c_%s_0" % e) for e in ENGS}
        self.ninstr = 0
        self.nsem = 5
        self.final = []
        self.dma_toks = {}

    def _newsem(self, name):
        self.nsem = getattr(self, "nsem", 0) + 1
        return self.es.enter_context(self.nc.semaphore(name))

    def _deps(self, reads, writes):
        deps = []
        for b in reads:
            if b.last_write is not None:
                deps.append(b.last_write)
        for b in writes:
            if b.last_write is not None:
                deps.append(b.last_write)
            deps.extend(b.readers)
        return deps

    def _reduce(self, eng, deps, sync_same):
        need = {}
        w = self.waited[eng]
        for sem, val, key in deps:
            if (not sync_same) and key[0] == eng:
                continue
            if w.get(key, 0) >= val:
                continue
            if need.get(key, (None, 0))[1] < val:
                need[key] = (sem, val)
        for key, (sem, val) in need.items():
            w[key] = val
        return list(need.values())

    def op(self, eng, fn, reads=(), writes=(), sync_same=True):
        deps = self._deps(reads, writes)
        waits = self._reduce(eng, deps, sync_same)
        if self.cnt[eng] >= EPOCH:
            self.epoch[eng] += 1
            self.cnt[eng] = 0
            self.esem[eng] = self._newsem("c_%s_%d" % (eng, self.epoch[eng]))
        self.cnt[eng] += 1
        tok = (self.esem[eng], self.cnt[eng], (eng, self.epoch[eng]))
        self.q[eng].append((fn, waits, (self.esem[eng], 1)))
        for b in reads:
            b.readers.append(tok)
        for b in writes:
            b.last_write = tok
            b.readers = []
        self.ninstr += 1
        return tok

    def dma(self, eng, fn, track, reads=(), writes=()):
        deps = self._deps(reads, writes)
        waits = self._reduce(eng, deps, True)
        if track.dsem is None:
            track.dsem = self._newsem("d_" + track.name)
        track.dcount += 16
        tok = (track.dsem, track.dcount, ("dma", id(track)))
        self.q[eng].append((fn, waits, (track.dsem, 16)))
        self.dma_toks[("dma", id(track))] = tok
        for b in reads:
            b.readers.append(tok)
        for b in writes:
            b.last_write = tok
            b.readers = []
        self.ninstr += 1
        return tok

    def finish(self, eng=SP):
        waits = self._reduce(eng, list(self.final), True)
        self.q[eng].append((None, waits, None))

    def barrier(self):
        toks = [(self.esem[e], self.cnt[e], (e, self.epoch[e])) for e in ENGS if self.cnt[e] > 0]
        toks += list(self.dma_toks.values())
        for e in ENGS:
            waits = self._reduce(e, toks, False)
            self.q[e].append((None, waits, None))

    def store(self, out, in_, src, eng=POOL, **kw):
        tok = self.dma(eng, lambda e: e.dma_start(out=out, in_=in_, **kw), src, [src], [])
        self.final.append(tok)
        return tok

    def build(self, block):
        q = self.q

        def runner(ename):
            def body(e):
                for fn, waits, inc in q[ename]:
                    for sem, val in waits:
                        e.wait_ge(sem, val)
                    if fn is not None:
                        fn(e).then_inc(inc[0], inc[1])
            return body
        block.tensor(runner(PE))
        block.scalar(runner(ACT))
        block.vector(runner(DVE))
        block.gpsimd(runner(POOL))
        block.sync(runner(SP))

    def mm(self, out, lhsT, rhs, start, stop, reads, writes):
        return self.op(PE, lambda e: e.matmul(out, lhsT, rhs, start=start, stop=stop), reads, writes,
                       sync_same=False)

    def act(self, out, in_, func, reads, writes, scale=1.0, bias=None):
        if bias is None:
            return self.op(ACT, lambda e: e.activation(out=out, in_=in_, func=func, scale=scale), reads, writes)
        return self.op(ACT, lambda e: e.activation(out=out, in_=in_, func=func, scale=scale, bias=bias),
                       reads, writes)

    def tt(self, eng, out, in0, in1, op, reads, writes):
        return self.op(eng, lambda e: e.tensor_tensor(out=out, in0=in0, in1=in1, op=op), reads, writes)

    def ts(self, eng, out, in0, s1, s2, op0, op1, reads, writes):
        if op1 is None:
            return self.op(eng, lambda e: e.tensor_scalar(out=out, in0=in0, scalar1=s1, scalar2=None, op0=op0),
                           reads, writes)
        return self.op(eng, lambda e: e.tensor_scalar(out=out, in0=in0, scalar1=s1, scalar2=s2, op0=op0, op1=op1),
                       reads, writes)

    def stt(self, out, in0, scalar, in1, op0, op1, reads, writes):
        return self.op(DVE, lambda e: e.scalar_tensor_tensor(out=out, in0=in0, scalar=scalar, in1=in1,
                                                             op0=op0, op1=op1), reads, writes)

    def load(self, out, in_, track, reads=(), writes=None, eng=SP, **kw):
        writes = [track] if writes is None else writes
        return self.dma(eng, lambda e: e.dma_start(out=out, in_=in_, **kw), track, reads, writes)


class Ctx:
    def __init__(self, name):
        self.nc = bass.Bass("TRN2", target_bir_lowering=False)
        self.es = contextlib.ExitStack()
        self.P = Prog(self.nc, self.es)
        self.name = name
        self.outs = []

    def din(self, name, shape, dt=F32):
        return self.nc.dram_tensor(name, list(shape), dt, kind="ExternalInput").ap()

    def dout(self, name, shape, dt=F32):
        return self.nc.dram_tensor(name, list(shape), dt, kind="ExternalOutput").ap()

    def sb(self, name, shape, dt=F32, es=None):
        t = (es or self.es).enter_context(self.nc.sbuf_tensor(name, list(shape), dt))
        return t, Buf(name)

    def ps(self, name, shape=(128, 512), dt=F32, es=None):
        t = (es or self.es).enter_context(self.nc.psum_tensor(name, list(shape), dt))
        return t, Buf(name)

    def done(self):
        self.P.finish()
        with self.nc.Block() as block:
            self.P.build(block)
        self.es.close()
        return self.nc


MODC = 9 * D // NCORE
CONV_PIECE = 4096


def build_prep(conv_len):
    C = Ctx("prep")
    P = C.P
    cst = C.din("cst", [128, KC, 3])
    mw = C.din("mw", [2, KC, 128, MODC])
    mb = C.din("mb", [2, 3, MODC])
    wf = C.din("wf", [128, conv_len])
    m_out = C.dout("m_out", [2, 3, MODC])
    wb = C.dout("wb", [128, conv_len], BF16)

    ct, ctb = C.sb("ct", [128, KC, 3])
    st, stb = C.sb("st", [128, KC, 3])
    P.load(ct[:], cst, ctb)
    P.act(st[:], ct[:], AF.Sigmoid, [ctb], [stb])
    P.tt(DVE, st[:], st[:], ct[:], ALU.mult, [stb, ctb], [stb])
    nslot = 3
    wt = [C.sb("mwt%d" % i, [128, MODC]) for i in range(nslot)]
    bt, btb = C.sb("bt", [3, 2, MODC])
    P.load(bt[:], mb.rearrange("l r c -> r l c"), btb)
    ot, otb = C.sb("ot", [3, 2, MODC])
    pss = [C.ps("mps%d" % i, [128, 512]) for i in range(5)]
    mob = Buf("m_out")
    cols = [(i * 512, min(512, MODC - i * 512)) for i in range(5)]
    it = 0
    for l in range(2):
        for kc in range(KC):
            t, tb = wt[it % nslot]
            it += 1
            P.load(t[:], mw[l, kc], tb)
            for i, (c0, cn) in enumerate(cols):
                P.mm(pss[i][0][0:3, 0:cn], st[:, kc, :], t[:, c0:c0 + cn], kc == 0, kc == KC - 1,
                     [stb, tb], [pss[i][1]])
        for i, (c0, cn) in enumerate(cols):
            P.tt(DVE, ot[:, l, c0:c0 + cn], pss[i][0][0:3, 0:cn], bt[:, l, c0:c0 + cn], ALU.add,
                 [pss[i][1], btb], [otb])
    P.store(m_out.rearrange("l r c -> r l c"), ot[:], otb, eng=SP)
    npiece = conv_len // CONV_PIECE
    cv = [C.sb("cv%d" % i, [128, CONV_PIECE], BF16) for i in range(4)]
    wbb = Buf("wb")
    for i in range(npiece):
        t, tb = cv[i % 4]
        sl = slice(i * CONV_PIECE, (i + 1) * CONV_PIECE)
        P.load(t[:].rearrange("p (a b) -> p a b", b=2048), wf[:, sl].rearrange("p (a b) -> p a b", b=2048), tb,
               eng=POOL)
        P.store(wb[:, sl], t[:], tb, eng=SP)
    return C.done()


NT_CORE = 64 + 4096
BLOCKS = [(0, 64, 0)] + [(64 + i * 512, 512, 1) for i in range(8)]


def build_token(stages):
    C = Ctx("token")
    P = C.P
    nc = C.nc
    zT = C.din("zT", [KC, 128, NT_CORE])
    modv = C.din("modv", [128, 2, 2, 9, KC])
    normg = C.din("normg", [128, 2, 3, KC])
    z_out = C.dout("z_out", [KC, 128, NT_CORE])
    wts = {}
    for s in stages:
        if s[0] == "ffn":
            _, l, i = s
            wts[s] = (C.din("wg_%d_%d" % (l, i), [22, 128, KC, 256], BF16),
                      C.din("wu_%d_%d" % (l, i), [22, 128, KC, 256], BF16),
                      C.din("wd_%d_%d" % (l, i), [KC, 128, FC, 128], BF16))
        elif s[0] == "inproj":
            wts[s] = (C.din("win_%d" % s[1], [22, 128, KC, 256], BF16),
                      C.dout("pT_%d" % s[1], [FC, 128, NT_CORE]))
        elif s[0] == "outproj":
            wts[s] = (C.din("wout_%d" % s[1], [KC, 128, KC, 128], BF16),
                      C.din("oT_%d" % s[1], [KC, 128, NT_CORE]),
                      C.din("zgT_%d" % s[1], [4, 128, NT_CORE]),
                      C.din("ssdg_%d" % s[1], [128, 4]),
                      C.din("ybT_%d" % s[1], [4, 128, NT_CORE]))

    ones, onesb = C.sb("ones", [128, 128])
    P.op(DVE, lambda e: e.memset(ones[:], 1.0), [], [onesb])
    epsT, epsb = C.sb("epsT", [128, 1])
    P.op(DVE, lambda e: e.memset(epsT[:], EPS), [], [epsb])
    mv, mvb = C.sb("mv", [128, 2, 2, 9, KC])
    P.load(mv[:], modv, mvb)
    ng, ngb = C.sb("ng", [128, 2, 3, KC])
    P.load(ng[:], normg, ngb)
    gs, gsb = C.sb("gs", [128, 2, 2, 3, KC])
    hg, hgb = C.sb("hg", [128, 2, 2, 3, KC])
    for l in range(2):
        for g in range(2):
            for j in range(3):
                P.ts(DVE, gs[:, l, g, j, :], mv[:, l, g, 3 * j + 1, :], 1.0, None, ALU.add, None, [mvb], [gsb])
                P.tt(DVE, gs[:, l, g, j, :], gs[:, l, g, j, :], ng[:, l, j, :], ALU.mult, [gsb, ngb], [gsb])
                P.ts(DVE, hg[:, l, g, j, :], mv[:, l, g, 3 * j + 2, :], (1.0 if j == 1 else 0.5), None,
                     ALU.mult, None, [mvb], [hgb])
    sg4 = None
    for s in stages:
        if s[0] == "outproj":
            sg4 = C.sb("sg4_%d" % s[1], [128, 4])
            P.load(sg4[0][:], wts[s][3], sg4[1])

    zt, ztb = C.sb("zt", [128, KC, 512])
    ht, htb = C.sb("ht", [128, KC, 512], BF16)
    actt, actb = C.sb("actt", [128, FC, 512], BF16)
    tmp = [C.sb("tmp%d" % i, [128, 512]) for i in range(3)]
    rstd, rstdb = C.sb("rstd", [128, 512])
    wgs = [C.sb("wgs%d" % i, [128, KC, 256], BF16) for i in range(2)]
    wus = [C.sb("wus%d" % i, [128, KC, 256], BF16) for i in range(2)]
    wds = [C.sb("wds%d" % i, [128, FC, 128], BF16) for i in range(2)]
    stg = [C.sb("stg%d" % i, [128, 512]) for i in range(3)]
    psG = [C.ps("psG%d" % i) for i in range(2)]
    psU = [C.ps("psU%d" % i) for i in range(2)]
    psD = [C.ps("psD%d" % i) for i in range(2)]
    psS = C.ps("psS")
    zob = Buf("z_out")
    cnt = {"w": 0, "d": 0, "t": 0, "s": 0, "g": 0}

    def adaln(l, j, lat, T):
        for kc in range(KC):
            t, tb = tmp[cnt["t"] % 3]
            cnt["t"] += 1
            P.act(t[:, :T], zt[:, kc, :T], AF.Square, [ztb], [tb])
            P.mm(psS[0][:, :T], ones[:], t[:, :T], kc == 0, kc == KC - 1, [onesb, tb], [psS[1]])
        P.act(rstd[:, :T], psS[0][:, :T], AF.Sqrt, [psS[1], epsb], [rstdb], scale=1.0 / D, bias=epsT[:])
        P.op(DVE, lambda e: e.reciprocal(out=rstd[:, :T], in_=rstd[:, :T]), [rstdb], [rstdb])
        for kc in range(KC):
            t, tb = tmp[cnt["t"] % 3]
            cnt["t"] += 1
            P.tt(DVE, t[:, :T], zt[:, kc, :T], rstd[:, :T], ALU.mult, [ztb, rstdb], [tb])
            if kc % 2 == 0:
                P.op(ACT, lambda e, t=t, kc=kc: e.activation(out=ht[:, kc, :T], in_=t[:, :T], func=AF.Identity,
                                                              scale=gs[:, l, lat, j, kc:kc + 1],
                                                              bias=mv[:, l, lat, 3 * j, kc:kc + 1]),
                     [tb, gsb, mvb], [htb])
            else:
                P.ts(DVE, ht[:, kc, :T], t[:, :T], gs[:, l, lat, j, kc:kc + 1], mv[:, l, lat, 3 * j, kc:kc + 1],
                     ALU.mult, ALU.add, [tb, gsb, mvb], [htb])

    def proj_T(w_ap, nchunk_out, nk, rhs_t, rhs_b, T, consume, slots):
        for oc in range(nchunk_out):
            wt_, wb_ = slots[cnt["d"] % 2]
            cnt["d"] += 1
            P.load(wt_[:, :nk, :], w_ap[oc], wb_)
            ps_, pb_ = psD[oc % 2]
            for k in range(nk):
                P.mm(ps_[:, :T], wt_[:, k, :], rhs_t[:, k, :T], k == 0, k == nk - 1, [wb_, rhs_b], [pb_])
            consume(oc, ps_, pb_)

    def ffn(l, i, lat, T):
        j = 0 if i == 0 else 2
        wg, wu, wd = wts[("ffn", l, i)]
        adaln(l, j, lat, T)
        for g in range(22):
            a, ab = wgs[g % 2]
            u, ub = wus[g % 2]
            P.load(a[:], wg[g], ab)
            P.load(u[:], wu[g], ub)
            for h in range(2):
                fc = g * 2 + h
                pg, pgb = psG[fc % 2]
                pu, pub = psU[fc % 2]
                for k in range(KC):
                    P.mm(pg[:, :T], a[:, k, h * 128:(h + 1) * 128], ht[:, k, :T], k == 0, k == KC - 1,
                         [ab, htb], [pgb])
                for k in range(KC):
                    P.mm(pu[:, :T], u[:, k, h * 128:(h + 1) * 128], ht[:, k, :T], k == 0, k == KC - 1,
                         [ub, htb], [pub])
                t, tb = tmp[cnt["t"] % 3]
                cnt["t"] += 1
                P.act(t[:, :T], pg[:, :T], AF.Silu, [pgb], [tb])
                P.tt(DVE, actt[:, fc, :T], t[:, :T], pu[:, :T], ALU.mult, [tb, pub], [actb])

        def consume(oc, ps_, pb_):
            P.stt(zt[:, oc, :T], ps_[:, :T], hg[:, l, lat, j, oc:oc + 1], zt[:, oc, :T], ALU.mult, ALU.add,
                  [pb_, hgb, ztb], [ztb])
        proj_T(wd, KC, FC, actt, actb, T, consume, wds)

    def inproj(l, lat, T, t0):
        win, pT = wts[("inproj", l)]
        adaln(l, 1, lat, T)
        pTb = pbufs[l]
        for g in range(22):
            a, ab = wgs[g % 2]
            P.load(a[:], win[g], ab)
            for h in range(2):
                oc = g * 2 + h
                ps_, pb_ = psD[oc % 2]
                for k in range(KC):
                    P.mm(ps_[:, :T], a[:, k, h * 128:(h + 1) * 128], ht[:, k, :T], k == 0, k == KC - 1,
                         [ab, htb], [pb_])
                s_, sb_ = stg[cnt["s"] % 3]
                cnt["s"] += 1
                if oc % 2 == 0:
                    P.act(s_[:, :T], ps_[:, :T], AF.Copy, [pb_], [sb_])
                else:
                    P.op(DVE, lambda e, s_=s_, ps_=ps_: e.tensor_copy(out=s_[:, :T], in_=ps_[:, :T]), [pb_], [sb_])
                P.store(pT[oc, :, t0:t0 + T], s_[:, :T], sb_)

    def outproj(l, lat, T, t0):
        wout, oT, zgT, _, ybT = wts[("outproj", l)]
        ssd_t = []
        for c4 in range(4):
            y_, yb_ = stg[cnt["s"] % 3]
            cnt["s"] += 1
            g_, gb_ = tmp[cnt["t"] % 3]
            cnt["t"] += 1
            P.load(y_[:, :T], oT[12 + c4, :, t0:t0 + T], yb_)
            P.load(g_[:, :T], zgT[c4, :, t0:t0 + T], gb_)
            y2_, y2b_ = stg[cnt["s"] % 3]
            cnt["s"] += 1
            P.load(y2_[:, :T], ybT[c4, :, t0:t0 + T], y2b_)
            P.tt(DVE, y_[:, :T], y_[:, :T], y2_[:, :T], ALU.add, [yb_, y2b_], [yb_])
            P.act(g_[:, :T], g_[:, :T], AF.Silu, [gb_], [gb_])
            P.tt(DVE, zsc[0][:, c4, :T], g_[:, :T], y_[:, :T], ALU.mult, [gb_, yb_], [zsc[1]])
            P.act(y_[:, :T], zsc[0][:, c4, :T], AF.Square, [zsc[1]], [yb_])
            P.mm(psS[0][:, :T], ones[:], y_[:, :T], c4 == 0, c4 == 3, [onesb, yb_], [psS[1]])
        P.act(rstd[:, :T], psS[0][:, :T], AF.Sqrt, [psS[1], epsb], [rstdb], scale=1.0 / 512, bias=epsT[:])
        P.op(DVE, lambda e: e.reciprocal(out=rstd[:, :T], in_=rstd[:, :T]), [rstdb], [rstdb])
        for c4 in range(4):
            P.stt(ht[:, 12 + c4, :T], zsc[0][:, c4, :T], sg4[0][:, c4:c4 + 1], rstd[:, :T], ALU.mult, ALU.mult,
                  [zsc[1], sg4[1], rstdb], [htb])
        for kc in range(12):
            y_, yb_ = stg[cnt["s"] % 3]
            cnt["s"] += 1
            P.load(y_[:, :T], oT[kc, :, t0:t0 + T], yb_)
            if kc % 2 == 0:
                P.act(ht[:, kc, :T], y_[:, :T], AF.Copy, [yb_], [htb])
            else:
                P.op(DVE, lambda e, y_=y_, kc=kc: e.tensor_copy(out=ht[:, kc, :T], in_=y_[:, :T]), [yb_], [htb])

        def consume(oc, ps_, pb_):
            P.stt(zt[:, oc, :T], ps_[:, :T], hg[:, l, lat, 1, oc:oc + 1], zt[:, oc, :T], ALU.mult, ALU.add,
                  [pb_, hgb, ztb], [ztb])
        proj_T(wout, KC, KC, ht, htb, T, consume, wds)

    zsc = None
    if any(s[0] == "outproj" for s in stages):
        zsc = C.sb("zsc", [128, 4, 512])
    pbufs = {s[1]: Buf("pT%d" % s[1]) for s in stages if s[0] == "inproj"}

    for (t0, T, lat) in BLOCKS:
        P.load(zt[:, :, :T], zT[:, :, t0:t0 + T].rearrange("k p t -> p k t"), ztb)
        for s in stages:
            if s[0] == "ffn":
                ffn(s[1], s[2], lat, T)
            elif s[0] == "inproj":
                inproj(s[1], lat, T, t0)
            elif s[0] == "outproj":
                outproj(s[1], lat, T, t0)
        P.store(z_out[:, :, t0:t0 + T].rearrange("k p t -> p k t"), zt[:, :, :T], ztb)
    print("token program", stages, "ninstr", P.ninstr, "nsem", P.nsem)
    return C.done()


def lay_w22(w):
    k, n = w.shape
    if n < 22 * 256:
        w = np.concatenate([w, np.zeros((k, 22 * 256 - n), w.dtype)], axis=1)
    return np.ascontiguousarray(w.reshape(KC, 128, 22, 256).transpose(2, 1, 0, 3))


def lay_wd(w):
    nf = w.shape[0] // 128
    return np.ascontiguousarray(w.reshape(nf, 128, KC, 128).transpose(2, 1, 0, 3))


def lay_vec(v):
    lead = v.shape[:-1]
    a = v.reshape(lead + (KC, 128))
    a = np.moveaxis(a, -1, 0)
    return np.ascontiguousarray(a)


def fm(zt_tokens_major):
    t, f = zt_tokens_major.shape
    return np.ascontiguousarray(zt_tokens_major.T.reshape(f // 128, 128, t))


def core_tokens(c):
    q = c % 4
    return np.concatenate([np.arange(q * 64, (q + 1) * 64), CTX + np.arange(q * 4096, (q + 1) * 4096)])


NKT = NTOK // 128
TBLK = [(0, 256)] + [(256 + 512 * i, 512) for i in range(32)]


def build_attn(parts=('dk', 'dq', 'mk', 'mq'), nblk=33, dbg=9):
    C = Ctx("attn")
    P = C.P
    nc = C.nc
    dq = C.din("dq", [128, NTOK]); dk = C.din("dk", [128, NTOK]); dv = C.din("dv", [NTOK, 128])
    cosd = C.din("cosd", [128, NTOK]); sind = C.din("sind", [128, NTOK])
    cosm = C.din("cosm", [96, NTOK]); sinm = C.din("sinm", [96, NTOK])
    cq = C.din("cq", [3, 128, NTOK]); ckv = C.din("ckv", [128, NTOK]); kr = C.din("kr", [32, NTOK])
    wuq = C.din("wuq", [128, 3, 192]); wukv = C.din("wukv", [128, 256])
    vecs = C.din("vecs", [128, 16])
    lamv = C.din("lamv", [64, 4])
    cmat = C.din("cmat", [4, 128, 128])
    o_diff = C.dout("o_diff", [128, NTOK]); o_mla = C.dout("o_mla", [128, NTOK])

    ones, onesb = C.sb("ones", [128, 128])
    P.op(DVE, lambda e: e.memset(ones[:], 1.0), [], [onesb])
    onesh, oneshb = C.sb("onesh", [128, 128], BF16)
    P.op(DVE, lambda e: e.memset(onesh[:], 1.0), [], [oneshb])
    epsT, epsb = C.sb("epsT", [128, 1])
    P.op(DVE, lambda e: e.memset(epsT[:], EPS), [], [epsb])
    vc, vcb = C.sb("vc", [128, 16])
    P.load(vc[:], vecs, vcb)
    cm, cmb = C.sb("cm", [128, 4, 128])
    P.load(cm[:], cmat.rearrange("m p c -> p m c"), cmb)
    lv, lvb = C.sb("lv", [64, 4])
    P.load(lv[:], lamv, lvb)
    wq_f, wq_fb = C.sb("wq_f", [128, 3, 192]); wq, wqb = C.sb("wq", [128, 3, 192], BF16)
    wkv_f, wkv_fb = C.sb("wkv_f", [128, 256]); wkv, wkvb = C.sb("wkv", [128, 256], BF16)
    P.load(wq_f[:], wuq, wq_fb); P.load(wkv_f[:], wukv, wkv_fb)
    P.op(DVE, lambda e: e.tensor_copy(out=wq[:], in_=wq_f[:]), [wq_fb], [wqb])
    P.op(DVE, lambda e: e.tensor_copy(out=wkv[:], in_=wkv_f[:]), [wkv_fb], [wkvb])

    psS = [C.ps("psS%d" % i) for i in range(2)]
    psO = [C.ps("psO%d" % i) for i in range(2)]
    psL = [C.ps("psL%d" % i) for i in range(2)]
    psE = C.ps("psE")
    psX = C.ps("psX")
    NTMP = 8
    tmps = [C.sb("atmp%d" % i, [128, 512]) for i in range(NTMP)]
    cnt = {"t": 0, "p": 0}

    def tmp():
        t = tmps[cnt["t"] % NTMP]
        cnt["t"] += 1
        return t

    l2, l2b = C.sb("l2", [64, 2])
    P.tt(DVE, l2[:, 0:1], lv[:, 0:1], lv[:, 1:2], ALU.mult, [lvb], [l2b])
    P.tt(DVE, l2[:, 1:2], lv[:, 2:3], lv[:, 3:4], ALU.mult, [lvb, l2b], [l2b])
    P.mm(psE[0][:, 0:2], ones[0:64, :], l2[:], True, True, [onesb, l2b], [psE[1]])
    le, leb = C.sb("le", [128, 2])
    P.act(le[:], psE[0][:, 0:2], AF.Exp, [psE[1]], [leb])
    nlam, nlamb = C.sb("nlam", [128, 1])
    P.tt(DVE, nlam[:], le[:, 1:2], le[:, 0:1], ALU.subtract, [leb], [nlamb])
    P.tt(DVE, nlam[:], nlam[:], vc[:, 3:4], ALU.subtract, [nlamb, vcb], [nlamb])
    sgd, sgdb = C.sb("sgd", [128, 1])
    P.tt(DVE, sgd[:], vc[:, 2:3], vc[:, 4:5], ALU.mult, [vcb], [sgdb])

    def norm_rope(src, srcb, R, T, gi, ri, gain_col, cos_ap, sin_ap, out_ap, outb):
        sq, sqb = tmp()
        P.act(sq[:R, :T], src, AF.Square, [srcb], [sqb])
        P.mm(psE[0][:R, :T], cm[:R, gi, :R], sq[:R, :T], True, True, [cmb, sqb], [psE[1]])
        rt, rtb = tmp()
        P.act(rt[:R, :T], psE[0][:R, :T], AF.Sqrt, [psE[1], epsb], [rtb], bias=epsT[:R, :])
        P.op(DVE, lambda e: e.reciprocal(out=rt[:R, :T], in_=rt[:R, :T]), [rtb], [rtb])
        xn, xnb = tmp()
        P.stt(xn[:R, :T], src, vc[:R, gain_col:gain_col + 1], rt[:R, :T], ALU.mult, ALU.mult,
              [srcb, vcb, rtb], [xnb])
        P.mm(psX[0][:R, :T], cm[:R, ri, :R], xn[:R, :T], True, True, [cmb, xnb], [psX[1]])
        ct, ctb = tmp(); st, stb = tmp()
        P.load(ct[:R, :T], cos_ap, ctb)
        P.load(st[:R, :T], sin_ap, stb)
        P.tt(DVE, ct[:R, :T], xn[:R, :T], ct[:R, :T], ALU.mult, [xnb, ctb], [ctb])
        P.tt(DVE, st[:R, :T], psX[0][:R, :T], st[:R, :T], ALU.mult, [psX[1], stb], [stb])
        P.tt(POOL, out_ap, ct[:R, :T], st[:R, :T], ALU.add, [ctb, stb], [outb])

    def attention(streams, q0, T, nkt, epilogue):
        pts = {}
        last = nkt - 1

        def pv(k):
            for si, s in enumerate(streams):
                pt, ptb = pts[(si, k)]
                for (po, pob, lf, lb) in s["pv"]:
                    P.mm(po[:, :T], lf(k), pt[:, :T], k == 0, k == last, [lb, ptb], [pob])
        sb2 = [[psS[0], psE], [psS[1], psX]]
        for k in range(nkt):
            for si, s in enumerate(streams):
                r0, r1 = s["rows"]
                sbk, sbkb = sb2[si][k % 2]
                P.mm(sbk[:, :T], s["KT"][r0:r1, k * 128:(k + 1) * 128], s["QT"][r0:r1, :T], True, True,
                     [s["KTb"], s["QTb"]], [sbkb])
            for si, s in enumerate(streams):
                pt, ptb = ptiles[si][k % 3]
                pts[(si, k)] = (pt, ptb)
                sbk, sbkb = sb2[si][k % 2]
                P.act(pt[:, :T], sbk[:, :T], AF.Exp, [sbkb], [ptb], scale=s["scale"])
            if k > 0:
                pv(k - 1)
        pv(last)
        epilogue(q0, T)

    ptiles = [[C.sb("pt%d_%d" % (si, j), [128, 512], BF16) for j in range(3)] for si in range(2)]

    with contextlib.ExitStack() as es:
        KT, KTb = C.sb("dKT", [128, NTOK], BF16, es)
        V, Vb = C.sb("dV", [128, NKT, 128], BF16, es)
        QTs = [C.sb("dQT%d" % i, [128, 512], BF16, es) for i in range(2)]
        srcs = [C.sb("dsrc%d" % i, [128, 512], F32, es) for i in range(2)]
        dvr = dv.rearrange("(t p) c -> p t c", p=128)
        for i in range(0, NKT, 13):
            P.load(V[:, i:i + 13, :], dvr[:, i:i + 13, :], Vb, reads=[], writes=[Vb], eng=POOL)
        for bi, (t0, T) in enumerate(TBLK[:nblk] if 'dk' in parts else []):
            s, sb_ = srcs[bi % 2]
            P.load(s[:, :T], dk[:, t0:t0 + T], sb_)
            norm_rope(s[:, :T], sb_, 128, T, 0, 1, 1, cosd[:, t0:t0 + T], sind[:, t0:t0 + T], KT[:, t0:t0 + T], KTb)

        def epi_diff(q0, T):
            r0, r0b = tmp(); r1, r1b = tmp()
            P.op(DVE, lambda e: e.reciprocal(out=r0[:, :T], in_=psL[0][0][:, :T]), [psL[0][1]], [r0b])
            P.op(DVE, lambda e: e.reciprocal(out=r1[:, :T], in_=psL[1][0][:, :T]), [psL[1][1]], [r1b])
            P.tt(DVE, r0[:, :T], psO[0][0][:, :T], r0[:, :T], ALU.mult, [psO[0][1], r0b], [r0b])
            P.tt(DVE, r1[:, :T], psO[1][0][:, :T], r1[:, :T], ALU.mult, [psO[1][1], r1b], [r1b])
            o, ob = tmp()
            P.stt(o[:, :T], r1[:, :T], nlam[:, 0:1], r0[:, :T], ALU.mult, ALU.add, [r1b, nlamb, r0b], [ob])
            sq, sqb = tmp()
            P.act(sq[:, :T], o[:, :T], AF.Square, [ob], [sqb])
            P.mm(psE[0][:, :T], ones[:], sq[:, :T], True, True, [onesb, sqb], [psE[1]])
            rt, rtb = tmp()
            P.act(rt[:, :T], psE[0][:, :T], AF.Sqrt, [psE[1], epsb], [rtb], scale=1.0 / 128, bias=epsT[:])
            P.op(DVE, lambda e: e.reciprocal(out=rt[:, :T], in_=rt[:, :T]), [rtb], [rtb])
            P.stt(o[:, :T], o[:, :T], sgd[:, 0:1], rt[:, :T], ALU.mult, ALU.mult, [ob, sgdb, rtb], [ob])
            P.store(o_diff[:, q0:q0 + T], o[:, :T], ob)

        for bi, (q0, T) in enumerate(TBLK[:nblk] if 'dq' in parts else []):
            s, sb_ = srcs[bi % 2]
            qt, qtb = QTs[bi % 2]
            P.load(s[:, :T], dq[:, q0:q0 + T], sb_)
            norm_rope(s[:, :T], sb_, 128, T, 0, 1, 0, cosd[:, q0:q0 + T], sind[:, q0:q0 + T], qt[:, :T], qtb)
            streams = []
            for m in range(2):
                streams.append(dict(KT=KT, KTb=KTb, rows=(m * 64, (m + 1) * 64), QT=qt, QTb=qtb, scale=0.125,
                                    pv=[(psO[m][0], psO[m][1], (lambda k: V[:, k, :]), Vb),
                                        (psL[m][0], psL[m][1], (lambda k: onesh[:]), oneshb)]))
            attention(streams, q0, T, 2 if bi == 0 else NKT, epi_diff)

    P.barrier()
    with contextlib.ExitStack() as es:
        KTs = [C.sb("mKT%d" % i, [96, NTOK], BF16, es) for i in range(2)]
        Vall, Vallb = C.sb("mV", [128, 2, NKT, 128], BF16, es)
        Vs = [(Vall[:, i], Vallb) for i in range(2)]
        QTs = [[C.sb("mQT%d_%d" % (h, i), [96, 512], BF16, es) for i in range(2)] for h in range(2)]
        cqt = [C.sb("cqt%d" % i, [128, 3, 512], F32, es) for i in range(2)]
        cqn, cqnb = C.sb("cqn", [128, 3, 512], BF16, es)
        ckn, cknb = C.sb("ckn", [128, 512], BF16, es)
        for h in range(2):
            P.op(POOL, lambda e, h=h: e.memset(Vall[:, h, :, 64:128], 1.0), [], [Vallb])
        for bi, (t0, T) in enumerate(TBLK[:nblk] if 'mk' in parts else []):
            c_, cb_ = tmp()
            P.load(c_[:, :T], ckv[:, t0:t0 + T], cb_)
            sq, sqb = tmp()
            P.act(sq[:, :T], c_[:, :T], AF.Square, [cb_], [sqb])
            P.mm(psE[0][:, :T], ones[:], sq[:, :T], True, True, [onesb, sqb], [psE[1]])
            rt, rtb = tmp()
            P.act(rt[:, :T], psE[0][:, :T], AF.Sqrt, [psE[1], epsb], [rtb], scale=1.0 / 128, bias=epsT[:])
            P.op(DVE, lambda e, rt=rt, T=T: e.reciprocal(out=rt[:, :T], in_=rt[:, :T]), [rtb], [rtb])
            P.stt(ckn[:, :T], c_[:, :T], vc[:, 8:9], rt[:, :T], ALU.mult, ALU.mult, [cb_, vcb, rtb], [cknb])
            for h in range(2 if dbg >= 2 else 0):
                s, sb_ = tmp()
                P.mm(psX[0][0:64, :T], wkv[:, h * 128:h * 128 + 64], ckn[:, :T], True, True, [wkvb, cknb], [psX[1]])
                P.load(s[64:96, :T], kr[:, t0:t0 + T], sb_)
                P.op(DVE, lambda e, s=s, T=T: e.tensor_copy(out=s[0:64, :T], in_=psX[0][0:64, :T]), [psX[1]], [sb_])
                if dbg >= 3:
                    norm_rope(s[0:96, :T], sb_, 96, T, 2, 3, 10, cosm[:, t0:t0 + T], sinm[:, t0:t0 + T],
                              KTs[h][0][:, t0:t0 + T], KTs[h][1])
            for sub in range(T // 128 if dbg >= 4 else 0):
                tix = t0 // 128 + sub
                P.mm(psX[0][:, 0:256], ckn[:, sub * 128:(sub + 1) * 128], wkv[:, :], True, True, [cknb, wkvb], [psX[1]])
                if dbg >= 5:
                    P.act(Vall[:, :, tix, 0:64], psX[0][:, 0:256].rearrange("p (h c) -> p h c", h=2)[:, :, 64:128],
                          AF.Copy, [psX[1]], [Vallb])

        osb = [C.sb("mo%d" % i, [128, 512], F32, es) for i in range(2)]
        ocnt = [0]

        def epi_mla(q0, T):
            o, ob = osb[ocnt[0] % 2]
            ocnt[0] += 1
            for h in range(2):
                r, rb = tmp()
                P.op(DVE, lambda e, r=r, h=h: e.reciprocal(out=r[0:64, :T], in_=psO[h][0][64:128, :T]), [psO[h][1]], [rb])
                P.tt(DVE, o[h * 64:(h + 1) * 64, :T], psO[h][0][0:64, :T], r[0:64, :T], ALU.mult, [psO[h][1], rb], [ob])
            P.store(o_mla[:, q0:q0 + T], o[:, :T], ob)

        for bi, (q0, T) in enumerate(TBLK[:nblk] if 'mq' in parts else []):
            c_, cb_ = cqt[bi % 2]
            P.load(c_[:, :, :T], cq[:, :, q0:q0 + T].rearrange("k p t -> p k t"), cb_)
            for kc in range(3):
                sq, sqb = tmp()
                P.act(sq[:, :T], c_[:, kc, :T], AF.Square, [cb_], [sqb])
                P.mm(psE[0][:, :T], ones[:], sq[:, :T], kc == 0, kc == 2, [onesb, sqb], [psE[1]])
            rt, rtb = tmp()
            P.act(rt[:, :T], psE[0][:, :T], AF.Sqrt, [psE[1], epsb], [rtb], scale=1.0 / 384, bias=epsT[:])
            P.op(DVE, lambda e, rt=rt, T=T: e.reciprocal(out=rt[:, :T], in_=rt[:, :T]), [rtb], [rtb])
            for kc in range(3):
                P.stt(cqn[:, kc, :T], c_[:, kc, :T], vc[:, 5 + kc:6 + kc], rt[:, :T], ALU.mult, ALU.mult,
                      [cb_, vcb, rtb], [cqnb])
            streams = []
            for h in range(2):
                for kc in range(3):
                    P.mm(psX[0][0:96, :T], wq[:, kc, h * 96:(h + 1) * 96], cqn[:, kc, :T], kc == 0, kc == 2,
                         [wqb, cqnb], [psX[1]])
                s, sb_ = tmp()
                P.op(DVE, lambda e, s=s, T=T: e.tensor_copy(out=s[0:96, :T], in_=psX[0][0:96, :T]), [psX[1]], [sb_])
                qt, qtb = QTs[h][bi % 2]
                norm_rope(s[0:96, :T], sb_, 96, T, 2, 3, 9, cosm[:, q0:q0 + T], sinm[:, q0:q0 + T], qt[:, :T], qtb)
                streams.append(dict(KT=KTs[h][0], KTb=KTs[h][1], rows=(0, 96), QT=qt, QTb=qtb,
                                    scale=96.0 ** -0.5,
                                    pv=[(psO[h][0], psO[h][1], (lambda k, h=h: Vall[:, h, k, :]), Vallb)]))
            attention(streams, q0, T, 2 if bi == 0 else NKT, epi_mla)
    print("attn program ninstr", P.ninstr, "nsem", P.nsem)
    return C.done()


OFF = dict(dq=0, dk=512, dv=1024, rw=1536, mcq=3456, mckv=3840, mkr=3968, sz=4000, sxbc=4512, sdt=5536)


def rope_tables(dim):
    rows = SEQ // 64
    quarter = dim // 4
    inv = (np.float32(10000.0) ** (-np.arange(quarter, dtype=np.float32) / np.float32(quarter))).astype(np.float32)
    pos_r = np.repeat(np.arange(rows), 64).astype(np.float32)
    pos_c = np.tile(np.arange(64), rows).astype(np.float32)
    ang = np.concatenate([pos_r[:, None] * inv, pos_c[:, None] * inv], axis=-1).astype(np.float32)
    return np.cos(ang).astype(np.float32), np.sin(ang).astype(np.float32)


_CONST = {}


def attn_consts():
    if "a" in _CONST:
        return _CONST["a"]
    cd, sd = rope_tables(64)
    cosd = np.ones((128, NTOK), np.float32); sind = np.zeros((128, NTOK), np.float32)
    for r in range(128):
        cosd[r, CTX:] = cd[:, (r % 64) % 32]; sind[r, CTX:] = sd[:, (r % 64) % 32]
    c32, s32 = rope_tables(32)
    cosm = np.ones((96, NTOK), np.float32); sinm = np.zeros((96, NTOK), np.float32)
    for i in range(32):
        cosm[64 + i, CTX:] = c32[:, i % 16]; sinm[64 + i, CTX:] = s32[:, i % 16]
    cmat = np.zeros((4, 128, 128), np.float32)
    for blk in range(2):
        cmat[0, blk * 64:(blk + 1) * 64, blk * 64:(blk + 1) * 64] = 1.0 / 64
        for m in range(64):
            if m < 32:
                cmat[1, blk * 64 + m + 32, blk * 64 + m] = -1.0
            else:
                cmat[1, blk * 64 + m - 32, blk * 64 + m] = 1.0
    cmat[2, 0:64, 0:64] = 1.0 / 64
    cmat[2, 64:96, 64:96] = 1.0 / 32
    for i in range(32):
        if i < 16:
            cmat[3, 64 + i + 16, 64 + i] = -1.0
        else:
            cmat[3, 64 + i - 16, 64 + i] = 1.0
    _CONST["a"] = (cosd, sind, cosm, sinm, cmat)
    return _CONST["a"]


def attn_inputs(PT, inp, l, j):
    cosd, sind, cosm, sinm, cmat = attn_consts()
    r = lambda o, a, b: np.ascontiguousarray(PT[OFF[o] + a:OFF[o] + b])
    lam_init = 0.8 - 0.6 * math.exp(-0.3 * l)
    vecs = np.zeros((128, 16), np.float32)
    vecs[:, 0] = np.tile(inp["diff_qk_g"][l, 0], 2)
    vecs[:, 1] = np.tile(inp["diff_qk_g"][l, 1], 2)
    vecs[:, 2] = inp["diff_subln_g"][l]
    vecs[:, 3] = lam_init
    vecs[:, 4] = 1.0 - lam_init
    vecs[:, 5:8] = inp["mla_q_norm_g"][l].reshape(3, 128).T
    vecs[:, 8] = inp["mla_kv_norm_g"][l]
    vecs[:96, 9] = np.concatenate([inp["mla_nope_g"][l, 0], inp["mla_rope_g"][l, 0]])
    vecs[:96, 10] = np.concatenate([inp["mla_nope_g"][l, 1], inp["mla_rope_g"][l, 1]])
    wuq = inp["mla_w_uq"][l][:, 2 * j * 96:(2 * j + 2) * 96]
    return {
        "dq": r("dq", j * 128, (j + 1) * 128), "dk": r("dk", j * 128, (j + 1) * 128),
        "dv": np.ascontiguousarray(PT[OFF["dv"] + j * 128:OFF["dv"] + (j + 1) * 128].T),
        "cosd": cosd, "sind": sind, "cosm": cosm, "sinm": sinm,
        "cq": np.ascontiguousarray(PT[OFF["mcq"]:OFF["mcq"] + 384].reshape(3, 128, NTOK)),
        "ckv": r("mckv", 0, 128), "kr": r("mkr", 0, 32),
        "wuq": np.ascontiguousarray(wuq.reshape(3, 128, 192).transpose(1, 0, 2)),
        "wukv": np.ascontiguousarray(inp["mla_w_ukv"][l][:, 2 * j * 128:(2 * j + 2) * 128]),
        "vecs": vecs, "lamv": np.ascontiguousarray(inp["diff_lambda"][l].T), "cmat": cmat,
    }


FWD_ORDER = list(range(NKT))
BWD_ORDER = [1, 0] + list(range(NKT - 1, 1, -1))


def build_ssd(nchunks=NKT):
    C = Ctx("ssd")
    P = C.P
    xs = C.din("xs", [128, NTOK]); Bm = C.din("Bm", [128, NTOK]); Cm = C.din("Cm", [128, NTOK])
    cw = C.din("cw", [128, 3, 5]); cb = C.din("cb", [128, 3])
    dtr = C.din("dtr", [NTOK, 4])
    sv = C.din("sv", [128, 10])
    cmat = C.din("cmat", [8, 128, 128])
    y_f = C.dout("y_f", [NTOK, 128]); y_b = C.dout("y_b", [NTOK, 128])

    cm, cmb = C.sb("cm", [128, 8, 128])
    P.load(cm[:], cmat.rearrange("m p c -> p m c"), cmb)
    identb, identbb = C.sb("identb", [128, 128], BF16)
    P.op(DVE, lambda e: e.tensor_copy(out=identb[:], in_=cm[:, 0, :]), [cmb], [identbb])
    ones, onesb = C.sb("ones", [128, 128])
    P.op(DVE, lambda e: e.memset(ones[:], 1.0), [], [onesb])
    cwt, cwb = C.sb("cwt", [128, 3, 5]); cbt, cbb = C.sb("cbt", [128, 3]); svt, svb = C.sb("svt", [128, 10])
    P.load(cwt[:], cw, cwb); P.load(cbt[:], cb, cbb); P.load(svt[:], sv, svb)
    nexpA, nexpAb = C.sb("nexpA", [128, 4])
    P.act(nexpA[:], svt[:, 4:8], AF.Exp, [svb], [nexpAb])
    P.ts(DVE, nexpA[:], nexpA[:], -1.0, None, ALU.mult, None, [nexpAb], [nexpAb])

    x_tm, x_tmb = C.sb("x_tm", [128, NKT, 128])
    Bf, Bfb = C.sb("Bf", [128, NTOK], BF16)
    Cf, Cfb = C.sb("Cf", [128, NTOK], BF16)
    B_tm, B_tmb = C.sb("B_tm", [128, NKT, 128], BF16)
    dt, dtb = C.sb("dt", [128, NKT, 4])
    adt, adtb = C.sb("adt", [128, NKT, 4])
    P.load(dt[:], dtr.rearrange("(t p) c -> p t c", p=128), dtb)
    for col in range(4):
        P.act(dt[:, :, col], dt[:, :, col], AF.Exp, [dtb, svb], [dtb], bias=svt[:, col:col + 1])
    for col in range(4):
        P.act(dt[:, :, col], dt[:, :, col], AF.Ln, [dtb], [dtb], bias=ones[:, 0:1])
    for col in range(4):
        P.ts(DVE, adt[:, :, col], dt[:, :, col], nexpA[:, col:col + 1], None, ALU.mult, None, [dtb, nexpAb], [adtb])

    psT = C.ps("psT")
    psTb_t = C.es.enter_context(C.nc.psum_tensor("psTb", [128, 1024], BF16)); psTb = (psTb_t, Buf("psTb"))
    psSeg = [C.ps("psSeg%d" % i) for i in range(2)]
    psG = C.ps("psG"); psY = C.ps("psY"); psH = C.ps("psH"); psA = C.ps("psA")

    xin = [C.sb("xin%d" % i, [128, 3, 516]) for i in range(2)]
    acc = [C.sb("cacc%d" % i, [128, 512]) for i in range(3)]
    xc, xcb = C.sb("xc", [128, 512])
    srcs = [xs, Bm, Cm]
    for bi, (t0, T) in enumerate(TBLK):
        seg0, seg1 = (0, CTX) if t0 < CTX else (CTX, NTOK)
        lo, hi = max(seg0, t0 - 2), min(seg1, t0 + T + 2)
        xi, xib = xin[bi % 2]
        P.op(POOL, lambda e, xi=xi: e.memset(xi[:], 0.0), [], [xib])
        for i3 in range(3):
            P.load(xi[:, i3, lo - (t0 - 2):hi - (t0 - 2)], srcs[i3][:, lo:hi], xib)
        for i3 in range(3):
            a, ab = acc[i3]
            P.ts(DVE, a[:, :T], xi[:, i3, 0:T], cwt[:, i3, 0:1], None, ALU.mult, None, [xib, cwb], [ab])
            for i in range(1, 5):
                P.stt(a[:, :T], xi[:, i3, i:i + T], cwt[:, i3, i:i + 1], a[:, :T], ALU.mult, ALU.add,
                      [xib, cwb, ab], [ab])
            if i3 == 0:
                P.act(xc[:, :T], a[:, :T], AF.Silu, [ab, cbb], [xcb], bias=cbt[:, 0:1])
            elif i3 == 1:
                P.act(Bf[:, t0:t0 + T], a[:, :T], AF.Silu, [ab, cbb], [Bfb], bias=cbt[:, 1:2])
            else:
                P.act(Cf[:, t0:t0 + T], a[:, :T], AF.Silu, [ab, cbb], [Cfb], bias=cbt[:, 2:3])
        for sub in range(T // 128):
            ci = t0 // 128 + sub
            P.op(PE, lambda e, sub=sub: e.transpose(psT[0][:, 0:128], xc[:, sub * 128:(sub + 1) * 128], cm[:, 0, :]),
                 [xcb, cmb], [psT[1]], sync_same=False)
            P.op(DVE, lambda e, ci=ci: e.tensor_copy(out=x_tm[:, ci, :], in_=psT[0][:, 0:128]), [psT[1]], [x_tmb])
            P.op(PE, lambda e, ci=ci: e.transpose(psTb[0][:, 0:128], Bf[:, ci * 128:(ci + 1) * 128], identb[:]),
                 [Bfb, identbb], [psTb[1]], sync_same=False)
            P.act(B_tm[:, ci, :], psTb[0][:, 0:128], AF.Copy, [psTb[1]], [B_tmb])

    hT = [C.sb("hT%d" % h, [128, 64]) for h in range(2)]
    hTh = [C.sb("hTh%d" % h, [128, 64], BF16) for h in range(2)]
    NS = 3
    abc = [[C.sb("abc%d_%d" % (h, i), [128, 128]) for i in range(NS)] for h in range(2)]
    dec = [[C.sb("dec%d_%d" % (h, i), [128, 128]) for i in range(NS)] for h in range(2)]
    sct = [[C.sb("sct%d_%d" % (h, i), [128, 128], BF16) for i in range(NS)] for h in range(2)]
    bw = [[C.sb("bw%d_%d" % (h, i), [128, 128], BF16) for i in range(NS)] for h in range(2)]
    xdt = [[C.sb("xdt%d_%d" % (h, i), [128, 64], BF16) for i in range(NS)] for h in range(2)]
    sm = [C.sb("sm%d" % i, [128, 8]) for i in range(NS)]
    ysb = [C.sb("ysb%d" % i, [128, 128]) for i in range(NS)]
    yo = [C.sb("yo%d" % i, [128, 128]) for i in range(NS)]
    it = 0
    for d in range(2):
        order = (FWD_ORDER if d == 0 else BWD_ORDER)[:nchunks]
        Ui, nUi, Mi = (1, 2, 5) if d == 0 else (3, 4, 6)
        yout = y_f if d == 0 else y_b
        for h in range(2):
            P.op(DVE, lambda e, h=h: e.memset(hT[h][0][:], 0.0), [], [hT[h][1]])
            P.op(POOL, lambda e, h=h: e.memset(hTh[h][0][:], 0.0), [], [hTh[h][1]])
        for c in order:
            sl = it % NS
            it += 1
            cs = slice(c * 128, (c + 1) * 128)
            a2 = adt[:, c, 2 * d:2 * d + 2]
            P.mm(psA[0][:, 0:2], cm[:, Ui, :], a2, True, True, [cmb, adtb], [psA[1]])
            P.mm(psA[0][:, 2:4], ones[:], a2, True, True, [onesb, adtb], [psA[1]])
            s_, sb_ = sm[sl]
            P.act(s_[:, 0:4], psA[0][:, 0:4], AF.Exp, [psA[1]], [sb_])
            P.tt(DVE, s_[:, 6:8], psA[0][:, 2:4], psA[0][:, 0:2], ALU.subtract, [psA[1]], [sb_]) if False else None
            P.act(s_[:, 6:8], psA[0][:, 0:2], AF.Copy, [psA[1]], [sb_])
            P.tt(DVE, s_[:, 6:8], psA[0][:, 2:4], s_[:, 6:8], ALU.subtract, [psA[1], sb_], [sb_])
            P.act(s_[:, 4:6], s_[:, 6:8], AF.Exp, [sb_], [sb_])
            P.mm(psG[0][:, 0:128], Bf[:, cs], Cf[:, cs], True, True, [Bfb, Cfb], [psG[1]])
            for h in range(2):
                ab_, abb_ = abc[h][sl]
                P.ts(DVE, ab_[:], ones[:], adt[:, c, 2 * d + h:2 * d + h + 1], None, ALU.mult, None, [onesb, adtb], [abb_])
                sg, sgb = psSeg[h]
                P.mm(sg[:, 0:128], ab_[:], cm[:, Ui, :], True, False, [abb_, cmb], [sgb])
                P.mm(sg[:, 0:128], cm[:, nUi, :], ab_[:], False, False, [abb_, cmb], [sgb])
                P.mm(sg[:, 0:128], cm[:, 0, :], cm[:, Mi, :], False, True, [cmb], [sgb])
                de, deb = dec[h][sl]
                P.act(de[:], sg[:, 0:128], AF.Exp, [sgb], [deb])
                sc, scb = sct[h][sl]
                P.tt(DVE, sc[:], psG[0][:, 0:128], de[:], ALU.mult, [psG[1], deb], [scb])
                xd, xdb = xdt[h][sl]
                P.ts(POOL, xd[:], x_tm[:, c, h * 64:(h + 1) * 64], dt[:, c, 2 * d + h:2 * d + h + 1], None, ALU.mult, None,
                     [x_tmb, dtb], [xdb])
                bw_, bwb_ = bw[h][sl]
                P.ts(POOL, bw_[:], B_tm[:, c, :], s_[:, 4 + h:5 + h], None, ALU.mult, None, [B_tmb, sb_], [bwb_])
                P.mm(psY[0][:, h * 64:(h + 1) * 64], sc[:], xd[:], True, True, [scb, xdb], [psY[1]])
                P.mm(psY[0][:, 128 + h * 64:128 + (h + 1) * 64], Cf[:, cs], hTh[h][0][:], True, True,
                     [Cfb, hTh[h][1]], [psY[1]])
                P.mm(psH[0][:, h * 64:(h + 1) * 64], bw_[:], xd[:], True, True, [bwb_, xdb], [psH[1]])
            ys, ysb_ = ysb[sl]
            P.act(ys[:], psY[0][:, 0:128], AF.Copy, [psY[1]], [ysb_])
            yo_, yob_ = yo[sl]
            for h in range(2):
                hs = slice(h * 64, (h + 1) * 64)
                P.stt(yo_[:, hs], psY[0][:, 128 + h * 64:128 + (h + 1) * 64], s_[:, h:h + 1], ys[:, hs], ALU.mult, ALU.add,
                      [psY[1], sb_, ysb_], [yob_])
                if d == 0:
                    P.stt(yo_[:, hs], x_tm[:, c, hs], svt[:, 8 + h:9 + h], yo_[:, hs], ALU.mult, ALU.add,
                          [x_tmb, svb, yob_], [yob_])
                P.stt(hT[h][0][:], hT[h][0][:], s_[:, 2 + h:3 + h], psH[0][:, hs], ALU.mult, ALU.add,
                      [hT[h][1], sb_, psH[1]], [hT[h][1]])
                P.op(POOL, lambda e, h=h: e.tensor_copy(out=hTh[h][0][:], in_=hT[h][0][:]), [hT[h][1]], [hTh[h][1]])
            P.store(yout[cs, :], yo_[:], yob_, eng=SP)
    print("ssd program ninstr", P.ninstr, "nsem", P.nsem)
    return C.done()


def ssd_consts():
    if "s" in _CONST:
        return _CONST["s"]
    cmat = np.zeros((8, 128, 128), np.float32)
    idx = np.arange(128)
    cmat[0] = np.eye(128, dtype=np.float32)
    U = (idx[:, None] <= idx[None, :]).astype(np.float32)
    cmat[1] = U; cmat[2] = -U; cmat[3] = U.T; cmat[4] = -U.T
    cmat[5] = np.where(idx[None, :] >= idx[:, None], 0.0, -30000.0)
    cmat[6] = np.where(idx[None, :] <= idx[:, None], 0.0, -30000.0)
    _CONST["s"] = cmat
    return cmat


def ssd_inputs(PT, inp, l, j):
    gi = j // 2
    o = OFF["sxbc"]
    cols = [np.arange(2 * j * 64, (2 * j + 2) * 64), 512 + gi * 128 + np.arange(128), 768 + gi * 128 + np.arange(128)]
    cwv = np.stack([inp["ssd_conv_w"][l][:, cc].T for cc in cols], axis=1)
    cbv = np.stack([inp["ssd_conv_b"][l][cc] for cc in cols], axis=1)
    sv = np.zeros((128, 10), np.float32)
    hh = [2 * j, 2 * j + 1]
    for d in range(2):
        for h in range(2):
            sv[:, 2 * d + h] = inp["ssd_dt_bias"][l][d, hh[h]]
            sv[:, 4 + 2 * d + h] = inp["ssd_a_log"][l][d, hh[h]]
    sv[:, 8] = inp["ssd_d"][l][hh[0]]; sv[:, 9] = inp["ssd_d"][l][hh[1]]
    dcols = [OFF["sdt"] + d * 8 + hh[h] for d in range(2) for h in range(2)]
    return {
        "xs": np.ascontiguousarray(PT[o + cols[0]]), "Bm": np.ascontiguousarray(PT[o + cols[1]]),
        "Cm": np.ascontiguousarray(PT[o + cols[2]]),
        "cw": np.ascontiguousarray(cwv), "cb": np.ascontiguousarray(cbv),
        "dtr": np.ascontiguousarray(PT[dcols].T), "sv": sv, "cmat": ssd_consts(),
    }


RC = 64
DECAY_SCALE = 0.606531
LN_EPS = 64e-5


def rwkv_dir_blocks(d):
    out = []
    if d == 0:
        for (t0, T) in TBLK:
            out.append((t0, T, [o for o in range(0, T, RC)]))
    else:
        out.append((0, 256, [o for o in range(256 - RC, -1, -RC)]))
        for (t0, T) in reversed(TBLK[1:]):
            out.append((t0, T, [o for o in range(T - RC, -1, -RC)]))
    return out


def build_rwkv(nsteps=None, phases=(1, 2, 3), dbg=9, dbg_scr=False):
    C = Ctx("rwkv")
    P = C.P
    nc = C.nc
    names = ["r", "k", "v", "lrw", "lra", "gd"]
    src = {n: C.din("rw_" + n, [128, NTOK]) for n in names}
    cw = C.din("cw", [128, 6, 3])
    vec = C.din("vec", [128, 12])
    wup = C.din("wup", [128, 128]); aup = C.din("aup", [128, 128]); gup = C.din("gup", [128, 128])
    cmat = C.din("cmat", [3, 128, 128])
    msk = C.din("msk", [64, 2, 640])
    o_rwkv = C.dout("o_rwkv", [128, NTOK])
    snames = ["r", "v", "kk", "lw0", "lw1", "kt0", "kt1", "be0", "be1", "bonus", "g", "yf", "yb"]
    scr = {n: nc.dram_tensor("scr_" + n, [128, NTOK], F32, kind=("ExternalOutput" if dbg_scr else "Internal")).ap()
           for n in snames}
    scrb = {n: Buf("scr_" + n) for n in snames}

    cm, cmb = C.sb("cm", [128, 3, 128]); P.load(cm[:], cmat.rearrange("m p c -> p m c"), cmb)
    mk, mkb = C.sb("mk", [64, 2, 640]); P.load(mk[:], msk, mkb)
    vc, vcb = C.sb("vc", [128, 12]); P.load(vc[:], vec, vcb)
    P.ts(DVE, vc[:, 6:7], vc[:, 5:6], -1.0, 1.0, ALU.mult, ALU.add, [vcb], [vcb])
    cwt, cwb = C.sb("cwt", [128, 6, 3]); P.load(cwt[:], cw, cwb)
    wu, wub = C.sb("wu", [128, 128]); P.load(wu[:], wup, wub)
    au, aub = C.sb("au", [128, 128]); P.load(au[:], aup, aub)
    gu, gub = C.sb("gu", [128, 128]); P.load(gu[:], gup, gub)
    ones, onesb = C.sb("ones", [128, 64]); P.op(DVE, lambda e: e.memset(ones[:], 1.0), [], [onesb])
    lneps, lnepsb = C.sb("lneps", [128, 1]); P.op(DVE, lambda e: e.memset(lneps[:], LN_EPS), [], [lnepsb])
    idr, idrb = C.sb("idr", [64, 2, 64])
    for h in range(2):
        P.op(DVE, lambda e, h=h: e.tensor_copy(out=idr[:, h, :], in_=cm[0:64, 0, 0:64]), [cmb], [idrb])

    ps = [C.ps("rps%d" % i) for i in range(8)]

    if 1 in phases:
        with contextlib.ExitStack() as es:
            xin = [C.sb("rxin%d" % i, [128, 6, 514], F32, es) for i in range(2)]
            cv = {n: C.sb("rcv_" + n, [128, 512], F32, es) for n in names}
            NT1 = 10
            t1 = [C.sb("rt1_%d" % i, [128, 512], F32, es) for i in range(NT1)]
            c1 = [0]

            def tmp():
                t = t1[c1[0] % NT1]
                c1[0] += 1
                return t

            def put(name, t, tb, t0, T):
                tok = P.dma(POOL, lambda e: e.dma_start(out=scr[name][:, t0:t0 + T], in_=t[:, :T]), tb, [tb], [scrb[name]])
                return tok

            for bi, (t0, T) in enumerate(TBLK):
                seg0, seg1 = (0, CTX) if t0 < CTX else (CTX, NTOK)
                lo, hi = max(seg0, t0 - 1), min(seg1, t0 + T + 1)
                xi, xib = xin[bi % 2]
                P.op(POOL, lambda e, xi=xi: e.memset(xi[:], 0.0), [], [xib])
                for i6, n in enumerate(names):
                    P.load(xi[:, i6, lo - (t0 - 1):hi - (t0 - 1)], src[n][:, lo:hi], xib)
                for i6, n in enumerate(names):
                    a, ab = cv[n]
                    P.ts(DVE, a[:, :T], xi[:, i6, 0:T], cwt[:, i6, 0:1], None, ALU.mult, None, [xib, cwb], [ab])
                    for i in range(1, 3):
                        P.stt(a[:, :T], xi[:, i6, i:i + T], cwt[:, i6, i:i + 1], a[:, :T], ALU.mult, ALU.add,
                              [xib, cwb, ab], [ab])
                r_, rb_ = cv["r"]; k_, kb_ = cv["k"]; v_, vb_ = cv["v"]
                put("r", r_, rb_, t0, T); put("v", v_, vb_, t0, T)
                tw, twb = tmp()
                P.act(tw[:, :T], cv["lrw"][0][:, :T], AF.Tanh, [cv["lrw"][1]], [twb])
                a_d = []
                for d in range(2):
                    hs = slice(d * 64, (d + 1) * 64)
                    P.mm(ps[d][0][:, :T], wu[hs, :], tw[hs, :T], True, True, [wub, twb], [ps[d][1]])
                    lw, lwb = tmp()
                    P.act(lw[:, :T], ps[d][0][:, :T], AF.Sigmoid, [ps[d][1], vcb], [lwb], bias=vc[:, d:d + 1])
                    P.ts(DVE, lw[:, :T], lw[:, :T], -DECAY_SCALE, None, ALU.mult, None, [lwb], [lwb])
                    put("lw%d" % d, lw, lwb, t0, T)
                    P.mm(ps[2 + d][0][:, :T], au[hs, :], cv["lra"][0][hs, :T], True, True, [aub, cv["lra"][1]],
                         [ps[2 + d][1]])
                    a, ab = tmp()
                    P.act(a[:, :T], ps[2 + d][0][:, :T], AF.Sigmoid, [ps[2 + d][1], vcb], [ab], bias=vc[:, 2 + d:3 + d])
                    a_d.append((a, ab))
                kq, kqb = tmp()
                P.ts(DVE, kq[:, :T], k_[:, :T], vc[:, 4:5], None, ALU.mult, None, [kb_, vcb], [kqb])
                sq, sqb = tmp()
                P.act(sq[:, :T], kq[:, :T], AF.Square, [kqb], [sqb])
                P.mm(ps[4][0][:, :T], cm[:, 1, :], sq[:, :T], True, True, [cmb, sqb], [ps[4][1]])
                P.act(sq[:, :T], ps[4][0][:, :T], AF.Sqrt, [ps[4][1]], [sqb])
                P.ts(DVE, sq[:, :T], sq[:, :T], 1e-12, None, ALU.max, None, [sqb], [sqb])
                P.op(DVE, lambda e, sq=sq, T=T: e.reciprocal(out=sq[:, :T], in_=sq[:, :T]), [sqb], [sqb])
                P.tt(DVE, kq[:, :T], kq[:, :T], sq[:, :T], ALU.mult, [kqb, sqb], [kqb])
                put("kk", kq, kqb, t0, T)
                for d in range(2):
                    a, ab = a_d[d]
                    u, ub = tmp()
                    P.ts(DVE, u[:, :T], a[:, :T], vc[:, 5:6], vc[:, 6:7], ALU.mult, ALU.add, [ab, vcb], [ub])
                    P.tt(POOL, u[:, :T], u[:, :T], k_[:, :T], ALU.mult, [ub, kb_], [ub])
                    put("kt%d" % d, u, ub, t0, T)
                    be, beb = tmp()
                    P.tt(POOL, be[:, :T], a[:, :T], kq[:, :T], ALU.mult, [ab, kqb], [beb])
                    put("be%d" % d, be, beb, t0, T)
                rk, rkb = tmp()
                P.stt(rk[:, :T], r_[:, :T], vc[:, 7:8], k_[:, :T], ALU.mult, ALU.mult, [rb_, vcb, kb_], [rkb])
                P.mm(ps[5][0][:, :T], cm[:, 1, :], rk[:, :T], True, True, [cmb, rkb], [ps[5][1]])
                P.tt(DVE, rk[:, :T], ps[5][0][:, :T], v_[:, :T], ALU.mult, [ps[5][1], vb_], [rkb])
                put("bonus", rk, rkb, t0, T)
                sg, sgb = tmp()
                P.act(sg[:, :T], cv["gd"][0][:, :T], AF.Sigmoid, [cv["gd"][1]], [sgb])
                P.mm(ps[6][0][:, :T], gu[:], sg[:, :T], True, True, [gub, sgb], [ps[6][1]])
                P.act(sg[:, :T], ps[6][0][:, :T], AF.Copy, [ps[6][1]], [sgb])
                put("g", sg, sgb, t0, T)

    P.barrier()
    if 2 in phases:
        with contextlib.ExitStack() as es:
            ST = [C.sb("ST%d" % d, [64, 128], F32, es) for d in range(2)]
            blk = [[C.sb("rblk%d_%d" % (d, i), [64, 6, 2, 512], F32, es) for i in range(2)] for d in range(2)]
            YB = [[C.sb("rYB%d_%d" % (d, i), [64, 2, 512], F32, es) for i in range(2)] for d in range(2)]

            def mk2(name, shape, n=2):
                return [[C.sb("%s%d_%d" % (name, d, i), shape, F32, es) for i in range(n)] for d in range(2)]
            CL = mk2("rCL", [64, 3, 2, 64], 3); EE = mk2("rEE", [64, 4, 2, 64], 3); ET = mk2("rET", [64, 2], 3)
            KR = mk2("rKR", [64, 2, 128], 3); BK = mk2("rBK", [64, 2, 2, 64], 3); KB2 = mk2("rKB2", [64, 2, 2, 64], 3)
            PW = mk2("rPW", [64, 512]); MS = mk2("rMS", [64, 128]); TR = mk2("rTR", [64, 384])
            ZS = mk2("rZS", [64, 128]); PP = mk2("rPP", [64, 256], 4); XS = mk2("rXS", [64, 128]); SAn = mk2("rSAn", [64, 128])
            for d in range(2):
                P.op(DVE, lambda e, d=d: e.memset(ST[d][0][:], 0.0), [], [ST[d][1]])
            seqs = []
            for d in range(2):
                lst = []
                for bidx, (t0, T, offs) in enumerate(rwkv_dir_blocks(d)):
                    for oi, o in enumerate(offs):
                        lst.append((bidx, t0, T, o, oi == 0, oi == len(offs) - 1))
                seqs.append(lst)
            nst = len(seqs[0]) if nsteps is None else nsteps
            snm = [["r", "v", "kk", "lw0", "kt0", "be0"], ["r", "v", "kk", "lw1", "kt1", "be1"]]
            ppc = [0, 0]
            id64 = cm[0:64, 0, 0:64]
            def stageAB(i):
                sl = i % 3
                for d in range(2):
                    bidx, t0, T, o, first, last = seqs[d][i]
                    b_, bb_ = blk[d][bidx % 2]
                    if first:
                        for i6, n in enumerate(snm[d]):
                            P.load(b_[:, i6, :, :T], scr[n][:, t0:t0 + T].rearrange("(h c) t -> c h t", h=2), bb_,
                                   reads=[scrb[n]], writes=[bb_])
                    cs = slice(o, o + RC)
                    cl, clb = CL[d][sl]; ee, eeb = EE[d][sl]; et, etb = ET[d][sl]
                    lw = b_[:, 3, :, cs]
                    for h in range(2):
                        P.op(DVE, lambda e, cl=cl, b_=b_, cs=cs, h=h: e.tensor_tensor_scan(
                            out=cl[:, 0, h, :], data0=ones[0:64, :], data1=b_[:, 3, h, cs], initial=0.0,
                            op0=ALU.mult, op1=ALU.add), [onesb, bb_], [clb])
                    if d == 1:
                        for h in range(2):
                            P.ts(DVE, cl[:, 2, h, :], cl[:, 0, h, :], -1.0, cl[:, 0, h, 63:64], ALU.mult, ALU.add, [clb], [clb])
                        P.tt(DVE, cl[:, 1], cl[:, 2], lw, ALU.add, [clb, bb_], [clb])
                        cin, cex = cl[:, 1], cl[:, 2]
                    else:
                        P.tt(DVE, cl[:, 1], cl[:, 0], lw, ALU.subtract, [clb, bb_], [clb])
                        cin, cex = cl[:, 0], cl[:, 1]
                    P.act(ee[:, 0], cin, AF.Exp, [clb], [eeb])
                    P.act(ee[:, 1], cex, AF.Exp, [clb], [eeb])
                    P.act(ee[:, 2], cin, AF.Exp, [clb], [eeb], scale=-1.0)
                    for h in range(2):
                        tot = cl[:, 0, h, 63:64]
                        P.act(ee[:, 3, h, :], cin[:, h, :], AF.Exp, [clb], [eeb], scale=-1.0, bias=tot)
                        P.act(et[:, h:h + 1], tot, AF.Exp, [clb], [etb])
                    kr, krb = KR[d][sl]; bk, bkb = BK[d][sl]; kb2, kb2b = KB2[d][sl]
                    P.tt(DVE, kr[:, :, 0:64], b_[:, 2, :, cs], ee[:, 1], ALU.mult, [bb_, eeb], [krb])
                    P.tt(POOL, kr[:, :, 64:128], b_[:, 0, :, cs], ee[:, 0], ALU.mult, [bb_, eeb], [krb])
                    P.tt(DVE, bk[:, 0], b_[:, 5, :, cs], ee[:, 2], ALU.mult, [bb_, eeb], [bkb])
                    P.tt(POOL, bk[:, 1], b_[:, 4, :, cs], ee[:, 2], ALU.mult, [bb_, eeb], [bkb])
                    P.tt(DVE, kb2[:, 0], b_[:, 4, :, cs], ee[:, 3], ALU.mult, [bb_, eeb], [kb2b])
                    P.tt(POOL, kb2[:, 1], b_[:, 5, :, cs], ee[:, 3], ALU.mult, [bb_, eeb], [kb2b])

            def stageC(i):
                sl = i % 2
                for d in range(2):
                    bidx, t0, T, o, first, last = seqs[d][i]
                    b_, bb_ = blk[d][bidx % 2]
                    cs = slice(o, o + RC)
                    kr, krb = KR[d][i % 3]; bk, bkb = BK[d][i % 3]; kb2, kb2b = KB2[d][i % 3]
                    pa, pab = ps[4 * d + 0]; pb, pbb = ps[4 * d + 1]; pd, pdb = ps[4 * d + 3]
                    for h in range(2):
                        P.mm(pa[0:64, h * 128:(h + 1) * 128], bk[:, 0, h, :], kr[:, h, :], True, True, [bkb, krb], [pab])
                        P.mm(pa[0:64, 256 + h * 128:256 + (h + 1) * 128], bk[:, 1, h, :], kr[:, h, :], True, True,
                             [bkb, krb], [pab])
                        P.mm(pb[0:64, h * 64:(h + 1) * 64], kr[:, h, 0:64], bk[:, 0, h, :], True, True, [krb, bkb], [pbM[d]])
                    for h in range(2):
                        for (srcap, rb, c0) in [(kb2[:, 0, h, :], kb2b, 64 + h * 64), (kb2[:, 1, h, :], kb2b, 192 + h * 64),
                                                (b_[:, 1, h, cs], bb_, 320 + h * 64)]:
                            P.op(PE, lambda e, pd=pd, srcap=srcap, c0=c0: e.transpose(pd[0:64, c0:c0 + 64], srcap, id64),
                                 [rb, cmb], [pdb], sync_same=False)
                for d in range(2):
                    pa, pab = ps[4 * d + 0]; pb, pbb = ps[4 * d + 1]; pd, pdb = ps[4 * d + 3]
                    pw, pwb = PW[d][sl]; ms, msb = MS[d][sl]; tr, trb = TR[d][sl]; zs, zsb = ZS[d][sl]
                    P.tt(DVE, pw[:], pa[0:64, :], mk[:, d, 0:512], ALU.mult, [pab, mkb], [pwb])
                    P.tt(DVE, ms[:], pb[0:64, 0:128], mk[:, d, 512:640], ALU.mult, [pbM[d], mkb], [msb])
                    P.act(tr[:], pd[0:64, 64:448], AF.Copy, [pdb], [trb])
                    mt_v = pw[:, 0:256].rearrange("p (h x) -> p h x", h=2)[:, :, 0:64]
                    P.tt(POOL, zs[:].rearrange("p (h x) -> p h x", h=2), idr[:], mt_v, ALU.subtract, [idrb, pwb], [zsb])

            def stageD(i):
                sl = i % 2
                cur = []
                for d in range(2):
                    pw, pwb = PW[d][sl]; ms, msb = MS[d][sl]
                    cur.append(([pw[:, h * 128:h * 128 + 64] for h in range(2)], pwb,
                                [ms[:, h * 64:(h + 1) * 64] for h in range(2)], msb))
                for kk_ in range(1, 6):
                    for d in range(2):
                        pb, pbb = ps[4 * d + 1]
                        Pm, Pmb, PTm, PTmb = cur[d]
                        for h in range(2):
                            P.mm(pb[0:64, 128 + h * 64:128 + (h + 1) * 64], PTm[h], Pm[h], True, True, [PTmb, Pmb], [pbb])
                            P.mm(pb[0:64, 256 + h * 64:256 + (h + 1) * 64], Pm[h], PTm[h], True, True, [PTmb, Pmb], [pbb])
                    yield
                    for d in range(2):
                        pb, pbb = ps[4 * d + 1]
                        pp, ppb = PP[d][ppc[d] % 4]
                        ppc[d] += 1
                        P.op(DVE, lambda e, pp=pp, pb=pb: e.tensor_copy(out=pp[:], in_=pb[0:64, 128:384]), [pbb], [ppb])
                        cur[d] = ([pp[:, h * 64:(h + 1) * 64] for h in range(2)], ppb,
                                  [pp[:, 128 + h * 64:128 + (h + 1) * 64] for h in range(2)], ppb)
                    yield
                    for d in range(2):
                        pb, pbb = ps[4 * d + 1]
                        zs, zsb = ZS[d][sl]
                        for h in range(2):
                            P.mm(pb[0:64, 384 + h * 64:384 + (h + 1) * 64], cur[d][2][h], zs[:, h * 64:(h + 1) * 64],
                                 True, True, [cur[d][1], zsb], [pbZ[d]])
                    yield
                    for d in range(2):
                        pb, pbb = ps[4 * d + 1]
                        zs, zsb = ZS[d][sl]
                        P.tt(DVE, zs[:], zs[:], pb[0:64, 384:512], ALU.add, [zsb, pbZ[d]], [zsb])
                    yield

            def stageE(i):
                sl = i % 2
                R = range(2)
                info = [seqs[d][i] for d in R]
                for d in R:
                    kr, krb = KR[d][i % 3]; pw, pwb = PW[d][sl]; tr, trb = TR[d][sl]
                    st, stb = ST[d]; pc, pcb = ps[4 * d + 2]
                    for h in range(2):
                        hs = slice(h * 64, (h + 1) * 64)
                        P.mm(pc[0:64, hs], kr[:, h, 0:64], st[:, hs], True, False, [krb, stb], [pcb])
                        P.mm(pc[0:64, hs], pw[:, 256 + h * 128:256 + h * 128 + 64], tr[:, 256 + h * 64:256 + (h + 1) * 64],
                             False, True, [pwb, trb], [pcb])
                yield
                for d in R:
                    pc, pcb = ps[4 * d + 2]; xs_, xsb_ = XS[d][sl]
                    P.op(DVE, lambda e, xs_=xs_, pc=pc: e.tensor_copy(out=xs_[:], in_=pc[0:64, 0:128]), [pcb], [xsb_])
                yield
                for d in R:
                    pc, pcb = ps[4 * d + 2]; xs_, xsb_ = XS[d][sl]; zs, zsb = ZS[d][sl]
                    for h in range(2):
                        hs = slice(h * 64, (h + 1) * 64)
                        P.mm(pc[0:64, 128 + h * 64:128 + (h + 1) * 64], zs[:, hs], xs_[:, hs], True, True, [zsb, xsb_], [pcb])
                yield
                for d in R:
                    pc, pcb = ps[4 * d + 2]; sa, sab = SAn[d][sl]
                    P.ts(DVE, sa[:], pc[0:64, 128:256], -1.0, None, ALU.mult, None, [pcb], [sab])
                yield
                for d in R:
                    kr, krb = KR[d][i % 3]; pw, pwb = PW[d][sl]; tr, trb = TR[d][sl]
                    st, stb = ST[d]; pc, pcb = ps[4 * d + 2]; sa, sab = SAn[d][sl]
                    for h in range(2):
                        hs = slice(h * 64, (h + 1) * 64)
                        yo_ = pc[0:64, 384 + h * 64:384 + (h + 1) * 64]
                        P.mm(yo_, st[:, hs], kr[:, h, 64:128], True, False, [stb, krb], [pcb])
                        P.mm(yo_, tr[:, 256 + h * 64:256 + (h + 1) * 64],
                             pw[:, 256 + h * 128 + 64:256 + (h + 1) * 128], False, False, [trb, pwb], [pcb])
                        P.mm(yo_, sa[:, hs], pw[:, h * 128 + 64:(h + 1) * 128], False, True, [sab, pwb], [pcb])
                    for h in range(2):
                        hs = slice(h * 64, (h + 1) * 64)
                        su_ = pc[0:64, 256 + h * 64:256 + (h + 1) * 64]
                        P.mm(su_, tr[:, hs], tr[:, 256 + h * 64:256 + (h + 1) * 64], True, False, [trb], [pcb])
                        P.mm(su_, tr[:, 128 + h * 64:128 + (h + 1) * 64], sa[:, hs], False, True, [trb, sab], [pcb])
                yield
                for d in R:
                    bidx, t0, T, o, first, last = info[d]
                    et, etb = ET[d][i % 3]; st, stb = ST[d]; pc, pcb = ps[4 * d + 2]
                    for h in range(2):
                        hs = slice(h * 64, (h + 1) * 64)
                        P.stt(st[:, hs], st[:, hs], et[:, h:h + 1], pc[0:64, 256 + h * 64:256 + (h + 1) * 64], ALU.mult, ALU.add,
                              [stb, etb, pcb], [stb])
                    yb, ybb = YB[d][bidx % 2]
                    P.op(DVE, lambda e, yb=yb, pc=pc, o=o: e.tensor_copy(
                        out=yb[:, :, o:o + RC], in_=pc[0:64, 384:512].rearrange("p (h x) -> p h x", h=2)), [pcb], [ybb])
                    if last:
                        nm = "yf" if d == 0 else "yb"
                        P.dma(SP, lambda e, nm=nm, t0=t0, T=T, yb=yb: e.dma_start(
                            out=scr[nm][:, t0:t0 + T].rearrange("(h c) t -> c h t", h=2), in_=yb[:, :, :T]),
                            ybb, [ybb], [scrb[nm]])

            pbM = [Buf("pbM%d" % d) for d in range(2)]
            pbZ = [Buf("pbZ%d" % d) for d in range(2)]
            def drain(*gens):
                gens = [g for g in gens if g is not None]
                while gens:
                    for g in list(gens):
                        try:
                            next(g)
                        except StopIteration:
                            gens.remove(g)

            stageAB(0)
            stageC(0)
            for i in range(nst):
                if i + 1 < nst:
                    stageAB(i + 1)
                drain(stageD(i))
                if i + 1 < nst:
                    stageC(i + 1)
                drain(stageE(i))

    P.barrier()
    if 3 in phases:
        with contextlib.ExitStack() as es:
            NT3 = 8
            t3 = [C.sb("rt3_%d" % i, [128, 512], F32, es) for i in range(NT3)]
            c3 = [0]

            def tmp3():
                t = t3[c3[0] % NT3]
                c3[0] += 1
                return t
            for bi, (t0, T) in enumerate(TBLK):
                yf, yfb = tmp3(); yb, ybb = tmp3(); bo, bob = tmp3(); g, gb = tmp3()
                P.load(yf[:, :T], scr["yf"][:, t0:t0 + T], yfb, reads=[scrb["yf"]], writes=[yfb])
                P.load(yb[:, :T], scr["yb"][:, t0:t0 + T], ybb, reads=[scrb["yb"]], writes=[ybb])
                P.load(bo[:, :T], scr["bonus"][:, t0:t0 + T], bob, reads=[scrb["bonus"]], writes=[bob])
                P.load(g[:, :T], scr["g"][:, t0:t0 + T], gb, reads=[scrb["g"]], writes=[gb])
                P.tt(DVE, yf[:, :T], yf[:, :T], yb[:, :T], ALU.add, [yfb, ybb], [yfb])
                P.mm(ps[0][0][:, :T], cm[:, 2, :], yf[:, :T], True, True, [cmb, yfb], [ps[0][1]])
                P.tt(DVE, yf[:, :T], yf[:, :T], ps[0][0][:, :T], ALU.subtract, [yfb, ps[0][1]], [yfb])
                P.act(yb[:, :T], yf[:, :T], AF.Square, [yfb], [ybb])
                P.mm(ps[1][0][:, :T], cm[:, 2, :], yb[:, :T], True, True, [cmb, ybb], [ps[1][1]])
                P.act(yb[:, :T], ps[1][0][:, :T], AF.Sqrt, [ps[1][1], lnepsb], [ybb], bias=lneps[:])
                P.op(DVE, lambda e, yb=yb, T=T: e.reciprocal(out=yb[:, :T], in_=yb[:, :T]), [ybb], [ybb])
                P.stt(yf[:, :T], yf[:, :T], vc[:, 8:9], yb[:, :T], ALU.mult, ALU.mult, [yfb, vcb, ybb], [yfb])
                P.stt(yf[:, :T], yf[:, :T], vc[:, 9:10], bo[:, :T], ALU.add, ALU.add, [yfb, vcb, bob], [yfb])
                P.tt(DVE, yf[:, :T], yf[:, :T], g[:, :T], ALU.mult, [yfb, gb], [yfb])
                P.store(o_rwkv[:, t0:t0 + T], yf[:, :T], yfb, eng=SP)
    print("rwkv program ninstr", P.ninstr, "nsem", P.nsem)
    return C.done()


def rwkv_consts():
    if "r" in _CONST:
        return _CONST["r"]
    cmat = np.zeros((3, 128, 128), np.float32)
    cmat[0] = np.eye(128, dtype=np.float32)
    for b in range(2):
        cmat[1, b * 64:(b + 1) * 64, b * 64:(b + 1) * 64] = 1.0
        cmat[2, b * 64:(b + 1) * 64, b * 64:(b + 1) * 64] = 1.0 / 64
    j = np.arange(64)[:, None]; t = np.arange(64)[None, :]
    msk = np.zeros((64, 2, 640), np.float32)
    strict = [(t > j), (t < j)]; incl = [(t >= j), (t <= j)]
    for d in range(2):
        for s in range(8):
            msk[:, d, s * 64:(s + 1) * 64] = (strict[d] if s % 2 == 0 else incl[d])
        mm = (t < j) if d == 0 else (t > j)
        for h in range(2):
            msk[:, d, 512 + h * 64:512 + (h + 1) * 64] = mm
    _CONST["r"] = (cmat, msk)
    return _CONST["r"]


def rwkv_inputs(PT, inp, l, j):
    cmat, msk = rwkv_consts()
    o = OFF["rw"]
    hc = np.arange(2 * j * 64, (2 * j + 2) * 64)
    rows = {"r": o + hc, "k": o + 512 + hc, "v": o + 1024 + hc, "lrw": o + 1536 + np.arange(128),
            "lra": o + 1536 + 128 + np.arange(128), "gd": o + 1536 + 256 + np.arange(128)}
    d = {"rw_" + n: np.ascontiguousarray(PT[rows[n]]) for n in rows}
    sw = inp["rwkv_shift_w"][l]
    d["cw"] = np.ascontiguousarray(np.stack([sw[:, rows[n] - o].T for n in ["r", "k", "v", "lrw", "lra", "gd"]], axis=1))
    vec = np.zeros((128, 12), np.float32)
    vec[:, 0] = inp["rwkv_w0"][l][0, hc]; vec[:, 1] = inp["rwkv_w0"][l][1, hc]
    vec[:, 2] = inp["rwkv_a0"][l][0, hc]; vec[:, 3] = inp["rwkv_a0"][l][1, hc]
    vec[:, 4] = inp["rwkv_k_k"][l][hc]; vec[:, 5] = inp["rwkv_k_a"][l][hc]
    vec[:, 7] = inp["rwkv_r_k"][l].reshape(-1)[hc]
    vec[:, 8] = inp["rwkv_ln_g"][l][hc]; vec[:, 9] = inp["rwkv_ln_b"][l][hc]
    d["vec"] = vec
    d["wup"] = np.ascontiguousarray(np.concatenate([inp["rwkv_w_up"][l][0][:, hc], inp["rwkv_w_up"][l][1][:, hc]], 0))
    d["aup"] = np.ascontiguousarray(np.concatenate([inp["rwkv_a_up"][l][0][:, hc], inp["rwkv_a_up"][l][1][:, hc]], 0))
    d["gup"] = np.ascontiguousarray(inp["rwkv_g_up"][l][:, hc])
    d["cmat"] = cmat; d["msk"] = msk
    return d


_PROGS = {}


def _prog(key, fn):
    if key not in _PROGS:
        _PROGS[key] = fn()
    return _PROGS[key]


def _run(nc, in_maps):
    return run_bass_kernel_spmd(nc, in_maps, core_ids=list(range(NCORE))).results


def _big_weights(inp):
    lst = []
    for l in range(2):
        for i in range(2):
            lst += [("wg_%d_%d" % (l, i), inp["ffn_w_gate"][l, i]), ("wu_%d_%d" % (l, i), inp["ffn_w_up"][l, i]),
                    ("wd_%d_%d" % (l, i), inp["ffn_w_down"][l, i])]
        lst += [("win_%d" % l, inp["w_in"][l]), ("wout_%d" % l, inp["w_out"][l])]
    return lst


def kernel(**inp):
    inp = {k: np.asarray(v) for k, v in inp.items()}
    big = _big_weights(inp)
    per_core = sum(w.size for _, w in big) // NCORE
    L = -(-per_core // (128 * CONV_PIECE)) * CONV_PIECE
    cs = np.concatenate([inp["c"], inp["c_ctx"][None]], 0)
    cst = np.ascontiguousarray(cs.reshape(3, KC, 128).transpose(2, 1, 0))
    in_maps = []
    for c in range(NCORE):
        flat = np.zeros(128 * L, np.float32)
        o = 0
        for _, w in big:
            r = w.shape[0] // NCORE
            sl = w[c * r:(c + 1) * r].ravel()
            flat[o:o + sl.size] = sl
            o += sl.size
        cols = slice(c * MODC, (c + 1) * MODC)
        in_maps.append({
            "cst": cst,
            "mw": np.ascontiguousarray(inp["mod_w"][:, :, cols].reshape(2, KC, 128, MODC)),
            "mb": np.ascontiguousarray(np.broadcast_to(inp["mod_b"].reshape(2, 1, -1)[:, :, cols], (2, 3, MODC))),
            "wf": flat.reshape(128, L),
        })
    res = _run(_prog(("prep", L), lambda: build_prep(L)), in_maps)
    del in_maps
    m = np.concatenate([r["m_out"] for r in res], axis=2)
    m9 = m.reshape(2, 3, 9, D)
    wbf = {}
    flats = [np.asarray(r["wb"]).reshape(-1) for r in res]
    o = 0
    for name, w in big:
        r = w.shape[0] // NCORE
        n = r * w.shape[1]
        wbf[name] = np.concatenate([f[o:o + n].reshape(r, w.shape[1]) for f in flats], axis=0)
        o += n
    del flats, res
    wl = {}
    for name, w in wbf.items():
        if name.startswith("wd_") or name.startswith("wout_"):
            wl[name] = lay_wd(w)
        else:
            wl[name] = lay_w22(w)
    del wbf
    normg = np.ascontiguousarray(lay_vec(inp["norm_g"]))
    modv = []
    for b in range(2):
        mv = np.zeros((128, 2, 2, 9, KC), np.float32)
        for l in range(2):
            mv[:, l, 0] = lay_vec(m9[l, 2])
            mv[:, l, 1] = lay_vec(m9[l, b])
        modv.append(mv)

    z_full = np.concatenate([inp["ctx"], inp["x"]], axis=1)
    toks = [core_tokens(c) for c in range(NCORE)]
    zT = [fm(z_full[c // 4][toks[c]]) for c in range(NCORE)]
    del z_full

    def token_launch(stages, zT, extra):
        nc = _prog(("token",) + tuple(stages), lambda: build_token(stages))
        in_maps = []
        for c in range(NCORE):
            d = {"zT": zT[c], "modv": modv[c // 4], "normg": normg}
            for s in stages:
                if s[0] == "ffn":
                    for p in ("wg", "wu", "wd"):
                        nm = "%s_%d_%d" % (p, s[1], s[2])
                        d[nm] = wl[nm]
                elif s[0] == "inproj":
                    d["win_%d" % s[1]] = wl["win_%d" % s[1]]
                elif s[0] == "outproj":
                    d["wout_%d" % s[1]] = wl["wout_%d" % s[1]]
            d.update(extra[c])
            in_maps.append(d)
        return _run(nc, in_maps)

    def assemble_PT(res, l):
        PT = np.zeros((2, INP, NTOK), np.float32)
        for c in range(NCORE):
            PT[c // 4][:, toks[c]] = np.asarray(res[c]["pT_%d" % l]).reshape(INP, NT_CORE)
        return PT

    def mixers(PT, l):
        oT = np.zeros((2, D, NTOK), np.float32)
        ybT = np.zeros((2, 512, NTOK), np.float32)
        ra = _run(_prog("attn", build_attn), [attn_inputs(PT[c // 4], inp, l, c % 4) for c in range(NCORE)])
        for c in range(NCORE):
            b, j = c // 4, c % 4
            oT[b, j * 128:(j + 1) * 128] = ra[c]["o_diff"]
            oT[b, 1024 + j * 128:1024 + (j + 1) * 128] = ra[c]["o_mla"]
        del ra
        rr = _run(_prog("rwkv", build_rwkv), [rwkv_inputs(PT[c // 4], inp, l, c % 4) for c in range(NCORE)])
        for c in range(NCORE):
            b, j = c // 4, c % 4
            oT[b, 512 + j * 128:512 + (j + 1) * 128] = rr[c]["o_rwkv"]
        del rr
        rs = _run(_prog("ssd", build_ssd), [ssd_inputs(PT[c // 4], inp, l, c % 4) for c in range(NCORE)])
        for c in range(NCORE):
            b, j = c // 4, c % 4
            oT[b, 1536 + j * 128:1536 + (j + 1) * 128] = np.asarray(rs[c]["y_f"]).T
            ybT[b, j * 128:(j + 1) * 128] = np.asarray(rs[c]["y_b"]).T
        return oT, ybT

    def outproj_extra(oT, ybT, PT, l):
        ex = []
        sg = np.ascontiguousarray(inp["ssd_norm_g"][l].reshape(4, 128).T)
        for c in range(NCORE):
            b = c // 4
            ex.append({
                "oT_%d" % l: np.ascontiguousarray(oT[b][:, toks[c]].reshape(KC, 128, NT_CORE)),
                "ybT_%d" % l: np.ascontiguousarray(ybT[b][:, toks[c]].reshape(4, 128, NT_CORE)),
                "zgT_%d" % l: np.ascontiguousarray(PT[b][OFF["sz"]:OFF["sz"] + 512][:, toks[c]].reshape(4, 128, NT_CORE)),
                "ssdg_%d" % l: sg,
            })
        return ex

    res = token_launch([("ffn", 0, 0), ("inproj", 0)], zT, [{} for _ in range(NCORE)])
    zT = [np.asarray(r["z_out"]) for r in res]
    PT = assemble_PT(res, 0)
    del res
    oT, ybT = mixers(PT, 0)
    res = token_launch([("outproj", 0), ("ffn", 0, 1), ("ffn", 1, 0), ("inproj", 1)], zT, outproj_extra(oT, ybT, PT, 0))
    zT = [np.asarray(r["z_out"]) for r in res]
    PT = assemble_PT(res, 1)
    del res
    oT, ybT = mixers(PT, 1)
    res = token_launch([("outproj", 1), ("ffn", 1, 1)], zT, outproj_extra(oT, ybT, PT, 1))
    out = np.zeros((2, SEQ, D), np.float32)
    for c in range(NCORE):
        b, q = c // 4, c % 4
        zc = np.asarray(res[c]["z_out"]).reshape(D, NT_CORE)
        out[b, q * 4096:(q + 1) * 4096] = zc[:, 64:].T
    return out
```
